# Optimizing a Trainium2 kernel written in Bass

```python
import jax, jax.numpy as jnp
from jax import lax
import numpy as np

D_MODEL = 1024
BATCH = 4
SEQ = 4096
DEPTH = 1

CTX_LEN = 256
GRID_W = 64
CONV_WIDTH = D_MODEL
CONV_K = 3
RET_HEADS = 8
RET_DV = D_MODEL // RET_HEADS
RET_DK = RET_DV // 2
RET_QK_WIDTH = RET_HEADS * RET_DK
RET_V_WIDTH = RET_HEADS * RET_DV
CHUNK = 128
ROPE_BASE = 10000.0
EPS = 1e-6
IN_WIDTHS = (CONV_WIDTH, CONV_WIDTH, CONV_WIDTH, CONV_WIDTH,
             RET_QK_WIDTH, RET_QK_WIDTH, RET_V_WIDTH, RET_V_WIDTH, D_MODEL, D_MODEL)
SPLIT_POINTS = tuple(int(s) for s in np.cumsum(IN_WIDTHS)[:-1])
IN_WIDTH = int(sum(IN_WIDTHS))

kernel_name = "hybrid_conv_retention_dit_block"


def rmsnorm(x, w):
    x32 = x.astype(jnp.float32)
    y = x32 * lax.rsqrt(jnp.mean(x32 * x32, axis=-1, keepdims=True) + EPS)
    return (y * w.astype(jnp.float32)).astype(x.dtype)


def dwconv_centred(u, w, b):
    L = u.shape[1]
    pad = CONV_K // 2
    up = jnp.pad(u, ((0, 0), (pad, pad), (0, 0)))
    return sum(up[:, j:j + L] * w[j] for j in range(CONV_K)) + b


def to_heads(t, d):
    B_, L, _ = t.shape
    return t.reshape(B_, L, RET_HEADS, d).transpose(0, 2, 1, 3)


def rope2d(t):
    L = t.shape[2]
    rows = L // GRID_W
    row = jnp.repeat(jnp.arange(rows), GRID_W).astype(jnp.float32)
    col = jnp.tile(jnp.arange(GRID_W), rows).astype(jnp.float32)
    nf = RET_DK // 4
    inv = ROPE_BASE ** (-jnp.arange(nf, dtype=jnp.float32) / nf)
    ang = jnp.concatenate([row[:, None] * inv, col[:, None] * inv], axis=-1)
    cos = jnp.cos(ang).astype(t.dtype)
    sin = jnp.sin(ang).astype(t.dtype)
    half = RET_DK // 2
    t1, t2 = t[..., :half], t[..., half:]
    return jnp.concatenate([t1 * cos - t2 * sin, t1 * sin + t2 * cos], axis=-1)


def retention_scan(q, k, v, log_gamma, s0):
    B_, H, L, dk = q.shape
    dv = v.shape[-1]
    n = L // CHUNK
    qc = q.astype(jnp.float32).reshape(B_, H, n, CHUNK, dk)
    kc = k.astype(jnp.float32).reshape(B_, H, n, CHUNK, dk)
    vc = v.astype(jnp.float32).reshape(B_, H, n, CHUNK, dv)
    idx = jnp.arange(CHUNK, dtype=jnp.float32)
    diff = idx[:, None] - idx[None, :]
    dmask = jnp.where(diff >= 0, jnp.exp(log_gamma[:, None, None] * jnp.maximum(diff, 0.0)), 0.0)
    scores = jnp.einsum('bhnid,bhnjd->bhnij', qc, kc) * dmask[None, :, None]
    inner = jnp.einsum('bhnij,bhnjv->bhniv', scores, vc)
    k_dec = jnp.exp(log_gamma[:, None] * (CHUNK - 1 - idx))
    kv = jnp.einsum('bhnjd,hj,bhnjv->bhndv', kc, k_dec, vc)
    chunk_decay = jnp.exp(log_gamma * CHUNK)[None, :, None, None]

    def step(s, kv_c):
        return chunk_decay * s + kv_c, s

    _, s_prev = lax.scan(step, s0.astype(jnp.float32), jnp.moveaxis(kv, 2, 0))
    s_prev = jnp.moveaxis(s_prev, 0, 2)
    q_dec = jnp.exp(log_gamma[:, None] * (idx + 1.0))
    cross = jnp.einsum('bhnid,hi,bhndv->bhniv', qc, q_dec, s_prev)
    return (inner + cross).reshape(B_, H, L, dv)


def bidir_retention(q, k, v, lg_f, lg_b, s0_f, s0_b):
    o_f = retention_scan(q, k, v, lg_f, s0_f)
    flip = lambda t: t[:, :, ::-1]
    o_b = retention_scan(flip(q), flip(k), flip(v), lg_b, s0_b)
    return o_f + flip(o_b)


def ctx_final_states(k, v, lg_f, lg_b):
    Lc = k.shape[2]
    m = jnp.arange(Lc, dtype=jnp.float32)
    k32 = k.astype(jnp.float32)
    v32 = v.astype(jnp.float32)
    dec_f = jnp.exp(lg_f[:, None] * (Lc - 1 - m))
    dec_b = jnp.exp(lg_b[:, None] * m)
    s_f = jnp.einsum('bhmd,hm,bhmv->bhdv', k32, dec_f, v32)
    s_b = jnp.einsum('bhmd,hm,bhmv->bhdv', k32, dec_b, v32)
    return s_f, s_b


def retention_groupnorm(ret, gn_w, dtype):
    mu = jnp.mean(ret, axis=-1, keepdims=True)
    var = jnp.mean(jnp.square(ret - mu), axis=-1, keepdims=True)
    rn = (ret - mu) * lax.rsqrt(var + EPS)
    B_, H, L, dv = rn.shape
    rn = rn.transpose(0, 2, 1, 3).reshape(B_, L, H * dv)
    return (rn * gn_w.astype(jnp.float32)).astype(dtype)


def split_heads(proj, rotary):
    h, bg, cg, za, q, k, v, zb, ga, gb = jnp.split(proj, SPLIT_POINTS, axis=-1)
    q = to_heads(q, RET_DK)
    k = to_heads(k, RET_DK) * (RET_DK ** -0.5)
    v = to_heads(v, RET_DV)
    if rotary:
        q, k = rope2d(q), rope2d(k)
    return (h, bg, cg, za, zb, ga, gb), (q, k, v)


def merge_branches(parts, ret, conv_w, conv_b, gn_w, w_a, w_b, w_out):
    h, bg, cg, za, zb, ga, gb = parts
    conv_out = dwconv_centred(cg * h, conv_w, conv_b)
    y_a = (jax.nn.silu(za) * bg * conv_out) @ w_a
    ret_n = retention_groupnorm(ret, gn_w, h.dtype)
    y_b = (jax.nn.silu(zb) * ret_n) @ w_b
    return (jax.nn.sigmoid(ga) * y_a + jax.nn.sigmoid(gb) * y_b) @ w_out


def setup_inputs(seed: int = 0) -> dict:
    key = jax.random.key(seed)
    ks = jax.random.split(key, 20)
    f32 = jnp.float32
    nrm = lambda k, shape, s: jax.random.normal(k, shape, f32) * s
    base_gamma = 1.0 - 2.0 ** (-5.0 - np.arange(RET_HEADS, dtype=np.float32))
    base_logit = jnp.asarray(np.log(base_gamma / (1.0 - base_gamma)), f32)
    decay_logit = base_logit[None, None, :] + nrm(ks[10], (DEPTH, 2, RET_HEADS), 0.1)
    return {
        "x": nrm(ks[0], (BATCH, SEQ, D_MODEL), 1.0),
        "c": nrm(ks[1], (BATCH, D_MODEL), 1.0),
        "ctx": nrm(ks[2], (BATCH, CTX_LEN, D_MODEL), 1.0),
        "c_ctx": nrm(ks[3], (D_MODEL,), 1.0),
        "norm_w": 1.0 + nrm(ks[4], (DEPTH, D_MODEL), 0.02),
        "ada_w": nrm(ks[5], (DEPTH, D_MODEL, 3 * D_MODEL), 0.5 * D_MODEL ** -0.5),
        "ada_b": nrm(ks[6], (DEPTH, 3 * D_MODEL), 0.02),
        "w_in": nrm(ks[7], (DEPTH, D_MODEL, IN_WIDTH), D_MODEL ** -0.5),
        "conv_w": nrm(ks[8], (DEPTH, CONV_K, CONV_WIDTH), CONV_K ** -0.5),
        "conv_b": nrm(ks[9], (DEPTH, CONV_WIDTH), 0.02),
        "decay_logit": decay_logit,
        "gn_w": 1.0 + nrm(ks[11], (DEPTH, RET_V_WIDTH), 0.02),
        "w_a": nrm(ks[12], (DEPTH, CONV_WIDTH, D_MODEL), CONV_WIDTH ** -0.5),
        "w_b": nrm(ks[13], (DEPTH, RET_V_WIDTH, D_MODEL), RET_V_WIDTH ** -0.5),
        "w_out": nrm(ks[14], (DEPTH, D_MODEL, D_MODEL), D_MODEL ** -0.5),
        "final_norm_w": 1.0 + nrm(ks[15], (D_MODEL,), 0.02),
    }


def reference(x, c, ctx, c_ctx, norm_w, ada_w, ada_b, w_in, conv_w, conv_b,
              decay_logit, gn_w, w_a, w_b, w_out, final_norm_w):
    for l in range(DEPTH):
        mod_x = jax.nn.silu(c) @ ada_w[l] + ada_b[l]
        sh_x, sc_x, g_x = jnp.split(mod_x[:, None, :], 3, axis=-1)
        mod_c = jax.nn.silu(c_ctx) @ ada_w[l] + ada_b[l]
        sh_c, sc_c, g_c = jnp.split(mod_c, 3, axis=-1)
        xm = rmsnorm(x, norm_w[l]) * (1.0 + sc_x) + sh_x
        cm = rmsnorm(ctx, norm_w[l]) * (1.0 + sc_c) + sh_c
        parts_x, (qx, kx, vx) = split_heads(xm @ w_in[l], rotary=True)
        parts_c, (qc, kc, vc) = split_heads(cm @ w_in[l], rotary=False)
        lg = jax.nn.log_sigmoid(decay_logit[l].astype(jnp.float32))
        s_f, s_b = ctx_final_states(kc, vc, lg[0], lg[1])
        ret_x = bidir_retention(qx, kx, vx, lg[0], lg[1], s_f, s_b)
        y_x = merge_branches(parts_x, ret_x, conv_w[l], conv_b[l], gn_w[l], w_a[l], w_b[l], w_out[l])
        if l < DEPTH - 1:
            zeros = jnp.zeros_like(s_f)
            ret_c = bidir_retention(qc, kc, vc, lg[0], lg[1], zeros, zeros)
            y_c = merge_branches(parts_c, ret_c, conv_w[l], conv_b[l], gn_w[l], w_a[l], w_b[l], w_out[l])
            ctx = ctx + g_c * y_c
        x = x + g_x * y_x
    return rmsnorm(x, final_norm_w)
```

```python
import contextlib
import numpy as np
import concourse.bass as bass
import concourse.mybir as mybir
from concourse.bass_utils import run_bass_kernel_spmd

F32 = mybir.dt.float32
BF16 = mybir.dt.bfloat16
AF = mybir.ActivationFunctionType
ALU = mybir.AluOpType

D = 1024
SEQ = 4096
HALF = 2048
NCH = 16
EPS = 1e-6
H = 8
O_H, O_BG, O_CG, O_ZA, O_Q, O_K, O_V, O_ZB, O_GA, O_GB = 0, 1024, 2048, 3072, 4096, 4608, 5120, 6144, 7168, 8192


class Prog:
    def __init__(self):
        self.streams = {e: [] for e in ("pe", "act", "dve", "pool", "sp")}
        self.cnt = {e: 0 for e in self.streams}
        self.lastw = {}
        self.readers = {}
        self.waited = {e: {} for e in self.streams}
        self.dcnt = {}
        self.out_tokens = []

    def _waits(self, eng, reads, writes):
        toks = []
        for r in reads:
            t = self.lastw.get(r)
            if t is not None:
                toks.append(("raw", t))
            if len(r) == 2 and r[0] == "P" and r[1].isdigit():
                for t in self.readers.get(r, ()):
                    toks.append(("rar", t))
        for w in writes:
            t = self.lastw.get(w)
            if t is not None:
                toks.append(("waw", t))
            for t in self.readers.get(w, ()):
                toks.append(("war", t))
        need = {}
        for kind, (skey, val, teng) in toks:
            if teng == eng and (eng == "pe" or kind == "rar"):
                continue
            if val > need.get(skey, 0):
                need[skey] = val
        waits = []
        for skey, val in need.items():
            if self.waited[eng].get(skey, 0) >= val:
                continue
            self.waited[eng][skey] = val
            waits.append((skey, val))
        return waits

    def _record(self, tok, reads, writes):
        for r in reads:
            self.readers.setdefault(r, []).append(tok)
        for w in writes:
            self.lastw[w] = tok
            self.readers[w] = []

    def op(self, eng, fn, reads=(), writes=()):
        waits = self._waits(eng, reads, writes)
        self.cnt[eng] += 1
        tok = (eng, self.cnt[eng], eng)
        self.streams[eng].append((waits, fn, (eng, 1)))
        self._record(tok, reads, writes)
        return tok

    def barrier(self, resources):
        for eng in self.streams:
            waits = self._waits(eng, (), resources)
            if waits:
                self.streams[eng].append((waits, None, None))

    def dma(self, q, fn, slot, reads=(), writes=(), n=1):
        waits = self._waits(q, reads, writes)
        self.dcnt[slot] = self.dcnt.get(slot, 0) + n
        skey = ("dma", slot)
        tok = (skey, 16 * self.dcnt[slot], "dma")
        self.streams[q].append((waits, fn, (skey, 16)))
        self._record(tok, reads, writes)
        return tok


def build_program(debug=False):
    nc = bass.Bass("TRN2", target_bir_lowering=False)

    def din(name, shape):
        return nc.dram_tensor(name, list(shape), F32, kind="ExternalInput").ap()

    x_d = din("x", (SEQ, D))
    ctx_d = din("ctx", (256, D))
    vecs_d = din("vecs", (11, D))
    ada_d = din("ada_w", (D, 3 * D))
    win_d = din("w_in", (D, 9216))
    wa_d = din("w_a", (D, D))
    wb_d = din("w_b", (D, D))
    wo_d = din("w_out", (D, D))
    dl_d = din("dl", (1, 16))
    fnw_d = din("fnw", (1, D))
    ident_d = din("ident", (128, 128))
    dpn_d = din("dpn", (128, 512))
    pidx_d = din("pidx", (128, 4))
    rope_d = din("rope", (34, 128, 256))
    out_d = nc.dram_tensor("out", [HALF, D], F32, kind="ExternalOutput").ap()

    P = Prog()
    es = contextlib.ExitStack()

    def dump(name, ap, shape, dt, reads):
        if not debug:
            return
        t = nc.dram_tensor("dbg_" + name, list(shape), dt, kind="ExternalOutput").ap()
        tok = P.dma("sp", lambda e: e.dma_start(out=t, in_=ap), "dbg_" + name, reads=reads)
        P.out_tokens.append(tok)
    with es:
        def arena(name, nbytes):
            return es.enter_context(nc.sbuf_tensor(name, [128, nbytes // 2], BF16))

        class Bump:
            def __init__(self, name, nbytes):
                self.t = arena(name, nbytes)
                self.n = nbytes
                self.off = 0

            def reset(self, off=0):
                self.off = off

            def get(self, shape, dt):
                esz = 2 if dt == BF16 else 4
                n = int(np.prod(shape[1:]))
                nb = (n * esz + 63) // 64 * 64
                assert self.off + nb <= self.n, (self.off, nb, self.n, shape)
                ap = self.t[:, self.off // 2:(self.off + n * esz) // 2]
                self.off += nb
                if dt != BF16:
                    ap = ap.bitcast(dt)
                if len(shape) == 3:
                    ap = ap.rearrange("p (a b) -> p a b", a=shape[1])
                elif len(shape) == 4:
                    ap = ap.rearrange("p (a b c) -> p a b c", a=shape[1], b=shape[2])
                return ap

        KB = 1024
        CM = Bump("cm", 20 * KB)
        R1 = Bump("r1", 32 * KB)
        R2 = Bump("r2", 96 * KB)
        TM = Bump("tm", 59 * KB)

        PS = [es.enter_context(nc.psum_tensor(f"P{i}", [128, 512], F32)) for i in range(8)]

        def pf(i):
            return PS[i][:, :]

        def pb(i):
            return PS[i][:, :].bitcast(BF16)

        xc = [CM.get([128, D], F32) for _ in range(2)]
        xs = [CM.get([128, D], BF16) for _ in range(2)]
        ropet = [CM.get([128, 256], F32) for _ in range(3)]
        vT = CM.get([128, 8, 16], F32)
        scT = CM.get([128, 8, 2], BF16)
        modT = CM.get([128, 24, 2], F32)
        Amod = CM.get([128, 8, 2], F32)
        identf = CM.get([128, 128], F32)
        identb = CM.get([128, 128], BF16)
        pidx = CM.get([128, 4], F32)
        nlg = CM.get([128, 16], F32)
        kdec = CM.get([128, 16], F32)
        qdec = CM.get([128, 16], F32)
        dch = CM.get([128, 16], F32)
        dchs0 = CM.get([128, 8], F32)
        neghalf = CM.get([128, 8], F32)
        ssb = [CM.get([128, 1], F32) for _ in range(2)]
        msb = [CM.get([128, 1], F32) for _ in range(2)]
        rsb = [CM.get([128, 1], F32) for _ in range(2)]
        xmTh = CM.get([128, 8, 2], BF16)
        rs_own = CM.get([128, 16], F32)
        sse = [CM.get([128, 1], F32) for _ in range(4)]
        mse = [CM.get([128, 1], F32) for _ in range(4)]
        rse = [CM.get([128, 1], F32) for _ in range(4)]
        diag = CM.get([128, 128], F32)
        onesf = CM.get([128, 128], F32)

        wq = R1.get([128, 8, 512], BF16)
        wk = R1.get([128, 8, 512], BF16)
        wv = R1.get([128, 8, 1024], BF16)
        R1.reset()
        retnT = R1.get([128, 8, HALF], BF16)

        q_rb = R2.get([128, NCH, 512], BF16)
        kT = R2.get([128, 4, HALF], BF16)
        vbuf = R2.get([128, NCH, 1024], BF16)
        states = R2.get([128, NCH, 8, 128], BF16)
        R2.reset(64 * KB)
        adas = [R2.get([128, 8, 512], BF16) for _ in range(3)]
        R2.reset()
        bufA = R2.get([128, 8, 1024], BF16)
        mbuf = R2.get([128, 8, 1024], BF16)
        xmT = R2.get([128, 8, 1024], BF16)
        fsl = [[R2.get([128, 8, 256], BF16) for _ in range(2)] for _ in range(4)]
        wo = [R2.get([128, 8, 512], BF16) for _ in range(2)]

        vrow = TM.get([128, D], F32)
        dpn = TM.get([128, 512], F32)
        dlb = TM.get([128, 16], F32)
        marg = TM.get([128, 8, 128], F32)
        tmp16 = TM.get([128, 16], F32)
        TM.reset()
        MT = TM.get([128, 2, 4, 128], F32)
        xmTc = [TM.get([128, 8, 128], BF16) for _ in range(3)]
        vtmp = [TM.get([128, 1024], BF16) for _ in range(2)]
        kc = TM.get([128, 8, 64], BF16)
        ks = TM.get([128, 8, 64], BF16)
        qc = TM.get([128, 8, 64], BF16)
        qs = TM.get([128, 8, 64], BF16)
        k_rb = [TM.get([128, 8, 64], BF16) for _ in range(2)]
        kfb = [TM.get([128, 8, 128], BF16) for _ in range(2)]
        Rst = [TM.get([128, 8, 128], F32) for _ in range(2)]
        Fst = [TM.get([128, 8, 128], F32) for _ in range(2)]
        qfb = [TM.get([128, 8, 128], BF16) for _ in range(2)]
        qdTc = [TM.get([128, 8, 128], BF16) for _ in range(2)]
        PT = TM.get([128, 2, 4, 128], BF16)
        retn = TM.get([128, 1024], BF16)
        bst = [TM.get([128, 8, 6], F32) for _ in range(2)]
        bmv = [TM.get([128, 8, 2], F32) for _ in range(2)]
        rstd8 = [TM.get([128, 8], F32) for _ in range(2)]
        nmr8 = [TM.get([128, 8], F32) for _ in range(2)]
        p1_end = TM.off
        TM.reset()
        gxb = TM.get([128, D], F32)
        fnwb = TM.get([128, D], F32)
        h_sbF = TM.get([128, 1024], F32)
        h_sb = [h_sbF[:, 0:512], h_sbF[:, 512:1024]]
        cgh = [TM.get([128, 1026], F32) for _ in range(2)]
        sza = [TM.get([128, 512], F32) for _ in range(2)]
        t1 = [TM.get([128, 1024], F32) for _ in range(2)]
        c0 = TM.get([128, 1024], F32)
        c1 = TM.get([128, 1024], F32)
        sig = [TM.get([128, 512], F32) for _ in range(2)]
        tmpd = [TM.get([128, 512], F32) for _ in range(2)]
        xn = [TM.get([128, D], F32) for _ in range(2)]
        hh = TM.get([128, 4], F32)

        P.dma("sp", lambda e: e.dma_start(out=vrow[0:11, :], in_=vecs_d[:, :]), "vrow", writes=["vrow"])
        P.dma("sp", lambda e: e.dma_start(out=identf, in_=ident_d[:, :]), "identf", writes=["identf"])
        P.dma("sp", lambda e: e.dma_start(out=dpn, in_=dpn_d[:, :]), "dpn", writes=["dpn"])
        P.dma("sp", lambda e: e.dma_start(out=pidx, in_=pidx_d[:, :]), "pidx", writes=["pidx"])
        P.dma("sp", lambda e: e.dma_start(out=dlb, in_=dl_d[0, :].partition_broadcast(128)), "dlb", writes=["dlb"])
        win_v = win_d.rearrange("(k p) c -> p k c", p=128)
        ada_v = ada_d.rearrange("(k p) c -> p k c", p=128)
        wa_v = wa_d.rearrange("(k p) c -> p k c", p=128)
        wb_v = wb_d.rearrange("(k p) c -> p k c", p=128)
        wo_v = wo_d.rearrange("(k p) c -> p k c", p=128)

        def ada_load(i):
            b = adas[i % 3]
            P.dma("pool", lambda e: e.dma_start(out=b, in_=ada_v[:, :, i * 512:(i + 1) * 512]),
                  f"adas{i % 3}", writes=[f"adas{i % 3}"])

        ada_load(0)
        ada_load(1)
        ada_load(2)

        P.op("dve", lambda e: e.tensor_copy(out=identb, in_=identf), reads=["identf"], writes=["identb"])

        def f_vtr(e):
            ins = None
            for k in range(8):
                ins = e.transpose(out=pf(6)[:, k * 16:k * 16 + 11], in_=vrow[0:11, k * 128:(k + 1) * 128],
                                  identity=identf[0:11, 0:11])
            return ins
        P.op("pe", f_vtr, reads=["vrow", "identf"], writes=["P6"])
        P.op("dve", lambda e: e.tensor_copy(out=vT[:, :, 0:11], in_=pf(6)[:, 0:128].rearrange("p (a b) -> p a b", a=8)[:, :, 0:11]),
             reads=["P6"], writes=["vT"])
        P.op("act", lambda e: e.activation(out=scT, in_=vT[:, :, 0:2], func=AF.Silu), reads=["vT"], writes=["scT"])

        P.op("act", lambda e: e.activation(out=tmp16, in_=dlb, func=AF.Exp, scale=-1.0), reads=["dlb"], writes=["tmp16"])
        P.op("act", lambda e: e.activation(out=nlg, in_=tmp16, func=AF.Ln, bias=1.0), reads=["tmp16"], writes=["nlg"])

        def f_decarg(e):
            e.memset(neghalf, -0.5)
            e.memset(onesf, 1.0)
            e.tensor_scalar(out=kdec[:, 0:8], in0=nlg[:, 0:8], scalar1=pidx[:, 1:2], scalar2=None, op0=ALU.mult)
            e.tensor_scalar(out=kdec[:, 8:16], in0=nlg[:, 8:16], scalar1=pidx[:, 0:1], scalar2=None, op0=ALU.mult)
            e.tensor_scalar(out=qdec[:, 0:8], in0=nlg[:, 0:8], scalar1=pidx[:, 2:3], scalar2=None, op0=ALU.mult)
            e.tensor_scalar(out=qdec[:, 8:16], in0=nlg[:, 8:16], scalar1=pidx[:, 3:4], scalar2=None, op0=ALU.mult)
            return e.tensor_scalar(out=dch, in0=nlg, scalar1=128.0, scalar2=None, op0=ALU.mult)
        P.op("dve", f_decarg, reads=["nlg", "pidx"], writes=["decarg", "neghalf", "onesf"])

        def f_decexp(e):
            e.activation(out=kdec, in_=kdec, func=AF.Exp, scale=-1.0)
            e.activation(out=qdec, in_=qdec, func=AF.Exp, scale=-1.0)
            return e.activation(out=dch, in_=dch, func=AF.Exp, scale=-1.0)
        P.op("act", f_decexp, reads=["decarg"], writes=["dec"])

        def f_dchs(e):
            e.memset(dchs0[0:64, :], 0.0)
            return e.tensor_copy(out=dchs0[64:128, :], in_=dch[64:128, 8:16])
        P.op("dve", f_dchs, reads=["dec"], writes=["dchs0"])

        def f_marg0(e):
            ins = None
            for h in range(8):
                ins = e.tensor_scalar(out=marg[:, h, :], in0=dpn[:, 0:128], scalar1=nlg[:, h:h + 1], scalar2=None, op0=ALU.mult)
            return ins
        P.op("dve", f_marg0, reads=["nlg", "dpn"], writes=["marg"])

        def f_marg(e):
            ins = None
            for h in range(8):
                ins = e.scalar_tensor_tensor(out=marg[:, h, :], in0=dpn[:, 128:256], scalar=nlg[:, 8 + h:9 + h],
                                             in1=marg[:, h, :], op0=ALU.mult, op1=ALU.add)
            return ins
        P.op("dve", f_marg, reads=["nlg", "dpn", "marg"], writes=["marg"])
        def f_marg3(e):
            ins = None
            for h in range(8):
                if h % 2 == 0:
                    iv, col = dpn[:, 256:384], nlg[:, h:h + 1]
                else:
                    iv, col = dpn[:, 384:512], nlg[:, 8 + h:9 + h]
                ins = e.scalar_tensor_tensor(out=marg[:, h, :], in0=iv, scalar=col, in1=marg[:, h, :], op0=ALU.mult, op1=ALU.subtract)
            return ins
        P.op("dve", f_marg3, reads=["nlg", "dpn", "marg"], writes=["marg"])
        P.op("act", lambda e: e.activation(out=marg, in_=marg, func=AF.Exp), reads=["marg"], writes=["marg2"])
        P.op("dve", lambda e: e.tensor_scalar(out=diag, in0=identf, scalar1=1.0, scalar2=None, op0=ALU.add), reads=["identf"], writes=["diag"])

        def f_mt(e):
            ins = None
            for h in range(8):
                ins = e.tensor_tensor(out=MT[:, h % 2, h // 2, :], in0=marg[:, h, :], in1=diag, op=ALU.mult)
            return ins
        P.op("dve", f_mt, reads=["marg2", "diag", "vT", "P6"], writes=["MT"])

        gcc = [0]

        def front_a1(rows_ap, np_):
            i = gcc[0] % 2
            gcc[0] += 1
            xcb, xsb = xc[i], xs[i]
            P.dma("sp", lambda e: e.dma_start(out=xcb[0:np_, :], in_=rows_ap), f"xc{i}", writes=[f"xc{i}", f"xc{i}h"])
            P.op("act", lambda e: e.activation(out=xsb[0:np_, :], in_=xcb[0:np_, :], func=AF.Square, accum_out=ssb[i][0:np_, :]),
                 reads=[f"xc{i}"], writes=[f"xs{i}", f"ss{i}"])
            return i

        def front_a2(i, np_, keep=None):
            xcb, xsb = xc[i], xs[i]
            rs_ap, rs_nm = (rsb[i][0:np_, :], f"rs{i}") if keep is None else keep
            P.op("pool", lambda e: e.tensor_scalar(out=msb[i][0:np_, :], in0=ssb[i][0:np_, :], scalar1=1.0 / D, scalar2=EPS, op0=ALU.mult, op1=ALU.add),
                 reads=[f"ss{i}"], writes=[f"ms{i}"])
            P.op("pool", lambda e: e.tensor_tensor(out=rs_ap, in0=msb[i][0:np_, :], in1=neghalf[0:np_, 0:1], op=ALU.pow),
                 reads=[f"ms{i}", "neghalf"], writes=[rs_nm])
            P.op("act", lambda e: e.activation(out=xsb[0:np_, :], in_=xcb[0:np_, :], func=AF.Copy, scale=rs_ap),
                 reads=[f"xc{i}", rs_nm], writes=[f"xs{i}"])

        def front_a(rows_ap, np_, keep=None):
            i = front_a1(rows_ap, np_)
            front_a2(i, np_, keep=keep)
            return i

        def front_a_cached(n):
            i = gcc[0] % 2
            gcc[0] += 1
            xcb, xsb = xc[i], xs[i]
            P.dma("sp", lambda e: e.dma_start(out=xcb, in_=x_d[n * 128:(n + 1) * 128, :]), f"xc{i}", writes=[f"xc{i}", f"xc{i}h"])
            P.op("act", lambda e: e.activation(out=xsb, in_=xcb, func=AF.Copy, scale=rs_own[:, n:n + 1]),
                 reads=[f"xc{i}", f"rso{n}"], writes=[f"xs{i}"])
            return i

        def front_b(i, np_, r, dst, dst_name, ptp=0, alias=()):
            xsb = xs[i]

            def f_tp(e):
                ins = None
                for k in range(8):
                    ins = e.transpose(out=pb(ptp)[:, k * 128:k * 128 + np_], in_=xsb[0:np_, k * 128:(k + 1) * 128],
                                      identity=identb[0:np_, 0:np_])
                return ins
            P.op("pe", f_tp, reads=[f"xs{i}", "identb"], writes=[f"P{ptp}"])

            def f_aff(e):
                ins = None
                for k in range(8):
                    ins = e.tensor_scalar(out=dst[:, k, :], in0=pb(ptp)[:, k * 128:k * 128 + np_],
                                          scalar1=Amod[:, k, r:r + 1], scalar2=modT[:, k, r:r + 1],
                                          op0=ALU.mult, op1=ALU.add)
                return ins
            P.op("dve", f_aff, reads=[f"P{ptp}", "Amod", "modT"], writes=[dst_name] + list(alias))

        def front(rows_ap, np_, r, dst, dst_name, ti=None, ptp=0):
            i = front_a(rows_ap, np_)
            front_b(i, np_, r, dst, dst_name, ptp=ptp)
            return i


        def rope_evac(pbank, pname, tab_lo, tab, tabname, dst_c, dst_s, dst_name):
            def f(e):
                src = pf(pbank).rearrange("p (h t f) -> p h t f", h=8, t=2)
                e.tensor_tensor(out=dst_c, in0=pf(pbank).rearrange("p (h f) -> p h f", h=8),
                                in1=tab[:, tab_lo:tab_lo + 64].unsqueeze(1).broadcast_to([128, 8, 64]), op=ALU.mult)
                e.tensor_tensor(out=dst_s[:, :, 0:32], in0=src[:, :, 1, :],
                                in1=tab[:, tab_lo + 64:tab_lo + 96].unsqueeze(1).broadcast_to([128, 8, 32]), op=ALU.mult)
                return e.tensor_tensor(out=dst_s[:, :, 32:64], in0=src[:, :, 0, :],
                                       in1=tab[:, tab_lo + 96:tab_lo + 128].unsqueeze(1).broadcast_to([128, 8, 32]), op=ALU.mult)
            P.op("dve", f, reads=[pname, tabname], writes=[dst_name])

        seq = [("ctx", ctx_d[128:256, :], 33, None), ("ctx", ctx_d[0:128, :], 32, None)]
        seq += [("other", x_d[n * 128:(n + 1) * 128, :], n, None) for n in range(31, 15, -1)]
        seq += [("own", x_d[n * 128:(n + 1) * 128, :], n, n) for n in range(15, -1, -1)]

        fidx = {}

        def s0a(c):
            kind, rows_ap, ti, n = seq[c]
            j3 = c % 3
            P.dma("sp", lambda e: e.dma_start(out=ropet[j3], in_=rope_d[ti, :, :]), f"ropet{j3}", writes=[f"ropet{j3}"])
            keep = (rs_own[:, n:n + 1], f"rso{n}") if kind == "own" else None
            fidx[c] = front_a(rows_ap, 128, keep=keep)

        def s0b(c):
            kind, rows_ap, ti, n = seq[c]
            r = 1 if kind == "ctx" else 0
            j3 = c % 3
            front_b(fidx[c], 128, r, xmTc[j3], f"xmTc{j3}")

        def s1(c):
            kind, rows_ap, ti, n = seq[c]
            own = kind == "own"
            j3, j = c % 3, c % 2
            xm_ = xmTc[j3]
            xmn = f"xmTc{j3}"
            vdst = vbuf[:, n, :] if own else vtmp[j]
            vname = f"v{n}" if own else f"vtmp{j}"

            kb = 1 if own else 1 + (c % 2)

            def f_kproj(e):
                ins = None
                for k in range(8):
                    ins = e.matmul(pf(kb), lhsT=xm_[:, k, :], rhs=wk[:, k, :], start=(k == 0), stop=(k == 7))
                return ins
            P.op("pe", f_kproj, reads=[xmn, "wk"], writes=[f"P{kb}"])
            for hv in range(2):
                def f_vproj(e, hv=hv):
                    ins = None
                    for k in range(8):
                        ins = e.matmul(pf(3 + hv), lhsT=xm_[:, k, :], rhs=wv[:, k, hv * 512:(hv + 1) * 512],
                                       start=(k == 0), stop=(k == 7))
                    return ins
                P.op("pe", f_vproj, reads=[xmn, f"wv{hv}"], writes=[f"P{3 + hv}"])
            if own:
                def f_qproj(e):
                    ins = None
                    for k in range(8):
                        ins = e.matmul(pf(2), lhsT=xm_[:, k, :], rhs=wq[:, k, :], start=(k == 0), stop=(k == 7))
                    return ins
                P.op("pe", f_qproj, reads=[xmn, "wq"], writes=["P2"])
            rope_evac(kb, f"P{kb}", 0, ropet[j3], f"ropet{j3}", kc, ks, "kcs")
            for hv in range(2):
                P.op("act", lambda e, hv=hv: e.activation(out=vdst[:, hv * 512:(hv + 1) * 512], in_=pf(3 + hv), func=AF.Copy),
                     reads=[f"P{3 + hv}"], writes=[vname + f"_{hv}"])
            P.op("dve", lambda e: e.tensor_tensor(out=k_rb[j], in0=kc, in1=ks, op=ALU.add), reads=["kcs"], writes=[f"k_rb{j}"])

            def f_kfb(e):
                e.tensor_tensor(out=kfb[j][:, :, 0:64], in0=k_rb[j], in1=kdec[:, 0:8].unsqueeze(2).broadcast_to([128, 8, 64]), op=ALU.mult)
                return e.tensor_tensor(out=kfb[j][:, :, 64:128], in0=k_rb[j], in1=kdec[:, 8:16].unsqueeze(2).broadcast_to([128, 8, 64]), op=ALU.mult)
            P.op("pool", f_kfb, reads=[f"k_rb{j}", "dec"], writes=[f"kfb{j}"])
            if own:
                rope_evac(2, "P2", 128, ropet[j3], f"ropet{j3}", qc, qs, "qcs")

                def f_ktp(e):
                    ins = None
                    kr = k_rb[j].rearrange("p h f -> p (h f)")
                    for hp in range(4):
                        ins = e.transpose(out=pb(7)[:, hp * 128:(hp + 1) * 128], in_=kr[:, hp * 128:(hp + 1) * 128], identity=identb)
                    return ins
                P.op("pe", f_ktp, reads=[f"k_rb{j}", "identb"], writes=["P7"])
                P.op("act", lambda e: e.activation(out=kT[:, :, n * 128:(n + 1) * 128],
                                                   in_=pb(7)[:, 0:512].rearrange("p (a b) -> p a b", a=4), func=AF.Copy),
                     reads=["P7"], writes=[f"kT{n}"])
                P.op("dve", lambda e: e.tensor_tensor(out=q_rb[:, n, :].rearrange("p (h f) -> p h f", h=8), in0=qc, in1=qs, op=ALU.add),
                     reads=["qcs"], writes=[f"q_rb{n}"])

        def s2(c):
            kind, rows_ap, ti, n = seq[c]
            own = kind == "own"
            j = c % 2
            ctx_second = c == 1
            vdst = vbuf[:, n, :] if own else vtmp[j]
            vname = f"v{n}" if own else f"vtmp{j}"

            def f_kv(e):
                ins = None
                for h in range(8):
                    ins = e.matmul(pf(5 + h // 4)[:, (h % 4) * 128:(h % 4 + 1) * 128], lhsT=kfb[j][:, h, :],
                                   rhs=vdst[:, h * 128:(h + 1) * 128], start=True, stop=True)
                return ins
            ro, rn = c % 2, (c + 1) % 2
            Ro, Rn = Rst[ro], Rst[rn]
            if own:
                P.op("act", lambda e: e.activation(out=states[64:128, n], in_=Ro[64:128], func=AF.Copy),
                     reads=[f"R{ro}"], writes=[f"stb{n}"])
            P.op("pe", f_kv, reads=[f"kfb{j}", vname + "_0", vname + "_1"], writes=["P5", "P6"])
            if own:
                def f_stf(e):
                    e.activation(out=states[0:64, n, 0:4, :], in_=pf(5)[0:64, :].rearrange("p (a b) -> p a b", a=4), func=AF.Copy)
                    return e.activation(out=states[0:64, n, 4:8, :], in_=pf(6)[0:64, :].rearrange("p (a b) -> p a b", a=4), func=AF.Copy)
                P.op("act", f_stf, reads=["P5", "P6"], writes=[f"stf{n}"])

            def f_state(e):
                ins = None
                for h in range(8):
                    pk_ = pf(5 + h // 4)[:, (h % 4) * 128:(h % 4 + 1) * 128]
                    if ctx_second:
                        e.scalar_tensor_tensor(out=Rn[0:64, h, :], in0=pk_[0:64, :], scalar=dch[0:64, h:h + 1],
                                               in1=Ro[0:64, h, :], op0=ALU.mult, op1=ALU.add)
                        ins = e.scalar_tensor_tensor(out=Rn[64:128, h, :], in0=Ro[64:128, h, :], scalar=dchs0[64:128, h:h + 1],
                                                     in1=pk_[64:128, :], op0=ALU.mult, op1=ALU.add)
                    else:
                        ins = e.scalar_tensor_tensor(out=Rn[:, h, :], in0=Ro[:, h, :], scalar=dchs0[:, h:h + 1],
                                                     in1=pk_, op0=ALU.mult, op1=ALU.add)
                return ins
            P.op("dve", f_state, reads=["P5", "P6", f"R{ro}", "dchs0", "dec"], writes=[f"R{rn}"])
            if ctx_second:
                P.op("act", lambda e: e.activation(out=Fst[0][0:64], in_=Rn[0:64], func=AF.Copy), reads=[f"R{rn}"], writes=["F0"])

        s0a(0)
        s0a(1)
        def ada_mm(i):
            def f_mod(e):
                ins = None
                b = adas[i % 3]
                for c4 in range(4):
                    ct = i * 4 + c4
                    for k in range(8):
                        ins = e.matmul(pf(7)[:, ct * 2:ct * 2 + 2], lhsT=b[:, k, c4 * 128:(c4 + 1) * 128],
                                       rhs=scT[:, k, :], start=(k == 0), stop=(k == 7))
                return ins
            P.op("pe", f_mod, reads=[f"adas{i % 3}", "scT"], writes=["P7"])

        for i in range(4):
            ada_mm(i)
            if i + 3 < 4:
                ada_load(i + 3)

        P.dma("pool", lambda e: e.dma_start(out=wk, in_=win_v[:, :, O_K:O_K + 512]), "wk", writes=["wk"])
        P.dma("pool", lambda e: e.dma_start(out=wv[:, :, 0:512], in_=win_v[:, :, O_V:O_V + 512]), "wv0", writes=["wv0"])
        P.dma("pool", lambda e: e.dma_start(out=wv[:, :, 512:1024], in_=win_v[:, :, O_V + 512:O_V + 1024]), "wv1", writes=["wv1"])

        def modT_part(ts, name):
            def f_modT(e):
                ins = None
                for t in ts:
                    ins = e.tensor_tensor(out=modT[:, t * 8:(t + 1) * 8, :],
                                          in0=pf(7)[:, t * 16:(t + 1) * 16].rearrange("p (a b) -> p a b", a=8),
                                          in1=vT[:, :, 8 + t:9 + t].broadcast_to([128, 8, 2]), op=ALU.add)
                return ins
            P.op("dve", f_modT, reads=["P7", "vT"], writes=[name])
        modT_part((0, 1), "modT")

        P.op("dve", lambda e: e.tensor_scalar(out=Amod, in0=modT[:, 8:16, :], scalar1=1.0, scalar2=None, op0=ALU.add),
             reads=["modT"], writes=["Amod"])
        P.op("dve", lambda e: e.tensor_tensor(out=Amod, in0=Amod, in1=vT[:, :, 2:3].broadcast_to([128, 8, 2]), op=ALU.mult),
             reads=["Amod", "vT"], writes=["Amod"])

        P.barrier(["vrow", "dpn", "dlb", "marg", "marg2", "tmp16", "decarg", "P6", "P7"])
        P.op("dve", lambda e: e.memset(Rst[0], 0.0), writes=["R0"])
        NS = len(seq)
        for t in range(NS + 3):
            if t == 4:
                P.dma("pool", lambda e: e.dma_start(out=wq, in_=win_v[:, :, O_Q:O_Q + 512]), "wq", writes=["wq"])
            if t == 5:
                ada_load(4)
                ada_load(5)
            if t == 9:
                ada_mm(4)
                ada_mm(5)
                modT_part((2,), "modTg")
            if 2 <= t < NS:
                s0a(t)
            if 0 <= t - 2 < NS:
                s1(t - 2)
            if t < NS:
                s0b(t)
            if 0 <= t - 3 < NS:
                s2(t - 3)

        RETB = [(4, 5), (6, 7)]

        def o_sweep(n):
            Fo, Fn = Fst[n % 2], Fst[(n + 1) % 2]

            def f_sweep(e):
                ins = None
                for h in range(8):
                    ins = e.scalar_tensor_tensor(out=Fn[0:64, h, :], in0=Fo[0:64, h, :], scalar=dch[0:64, h:h + 1],
                                                 in1=states[0:64, n, h, :], op0=ALU.mult, op1=ALU.add)
                return ins
            P.op("dve", f_sweep, reads=[f"F{n % 2}", f"stf{n}", "dec"], writes=[f"F{(n + 1) % 2}"])

        def o_stcopy(n):
            Fo = Fst[n % 2]
            P.op("act", lambda e: e.activation(out=states[0:64, n], in_=Fo[0:64], func=AF.Copy),
                 reads=[f"F{n % 2}"], writes=[f"stf{n}"])

        def o_qfb(n):
            j = n % 2

            def f_qfb(e):
                qv = q_rb[:, n, :].rearrange("p (h f) -> p h f", h=8)
                e.tensor_tensor(out=qfb[j][:, :, 0:64], in0=qv, in1=qdec[:, 0:8].unsqueeze(2).broadcast_to([128, 8, 64]), op=ALU.mult)
                return e.tensor_tensor(out=qfb[j][:, :, 64:128], in0=qv, in1=qdec[:, 8:16].unsqueeze(2).broadcast_to([128, 8, 64]), op=ALU.mult)
            P.op("pool", f_qfb, reads=[f"q_rb{n}", "dec"], writes=[f"qfb{j}"])

        def o_qdtp(n):
            j = n % 2

            def f_qdtp(e):
                ins = None
                for h in range(8):
                    ins = e.transpose(out=pb(1)[:, h * 128:(h + 1) * 128], in_=qfb[j][:, h, :], identity=identb)
                return ins
            P.op("pe", f_qdtp, reads=[f"qfb{j}", "identb"], writes=["P1"])

        def o_qdTc(n):
            j = n % 2
            P.op("act", lambda e: e.activation(out=qdTc[j], in_=pb(1).rearrange("p (a b) -> p a b", a=8), func=AF.Copy),
                 reads=["P1"], writes=[f"qdTc{j}"])

        def o_sc(n):
            j = n % 2

            def f_sc(e):
                ins = None
                for h in range(8):
                    par, hp = h % 2, h // 2
                    b0 = 64 * par
                    ins = e.matmul(pf(2 + par)[:, hp * 128:(hp + 1) * 128], lhsT=kT[b0:b0 + 64, hp, n * 128:(n + 1) * 128],
                                   rhs=qdTc[j][b0:b0 + 64, h, :], start=True, stop=True)
                return ins
            P.op("pe", f_sc, reads=[f"kT{n}", f"qdTc{j}"], writes=["P2", "P3"])

        def o_mask(n):
            def f_mask(e):
                ins = None
                for par in range(2):
                    ins = e.tensor_tensor(out=PT[:, par], in0=pf(2 + par).rearrange("p (a b) -> p a b", a=4), in1=MT[:, par], op=ALU.mult)
                return ins
            P.op("dve", f_mask, reads=["P2", "P3", "MT"], writes=["PT"])

        def o_ret(n):
            j = n % 2
            pA, pB = RETB[n % 2]

            def f_ret(e):
                ins = None
                for h in range(8):
                    par, hp = h % 2, h // 2
                    o = pf((pA, pB)[h // 4])[:, (h % 4) * 128:(h % 4 + 1) * 128]
                    e.matmul(o, lhsT=PT[:, par, hp, :], rhs=vbuf[:, n, h * 128:(h + 1) * 128], start=True, stop=False)
                    ins = e.matmul(o, lhsT=qdTc[j][:, h, :], rhs=states[:, n, h, :], start=False, stop=True)
                return ins
            P.op("pe", f_ret, reads=["PT", f"v{n}_0", f"v{n}_1", f"qdTc{j}", f"stf{n}", f"stb{n}"], writes=[f"P{pA}", f"P{pB}"])

        def o_bn(n):
            pA, pB = RETB[n % 2]
            sj = n % 2

            def f_bn(e):
                ins = None
                for h in range(8):
                    o = pf((pA, pB)[h // 4])[:, (h % 4) * 128:(h % 4 + 1) * 128]
                    ins = e.bn_stats(out=bst[sj][:, h, :], in_=o)
                return ins
            P.op("dve", f_bn, reads=[f"P{pA}", f"P{pB}"], writes=[f"bst{sj}"])

            def f_bna(e):
                ins = None
                for h in range(8):
                    ins = e.bn_aggr(out=bmv[sj][:, h, :], in_=bst[sj][:, h, :])
                return ins
            P.op("dve", f_bna, reads=[f"bst{sj}"], writes=[f"bmv{sj}"])

        def o_stats(n):
            sj = n % 2
            P.op("pool", lambda e: e.tensor_scalar(out=rstd8[sj], in0=bmv[sj][:, :, 1], scalar1=EPS, scalar2=None, op0=ALU.add),
                 reads=[f"bmv{sj}"], writes=[f"rstd8{sj}"])
            P.op("pool", lambda e: e.tensor_tensor(out=rstd8[sj], in0=rstd8[sj], in1=neghalf, op=ALU.pow),
                 reads=[f"rstd8{sj}", "neghalf"], writes=[f"rstd8{sj}"])
            P.op("pool", lambda e: e.tensor_tensor(out=nmr8[sj], in0=bmv[sj][:, :, 0], in1=rstd8[sj], op=ALU.mult),
                 reads=[f"bmv{sj}", f"rstd8{sj}"], writes=[f"nmr8{sj}"])
            P.op("pool", lambda e: e.tensor_scalar(out=nmr8[sj], in0=nmr8[sj], scalar1=-1.0, scalar2=None, op0=ALU.mult),
                 reads=[f"nmr8{sj}"], writes=[f"nmr8{sj}"])

        def o_norm(n):
            pA, pB = RETB[n % 2]
            sj = n % 2

            def f_norm(e):
                ins = None
                for h in range(8):
                    o = pf((pA, pB)[h // 4])[:, (h % 4) * 128:(h % 4 + 1) * 128]
                    ins = e.activation(out=retn[:, h * 128:(h + 1) * 128], in_=o, func=AF.Identity,
                                       scale=rstd8[sj][:, h:h + 1], bias=nmr8[sj][:, h:h + 1])
                return ins
            P.op("act", f_norm, reads=[f"P{pA}", f"P{pB}", f"nmr8{sj}", f"rstd8{sj}"], writes=["retn"])

        def o_rtp(n):
            def f_rtp(e):
                ins = None
                for c in range(8):
                    ins = e.transpose(out=pb(0)[:, c * 128:(c + 1) * 128], in_=retn[:, c * 128:(c + 1) * 128], identity=identb)
                return ins
            P.op("pe", f_rtp, reads=["retn", "identb"], writes=["P0"])

        def o_retnT(n):
            P.op("act", lambda e: e.activation(out=retnT[:, :, n * 128:(n + 1) * 128], in_=pb(0).rearrange("p (a b) -> p a b", a=8), func=AF.Copy),
                 reads=["P0"], writes=["wq", "wk", "wv0", "wv1", f"retnT_{n}"])

        ok = lambda n: 0 <= n < NCH
        o_sweep(0)
        o_stcopy(0)
        o_qfb(0)
        o_qdtp(0)
        o_qdTc(0)
        VB07 = [f"v{n}_{h}" for n in range(8) for h in range(2)]
        s0f = {}
        for t in range(NCH + 2):
            a, b_, c_ = t, t - 1, t - 2
            if ok(a + 1):
                o_qfb(a + 1)
            if ok(b_):
                o_bn(b_)
            if ok(c_):
                o_norm(c_)
            if ok(a + 1):
                o_sweep(a + 1)
                o_qdtp(a + 1)
            if ok(b_):
                o_stats(b_)
            if ok(c_):
                o_rtp(c_)
            if ok(a + 1):
                o_qdTc(a + 1)
                o_stcopy(a + 1)
            if ok(c_):
                o_retnT(c_)
            if ok(a):
                o_sc(a)
                o_mask(a)
                o_ret(a)
            if 9 <= t < 17:
                front_b(s0f[t - 9], 128, 0, xmT[:, :, (t - 9) * 128:(t - 8) * 128], f"xmT{t - 9}", alias=VB07)
            if 8 <= t < 16:
                s0f[t - 8] = front_a_cached(t - 8)

        ALL_RETN = [f"retnT_{n}" for n in range(NCH)]
        PH1_R2 = [f"q_rb{n}" for n in range(NCH)] + [f"kT{n}" for n in range(NCH)] + \
                 [f"v{n}_{h}" for n in range(NCH) for h in range(2)] + [f"stf{n}" for n in range(NCH)] + [f"stb{n}" for n in range(NCH)]
        PH1_TM = ["MT", "PT", "retn", "qfb0", "qfb1", "qdTc0", "qdTc1", "bmv0", "bmv1", "bst0", "bst1", "rstd80", "rstd81", "nmr80", "nmr81", "F0", "F1", "R0", "R1", "kcs", "qcs",
                  "k_rb0", "k_rb1", "kfb0", "kfb1", "xmTc0", "xmTc1", "xmTc2", "vtmp0_0", "vtmp0_1", "vtmp1_0", "vtmp1_1"]

        dump("q_rb", q_rb, [128, NCH, 512], BF16, [f"q_rb{n}" for n in range(NCH)])
        dump("kT", kT, [128, 4, HALF], BF16, [f"kT{n}" for n in range(NCH)])
        dump("vbuf", vbuf, [128, NCH, 1024], BF16, [f"v{n}_{h}" for n in range(NCH) for h in range(2)])
        dump("states", states, [128, NCH, 8, 128], BF16, [f"stf{n}" for n in range(NCH)] + [f"stb{n}" for n in range(NCH)])
        dump("MT", MT, [128, 2, 4, 128], F32, ["MT"])
        dump("F0", Fst[0], [128, 8, 128], F32, ["F0"])
        dump("retn", retn, [128, 1024], BF16, ["retn"])
        dump("PT", PT, [128, 2, 4, 128], BF16, ["PT"])
        dump("qdTc", qdTc[1], [128, 8, 128], BF16, ["qdTc1"])
        P.barrier(PH1_R2 + PH1_TM)
        P.dma("sp", lambda e: e.dma_start(out=fnwb, in_=fnw_d[0, :].partition_broadcast(128)), "fnwb", writes=["fnwb"])
        slot_use = {}

        def slab_load(g, src_v, col0, width=256):
            u = slot_use.get(g, 0)
            slot_use[g] = u + 1
            b = fsl[g][u % 2]
            nm = f"fsl{g}_{u % 2}"
            P.dma("pool", lambda e: e.dma_start(out=b[:, :, 0:width], in_=src_v[:, :, col0:col0 + width]), nm,
                  writes=[nm])
            return b, nm

        def emit_gate_tile():
            for k in range(8):
                P.op("dve", lambda e, k=k: e.tensor_scalar(out=diag, in0=identf, scalar1=modT[:, 16 + k, 0:1], scalar2=None, op0=ALU.mult),
                     reads=["identf", "modTg"], writes=["diag"])
                P.op("pe", lambda e, k=k: e.matmul(pf(k // 4)[:, (k % 4) * 128:(k % 4 + 1) * 128], lhsT=onesf, rhs=diag, start=True, stop=True),
                     reads=["diag", "onesf"], writes=[f"P{k // 4}"])
            for hv in range(2):
                P.op("act", lambda e, hv=hv: e.activation(out=gxb[:, hv * 512:(hv + 1) * 512], in_=pf(hv), func=AF.Copy),
                     reads=[f"P{hv}"], writes=[f"gxb{hv}"])

        step_specs = []
        for sb_ in range(2):
            for ctp_ in range(4):
                step_specs.append([(g_, win_v, off_ + ctp_ * 256) for g_, off_ in enumerate((O_H, O_CG, O_BG, O_ZA))])
            for ctp_ in range(4):
                step_specs.append([(0, wa_v, ctp_ * 256), (1, win_v, O_GA + ctp_ * 256)])
            for ctp_ in range(4):
                step_specs.append([(2, win_v, O_ZB + ctp_ * 256)])
            for ctp_ in range(4):
                step_specs.append([(3, wb_v, ctp_ * 256), (0, win_v, O_GB + ctp_ * 256)])
            step_specs.append("wo")
        step_res = {}
        step_ptr = [0]

        def issue_step(si):
            if si >= len(step_specs) or si in step_res:
                return
            spec = step_specs[si]
            if spec == "wo":
                for hv in range(2):
                    P.dma("pool", lambda e, hv=hv: e.dma_start(out=wo[hv], in_=wo_v[:, :, hv * 512:(hv + 1) * 512]), f"wo{hv}",
                          writes=[f"wo{hv}"])
                step_res[si] = None
            else:
                step_res[si] = [slab_load(g_, v_, c_) for (g_, v_, c_) in spec]

        def take_step():
            si = step_ptr[0]
            step_ptr[0] += 1
            issue_step(si)
            issue_step(si + 1)
            if si + 2 < len(step_specs) and step_specs[si + 2] == "wo":
                issue_step(si + 2)
            return step_res[si]

        issue_step(0)
        pair_i = [0]

        def next_pair():
            p = 2 + 2 * (pair_i[0] % 3)
            pair_i[0] += 1
            return p, p + 1

        def mm8(e, pbank, slab, c0_, rhs_fn, n_=512):
            ins = None
            for k in range(8):
                ins = e.matmul(pf(pbank)[:, 0:n_], lhsT=slab[:, k, c0_:c0_ + 128], rhs=rhs_fn(k), start=(k == 0), stop=(k == 7))
            return ins

        def stage0_chunk(sb_, jx):
            T0_ = sb_ * 1024
            front(x_d[T0_ + jx * 128:T0_ + (jx + 1) * 128, :], 128, 0, xmT[:, :, jx * 128:(jx + 1) * 128], f"xmT{jx}")

        def stage0_halo(sb_):
            T0_ = sb_ * 1024
            lrow = max(T0_ - 1, 0)
            i = gcc[0] % 2
            gcc[0] += 1
            P.dma("sp", lambda e: e.dma_start(out=xc[i][0:1, :], in_=x_d[lrow:lrow + 1, :]), "haloL", writes=[f"xc{i}"], n=1)
            P.dma("sp", lambda e: e.dma_start(out=xc[i][1:2, :], in_=x_d[T0_ + 1024:T0_ + 1025, :]), "haloR", writes=[f"xc{i}h"], n=1)
            P.op("act", lambda e: e.activation(out=xs[i][0:2, :], in_=xc[i][0:2, :], func=AF.Square, accum_out=ssb[i][0:2, :]),
                 reads=[f"xc{i}", f"xc{i}h"], writes=[f"xs{i}", f"ss{i}"])
            P.op("pool", lambda e: e.tensor_scalar(out=msb[i][0:2, :], in0=ssb[i][0:2, :], scalar1=1.0 / D, scalar2=EPS, op0=ALU.mult, op1=ALU.add),
                 reads=[f"ss{i}"], writes=[f"ms{i}"])
            P.op("pool", lambda e: e.tensor_tensor(out=rsb[i][0:2, :], in0=msb[i][0:2, :], in1=neghalf[0:2, 0:1], op=ALU.pow),
                 reads=[f"ms{i}", "neghalf"], writes=[f"rs{i}"])
            P.op("act", lambda e: e.activation(out=xs[i][0:2, :], in_=xc[i][0:2, :], func=AF.Copy, scale=rsb[i][0:2, :]),
                 reads=[f"xc{i}", f"xc{i}h", f"rs{i}"], writes=[f"xs{i}"])

            def f_tph(e):
                ins = None
                for k in range(8):
                    ins = e.transpose(out=pb(0)[:, k * 2:k * 2 + 2], in_=xs[i][0:2, k * 128:(k + 1) * 128], identity=identb[0:2, 0:2])
                return ins
            P.op("pe", f_tph, reads=[f"xs{i}", "identb"], writes=["P0"])

            def f_affh(e):
                ins = None
                for k in range(8):
                    ins = e.tensor_scalar(out=xmTh[:, k, :], in0=pb(0)[:, k * 2:k * 2 + 2], scalar1=Amod[:, k, 0:1], scalar2=modT[:, k, 0:1],
                                          op0=ALU.mult, op1=ALU.add)
                return ins
            P.op("dve", f_affh, reads=["P0", "Amod", "modT"], writes=["xmTh"])

        for sb in range(2):
            T0 = sb * 1024
            if sb == 0:
                stage0_halo(0)
            XM = [f"xmT{jx}" for jx in range(8)]

            for ctp in range(4):
                sl = dict(enumerate(take_step()))
                for c2 in range(2):
                    ct = ctp * 2 + c2
                    cb = cgh[ct % 2]
                    cbn = f"cgh{ct % 2}"
                    tb1 = t1[ct % 2]
                    def f_halo(e, c2=c2, sl=sl):
                        ins = None
                        for gi, g in enumerate((0, 1)):
                            for k in range(8):
                                ins = e.matmul(pf(1)[:, gi * 2:gi * 2 + 2], lhsT=sl[g][0][:, k, c2 * 128:(c2 + 1) * 128], rhs=xmTh[:, k, :],
                                               start=(k == 0), stop=(k == 7))
                        return ins
                    P.op("pe", f_halo, reads=[sl[0][1], sl[1][1], "xmTh"], writes=["P1"])
                    P.op("act", lambda e: e.activation(out=hh[:, 0:2], in_=pf(1)[:, 0:2], func=AF.Copy), reads=["P1"], writes=["hh"])

                    def f_halo2(e, cb=cb, sb=sb):
                        if sb == 0:
                            e.memset(cb[:, 0:1], 0.0)
                        else:
                            e.tensor_tensor(out=cb[:, 0:1], in0=pf(1)[:, 2:3], in1=hh[:, 0:1], op=ALU.mult)
                        return e.tensor_tensor(out=cb[:, 1025:1026], in0=pf(1)[:, 3:4], in1=hh[:, 1:2], op=ALU.mult)
                    P.op("dve", f_halo2, reads=["P1", "hh"], writes=[cbn + "h"])
                    for tb in range(2):
                        pa, pb_ = next_pair()
                        hs = h_sb[tb]

                        def f_hcg(e, c2=c2, tb=tb, pa=pa, pb_=pb_, sl=sl):
                            mm8(e, pa, sl[0][0], c2 * 128, lambda k: xmT[:, k, tb * 512:(tb + 1) * 512])
                            return mm8(e, pb_, sl[1][0], c2 * 128, lambda k: xmT[:, k, tb * 512:(tb + 1) * 512])
                        P.op("pe", f_hcg, reads=[sl[0][1], sl[1][1]] + XM[tb * 4:(tb + 1) * 4], writes=[f"P{pa}", f"P{pb_}"])
                        P.op("act", lambda e, pa=pa, hs=hs: e.activation(out=hs, in_=pf(pa), func=AF.Copy), reads=[f"P{pa}"], writes=[f"h_sb{tb}"])
                        P.op("dve", lambda e, pb_=pb_, hs=hs, cb=cb, tb=tb: e.tensor_tensor(out=cb[:, 1 + tb * 512:1 + (tb + 1) * 512], in0=pf(pb_), in1=hs, op=ALU.mult),
                             reads=[f"P{pb_}", f"h_sb{tb}"], writes=[cbn + f"_{tb}"])
                        pa2, pb2 = next_pair()
                        sz = sza[tb]

                        def f_bgza(e, c2=c2, tb=tb, pa2=pa2, pb2=pb2, sl=sl):
                            mm8(e, pa2, sl[2][0], c2 * 128, lambda k: xmT[:, k, tb * 512:(tb + 1) * 512])
                            return mm8(e, pb2, sl[3][0], c2 * 128, lambda k: xmT[:, k, tb * 512:(tb + 1) * 512])
                        P.op("pe", f_bgza, reads=[sl[2][1], sl[3][1]] + XM[tb * 4:(tb + 1) * 4], writes=[f"P{pa2}", f"P{pb2}"])
                        P.op("act", lambda e, pb2=pb2, sz=sz: e.activation(out=sz, in_=pf(pb2), func=AF.Silu), reads=[f"P{pb2}"], writes=[f"sza{tb}"])
                        P.op("dve", lambda e, pa2=pa2, sz=sz, tb1=tb1, tb=tb: e.tensor_tensor(out=tb1[:, tb * 512:(tb + 1) * 512], in0=pf(pa2), in1=sz, op=ALU.mult),
                             reads=[f"P{pa2}", f"sza{tb}"], writes=[f"t1_{ct % 2}_{tb}"])
                    CG = [cbn + "h", cbn + "_0", cbn + "_1"]
                    P.op("act", lambda e, cb=cb, ct=ct: e.activation(out=c0, in_=cb[:, 1:1025], func=AF.Identity, scale=vT[:, ct, 4:5], bias=vT[:, ct, 6:7]),
                         reads=CG + ["vT"], writes=["c0"])
                    P.op("dve", lambda e, cb=cb, ct=ct: e.scalar_tensor_tensor(out=c1, in0=cb[:, 0:1024], scalar=vT[:, ct, 3:4], in1=c0, op0=ALU.mult, op1=ALU.add),
                         reads=CG + ["c0", "vT"], writes=["c1"])
                    P.op("dve", lambda e, cb=cb, ct=ct: e.scalar_tensor_tensor(out=c0, in0=cb[:, 2:1026], scalar=vT[:, ct, 5:6], in1=c1, op0=ALU.mult, op1=ALU.add),
                         reads=CG + ["c1", "vT"], writes=["c0"])
                    P.op("pool", lambda e, ct=ct, tb1=tb1: e.tensor_tensor(out=bufA[:, ct, :], in0=c0, in1=tb1, op=ALU.mult),
                         reads=["c0", f"t1_{ct % 2}_0", f"t1_{ct % 2}_1"], writes=[f"bufA{ct}"])
            BA = [f"bufA{c}" for c in range(8)]

            for ctp in range(4):
                s_wa, s_ga = take_step()
                if sb == 0 and ctp == 1:
                    emit_gate_tile()
                for c2 in range(2):
                    ct = ctp * 2 + c2
                    for tb in range(2):
                        pa, pb_ = next_pair()

                        def f_b(e, c2=c2, tb=tb, pa=pa, pb_=pb_, s_wa=s_wa, s_ga=s_ga):
                            mm8(e, pa, s_wa[0], c2 * 128, lambda k: bufA[:, k, tb * 512:(tb + 1) * 512])
                            return mm8(e, pb_, s_ga[0], c2 * 128, lambda k: xmT[:, k, tb * 512:(tb + 1) * 512])
                        P.op("pe", f_b, reads=[s_wa[1], s_ga[1]] + BA + XM[tb * 4:(tb + 1) * 4], writes=[f"P{pa}", f"P{pb_}"])
                        sg = sig[tb]
                        P.op("act", lambda e, pb_=pb_, sg=sg: e.activation(out=sg, in_=pf(pb_), func=AF.Sigmoid), reads=[f"P{pb_}"], writes=[f"sig{tb}"])
                        P.op("dve", lambda e, pa=pa, sg=sg, ct=ct, tb=tb: e.tensor_tensor(out=mbuf[:, ct, tb * 512:(tb + 1) * 512], in0=pf(pa), in1=sg, op=ALU.mult),
                             reads=[f"P{pa}", f"sig{tb}"], writes=[f"m{ct}_{tb}"])

            for ctp in range(4):
                (s_zb,) = take_step()
                for c2 in range(2):
                    ct = ctp * 2 + c2
                    for tb in range(2):
                        pa, pb_ = next_pair()
                        P.op("pe", lambda e, c2=c2, tb=tb, pa=pa, s_zb=s_zb: mm8(e, pa, s_zb[0], c2 * 128, lambda k: xmT[:, k, tb * 512:(tb + 1) * 512]),
                             reads=[s_zb[1]] + XM[tb * 4:(tb + 1) * 4], writes=[f"P{pa}", f"P{pb_}"])
                        sz = sza[tb]
                        P.op("act", lambda e, pa=pa, sz=sz: e.activation(out=sz, in_=pf(pa), func=AF.Silu), reads=[f"P{pa}"], writes=[f"sza{tb}"])
                        P.op("dve", lambda e, sz=sz, ct=ct, tb=tb, T0=T0: e.scalar_tensor_tensor(
                            out=bufA[:, ct, tb * 512:(tb + 1) * 512], in0=retnT[:, ct, T0 + tb * 512:T0 + (tb + 1) * 512],
                            scalar=vT[:, ct, 7:8], in1=sz, op0=ALU.mult, op1=ALU.mult),
                             reads=[f"sza{tb}", "vT"] + ALL_RETN, writes=[f"bufA{ct}"])

            for ctp in range(4):
                s_wb, s_gb = take_step()
                if ctp == 3:
                    for hv in range(2):
                        P.op("dve", lambda e, hv=hv: e.tensor_tensor(out=wo[hv], in0=wo[hv],
                                                                     in1=gxb[:, hv * 512:(hv + 1) * 512].unsqueeze(1).broadcast_to([128, 8, 512]),
                                                                     op=ALU.mult),
                             reads=[f"wo{hv}", f"gxb{hv}"], writes=[f"wo{hv}"])
                for c2 in range(2):
                    ct = ctp * 2 + c2
                    for tb in range(2):
                        pa, pb_ = next_pair()

                        def f_d(e, c2=c2, tb=tb, pa=pa, pb_=pb_, s_wb=s_wb, s_gb=s_gb):
                            mm8(e, pa, s_wb[0], c2 * 128, lambda k: bufA[:, k, tb * 512:(tb + 1) * 512])
                            return mm8(e, pb_, s_gb[0], c2 * 128, lambda k: xmT[:, k, tb * 512:(tb + 1) * 512])
                        P.op("pe", f_d, reads=[s_wb[1], s_gb[1]] + BA + XM[tb * 4:(tb + 1) * 4], writes=[f"P{pa}", f"P{pb_}"])
                        sg = sig[tb]
                        td = tmpd[tb]
                        P.op("act", lambda e, pb_=pb_, sg=sg: e.activation(out=sg, in_=pf(pb_), func=AF.Sigmoid), reads=[f"P{pb_}"], writes=[f"sig{tb}"])
                        P.op("dve", lambda e, pa=pa, sg=sg, td=td: e.tensor_tensor(out=td, in0=pf(pa), in1=sg, op=ALU.mult),
                             reads=[f"P{pa}", f"sig{tb}"], writes=[f"tmpd{tb}"])
                        P.op("pool", lambda e, td=td, ct=ct, tb=tb: e.tensor_tensor(out=mbuf[:, ct, tb * 512:(tb + 1) * 512],
                                                                                    in0=mbuf[:, ct, tb * 512:(tb + 1) * 512], in1=td, op=ALU.add),
                             reads=[f"tmpd{tb}", f"m{ct}_{tb}"], writes=[f"m{ct}_{tb}"])
            MM = [f"m{c}_{t}" for c in range(8) for t in range(2)]

            take_step()

            ring = [xn[0], xn[1], c1, h_sbF]
            ringn = [["xn0"], ["xn1"], ["c1"], ["h_sb0", "h_sb1"]]

            def e_load(jx):
                r = jx % 4
                r0 = T0 + jx * 128
                P.dma("sp", lambda e: e.dma_start(out=ring[r], in_=x_d[r0:r0 + 128, :]), f"xnr{r}", writes=ringn[r])

            def e_mm(jx):
                r = jx % 4
                xnb = ring[r]
                pa, pb_ = next_pair()

                def f_e(e):
                    ins = None
                    for hv, pbk in enumerate((pa, pb_)):
                        for k in range(8):
                            ins = e.matmul(pf(pbk), lhsT=mbuf[:, k, jx * 128:(jx + 1) * 128], rhs=wo[hv][:, k, :], start=(k == 0), stop=(k == 7))
                    return ins
                P.op("pe", f_e, reads=MM + ["wo0", "wo1"], writes=[f"P{pa}", f"P{pb_}"])
                def f_res(e):
                    e.tensor_tensor(out=xnb[:, 0:512], in0=pf(pa), in1=xnb[:, 0:512], op=ALU.add)
                    return e.tensor_tensor(out=xnb[:, 512:1024], in0=pf(pb_), in1=xnb[:, 512:1024], op=ALU.add)
                P.op("dve", f_res, reads=[f"P{pa}", f"P{pb_}"] + ringn[r], writes=ringn[r])

            def e_stats(jx):
                r = jx % 4
                xnb = ring[r]
                P.op("act", lambda e: e.activation(out=c0, in_=xnb, func=AF.Square, accum_out=sse[r]),
                     reads=ringn[r], writes=["c0", f"sse{r}"])
                P.op("pool", lambda e: e.tensor_scalar(out=mse[r], in0=sse[r], scalar1=1.0 / D, scalar2=EPS, op0=ALU.mult, op1=ALU.add),
                     reads=[f"sse{r}"], writes=[f"mse{r}"])
                P.op("pool", lambda e: e.tensor_tensor(out=rse[r], in0=mse[r], in1=neghalf[:, 0:1], op=ALU.pow),
                     reads=[f"mse{r}", "neghalf"], writes=[f"rse{r}"])

            def e_part2(jx):
                r = jx % 4
                r0 = T0 + jx * 128
                xnb = ring[r]
                P.op("dve", lambda e: e.scalar_tensor_tensor(out=xnb, in0=xnb, scalar=rse[r], in1=fnwb, op0=ALU.mult, op1=ALU.mult),
                     reads=ringn[r] + [f"rse{r}", "fnwb"], writes=ringn[r])
                tok = P.dma("sp", lambda e: e.dma_start(out=out_d[r0:r0 + 128, :], in_=xnb), f"outr{r}", reads=ringn[r])
                P.out_tokens.append(tok)

            fi = {}
            e_load(0)
            if sb == 0:
                fi[0] = front_a_cached(8)
                front_b(fi[0], 128, 0, xmT[:, :, 0:128], "xmT0")
            for jx in range(8):
                if jx + 1 < 8:
                    e_load(jx + 1)
                    if sb == 0:
                        fi[jx + 1] = front_a_cached(8 + jx + 1)
                if jx >= 2:
                    e_part2(jx - 2)
                e_mm(jx)
                if sb == 0 and jx + 1 < 8:
                    front_b(fi[jx + 1], 128, 0, xmT[:, :, (jx + 1) * 128:(jx + 2) * 128], f"xmT{jx + 1}")
                e_stats(jx)
            e_part2(6)
            e_part2(7)
            if sb == 0:
                stage0_halo(1)

        dump("retnT", retnT, [128, 8, HALF], BF16, ALL_RETN)
        dump("xmT", xmT, [128, 8, 1024], BF16, XM)
        dump("bufA", bufA, [128, 8, 1024], BF16, BA)
        dump("mbuf", mbuf, [128, 8, 1024], BF16, MM)
        dump("vT", vT, [128, 8, 16], F32, ["vT"])
        dump("modT", modT, [128, 24, 2], F32, ["modT"])
        dump("Amod", Amod, [128, 8, 2], F32, ["Amod"])
        dump("kdec", kdec, [128, 16], F32, ["dec"])
        dump("qdec", qdec, [128, 16], F32, ["dec"])
        dump("dch", dch, [128, 16], F32, ["dec"])
        dump("gxb", gxb, [128, D], F32, ["gxb0", "gxb1"])
        fin = {}
        for skey, val, _ in P.out_tokens:
            fin[skey] = max(fin.get(skey, 0), val)
        P.streams["sp"].append((list(fin.items()), None, None))

        sems = {}
        keys = set()
        for st in P.streams.values():
            for waits, fn, inc in st:
                for skey, _ in waits:
                    keys.add(skey)
                if inc is not None:
                    keys.add(inc[0])
        for idx, k in enumerate(sorted(keys, key=str)):
            sems[k] = es.enter_context(nc.semaphore(f"sem{idx}"))

        def replay(name, eng):
            for waits, fn, inc in P.streams[name]:
                for skey, val in waits:
                    eng.wait_ge(sems[skey], val)
                if fn is None:
                    continue
                ins = fn(eng)
                ins.then_inc(sems[inc[0]], inc[1])

        block = es.enter_context(nc.Block())

        @block.sync
        def _(e):
            replay("sp", e)

        @block.tensor
        def _(e):
            replay("pe", e)

        @block.scalar
        def _(e):
            replay("act", e)

        @block.vector
        def _(e):
            replay("dve", e)

        @block.gpsimd
        def _(e):
            replay("pool", e)

    return nc


def _host_consts(flip):
    ident = np.eye(128, dtype=np.float32)
    jj = np.arange(128, dtype=np.float32)[:, None]
    ii = np.arange(128, dtype=np.float32)[None, :]
    dmat = ii - jj
    dpn = np.concatenate([np.maximum(dmat, 0), np.maximum(-dmat, 0), np.broadcast_to(ii + 1, (128, 128)),
                          np.broadcast_to(128 - ii, (128, 128))], axis=1).astype(np.float32)
    p = np.arange(128, dtype=np.float32)
    pidx = np.stack([p, 127 - p, p + 1, 128 - p], axis=1).astype(np.float32)
    t = np.arange(SEQ)
    pos = (SEQ - 1 - t) if flip else t
    row = (pos // 64).astype(np.float32)
    col = (pos % 64).astype(np.float32)
    nf = 16
    inv = (10000.0 ** (-np.arange(nf, dtype=np.float32) / nf)).astype(np.float32)
    ang = np.concatenate([row[:, None] * inv, col[:, None] * inv], axis=-1).astype(np.float32)
    cos = np.cos(ang).astype(np.float32)
    sin = np.sin(ang).astype(np.float32)
    qt = np.concatenate([cos, cos, -sin, sin], axis=1)
    kt = (qt * np.float32(0.125)).astype(np.float32)
    rope = np.zeros((34, 128, 256), np.float32)
    rope[:32, :, 0:128] = kt.reshape(32, 128, 128)
    rope[:32, :, 128:256] = qt.reshape(32, 128, 128)
    rope[32:, :, 0:64] = 0.125
    return ident, dpn, pidx, rope


_NC_CACHE = {}


def _in_maps(x, c, ctx, c_ctx, norm_w, ada_w, ada_b, w_in, conv_w, conv_b, decay_logit, gn_w, w_a, w_b, w_out, final_norm_w):
    f = lambda a: np.ascontiguousarray(np.asarray(a, dtype=np.float32))
    x, c, ctx, c_ctx = f(x), f(c), f(ctx), f(c_ctx)
    norm_w, ada_w, ada_b, w_in = f(norm_w)[0], f(ada_w)[0], f(ada_b)[0], f(w_in)[0]
    conv_w, conv_b, decay_logit, gn_w = f(conv_w)[0], f(conv_b)[0], f(decay_logit)[0], f(gn_w)[0]
    w_a, w_b, w_out, fnw = f(w_a)[0], f(w_b)[0], f(w_out)[0], f(final_norm_w)
    consts = {fl: _host_consts(fl) for fl in (False, True)}
    in_maps = []
    for core in range(8):
        b, half = core // 2, core % 2
        flip = half == 1
        ident, dpn, pidx, rope = consts[flip]
        xb = x[b, ::-1] if flip else x[b]
        cb = ctx[b, ::-1] if flip else ctx[b]
        cw = conv_w[::-1] if flip else conv_w
        dl = decay_logit[::-1] if flip else decay_logit
        vecs = np.stack([c[b], c_ctx, norm_w, cw[0], cw[1], cw[2], conv_b, gn_w,
                         ada_b[0:D], ada_b[D:2 * D], ada_b[2 * D:3 * D]], axis=0)
        in_maps.append({
            "x": np.ascontiguousarray(xb), "ctx": np.ascontiguousarray(cb), "vecs": np.ascontiguousarray(vecs),
            "ada_w": ada_w, "w_in": w_in, "w_a": w_a, "w_b": w_b, "w_out": w_out,
            "dl": np.ascontiguousarray(dl.reshape(1, 16)), "fnw": fnw.reshape(1, D),
            "ident": ident, "dpn": dpn, "pidx": pidx, "rope": rope,
        })
    return in_maps


def _gather(results):
    out = np.empty((4, SEQ, D), np.float32)
    for core in range(8):
        b, half = core // 2, core % 2
        o = np.asarray(results[core]["out"], dtype=np.float32)
        if half == 0:
            out[b, 0:HALF] = o
        else:
            out[b, HALF:SEQ] = o[::-1]
    return out


def kernel(x, c, ctx, c_ctx, norm_w, ada_w, ada_b, w_in, conv_w, conv_b, decay_logit, gn_w, w_a, w_b, w_out, final_norm_w):
    in_maps = _in_maps(x, c, ctx, c_ctx, norm_w, ada_w, ada_b, w_in, conv_w, conv_b, decay_logit, gn_w, w_a, w_b, w_out, final_norm_w)
    if "nc" not in _NC_CACHE:
        _NC_CACHE["nc"] = build_program()
    res = run_bass_kernel_spmd(_NC_CACHE["nc"], in_maps, core_ids=list(range(8)))
    return _gather(res.results)
```

```python
import contextlib
import numpy as np
import concourse.bass as bass
import concourse.mybir as mybir
from concourse.bass_utils import run_bass_kernel_spmd

F32 = mybir.dt.float32
BF16 = mybir.dt.bfloat16
AF = mybir.ActivationFunctionType
ALU = mybir.AluOpType

D = 1024
SEQ = 4096
HALF = 2048
NCH = 16
EPS = 1e-6
H = 8
O_H, O_BG, O_CG, O_ZA, O_Q, O_K, O_V, O_ZB, O_GA, O_GB = 0, 1024, 2048, 3072, 4096, 4608, 5120, 6144, 7168, 8192


class Prog:
    def __init__(self):
        self.streams = {e: [] for e in ("pe", "act", "dve", "pool", "sp")}
        self.cnt = {e: 0 for e in self.streams}
        self.lastw = {}
        self.readers = {}
        self.waited = {e: {} for e in self.streams}
        self.dcnt = {}
        self.out_tokens = []

    def _waits(self, eng, reads, writes):
        toks = []
        for r in reads:
            t = self.lastw.get(r)
            if t is not None:
                toks.append(("raw", t))
            if len(r) == 2 and r[0] == "P" and r[1].isdigit():
                for t in self.readers.get(r, ()):
                    toks.append(("rar", t))
        for w in writes:
            t = self.lastw.get(w)
            if t is not None:
                toks.append(("waw", t))
            for t in self.readers.get(w, ()):
                toks.append(("war", t))
        need = {}
        for kind, (skey, val, teng) in toks:
            if teng == eng and (eng == "pe" or kind == "rar"):
                continue
            if val > need.get(skey, 0):
                need[skey] = val
        waits = []
        for skey, val in need.items():
            if self.waited[eng].get(skey, 0) >= val:
                continue
            self.waited[eng][skey] = val
            waits.append((skey, val))
        return waits

    def _record(self, tok, reads, writes):
        for r in reads:
            self.readers.setdefault(r, []).append(tok)
        for w in writes:
            self.lastw[w] = tok
            self.readers[w] = []

    def op(self, eng, fn, reads=(), writes=()):
        waits = self._waits(eng, reads, writes)
        self.cnt[eng] += 1
        tok = (eng, self.cnt[eng], eng)
        self.streams[eng].append((waits, fn, (eng, 1)))
        self._record(tok, reads, writes)
        return tok

    def barrier(self, resources):
        for eng in self.streams:
            waits = self._waits(eng, (), resources)
            if waits:
                self.streams[eng].append((waits, None, None))

    def dma(self, q, fn, slot, reads=(), writes=(), n=1):
        waits = self._waits(q, reads, writes)
        self.dcnt[slot] = self.dcnt.get(slot, 0) + n
        skey = ("dma", slot)
        tok = (skey, 16 * self.dcnt[slot], "dma")
        self.streams[q].append((waits, fn, (skey, 16)))
        self._record(tok, reads, writes)
        return tok


def build_program(debug=False):
    nc = bass.Bass("TRN2", target_bir_lowering=False)

    def din(name, shape):
        return nc.dram_tensor(name, list(shape), F32, kind="ExternalInput").ap()

    x_d = din("x", (SEQ, D))
    ctx_d = din("ctx", (256, D))
    vecs_d = din("vecs", (11, D))
    ada_d = din("ada_w", (D, 3 * D))
    win_d = din("w_in", (D, 9216))
    wa_d = din("w_a", (D, D))
    wb_d = din("w_b", (D, D))
    wo_d = din("w_out", (D, D))
    dl_d = din("dl", (1, 16))
    fnw_d = din("fnw", (1, D))
    ident_d = din("ident", (128, 128))
    dpn_d = din("dpn", (128, 512))
    pidx_d = din("pidx", (128, 4))
    rope_d = din("rope", (34, 128, 256))
    out_d = nc.dram_tensor("out", [HALF, D], F32, kind="ExternalOutput").ap()

    P = Prog()
    es = contextlib.ExitStack()

    def dump(name, ap, shape, dt, reads):
        if not debug:
            return
        t = nc.dram_tensor("dbg_" + name, list(shape), dt, kind="ExternalOutput").ap()
        tok = P.dma("sp", lambda e: e.dma_start(out=t, in_=ap), "dbg_" + name, reads=reads)
        P.out_tokens.append(tok)
    with es:
        def arena(name, nbytes):
            return es.enter_context(nc.sbuf_tensor(name, [128, nbytes // 2], BF16))

        class Bump:
            def __init__(self, name, nbytes):
                self.t = arena(name, nbytes)
                self.n = nbytes
                self.off = 0

            def reset(self, off=0):
                self.off = off

            def get(self, shape, dt):
                esz = 2 if dt == BF16 else 4
                n = int(np.prod(shape[1:]))
                nb = (n * esz + 63) // 64 * 64
                assert self.off + nb <= self.n, (self.off, nb, self.n, shape)
                ap = self.t[:, self.off // 2:(self.off + n * esz) // 2]
                self.off += nb
                if dt != BF16:
                    ap = ap.bitcast(dt)
                if len(shape) == 3:
                    ap = ap.rearrange("p (a b) -> p a b", a=shape[1])
                elif len(shape) == 4:
                    ap = ap.rearrange("p (a b c) -> p a b c", a=shape[1], b=shape[2])
                return ap

        KB = 1024
        CM = Bump("cm", 20 * KB)
        R1 = Bump("r1", 32 * KB)
        R2 = Bump("r2", 96 * KB)
        TM = Bump("tm", 59 * KB)

        PS = [es.enter_context(nc.psum_tensor(f"P{i}", [128, 512], F32)) for i in range(8)]

        def pf(i):
            return PS[i][:, :]

        def pb(i):
            return PS[i][:, :].bitcast(BF16)

        xc = [CM.get([128, D], F32) for _ in range(2)]
        xs = [CM.get([128, D], BF16) for _ in range(2)]
        ropet = [CM.get([128, 256], F32) for _ in range(3)]
        vT = CM.get([128, 8, 16], F32)
        scT = CM.get([128, 8, 2], BF16)
        modT = CM.get([128, 24, 2], F32)
        Amod = CM.get([128, 8, 2], F32)
        identf = CM.get([128, 128], F32)
        identb = CM.get([128, 128], BF16)
        pidx = CM.get([128, 4], F32)
        nlg = CM.get([128, 16], F32)
        kdec = CM.get([128, 16], F32)
        qdec = CM.get([128, 16], F32)
        dch = CM.get([128, 16], F32)
        dchs0 = CM.get([128, 8], F32)
        neghalf = CM.get([128, 8], F32)
        ssb = [CM.get([128, 1], F32) for _ in range(2)]
        msb = [CM.get([128, 1], F32) for _ in range(2)]
        rsb = [CM.get([128, 1], F32) for _ in range(2)]
        xmTh = CM.get([128, 8, 2], BF16)
        rs_own = CM.get([128, 16], F32)
        sse = [CM.get([128, 1], F32) for _ in range(4)]
        mse = [CM.get([128, 1], F32) for _ in range(4)]
        rse = [CM.get([128, 1], F32) for _ in range(4)]
        diag = CM.get([128, 128], F32)
        onesf = CM.get([128, 128], F32)

        wq = R1.get([128, 8, 512], BF16)
        wk = R1.get([128, 8, 512], BF16)
        wv = R1.get([128, 8, 1024], BF16)
        R1.reset()
        retnT = R1.get([128, 8, HALF], BF16)

        q_rb = R2.get([128, NCH, 512], BF16)
        kT = R2.get([128, 4, HALF], BF16)
        vbuf = R2.get([128, NCH, 1024], BF16)
        states = R2.get([128, NCH, 8, 128], BF16)
        R2.reset(64 * KB)
        adas = [R2.get([128, 8, 512], BF16) for _ in range(3)]
        R2.reset()
        bufA = R2.get([128, 8, 1024], BF16)
        mbuf = R2.get([128, 8, 1024], BF16)
        xmT = R2.get([128, 8, 1024], BF16)
        fsl = [[R2.get([128, 8, 256], BF16) for _ in range(2)] for _ in range(4)]
        wo = [R2.get([128, 8, 512], BF16) for _ in range(2)]

        vrow = TM.get([128, D], F32)
        dpn = TM.get([128, 512], F32)
        dlb = TM.get([128, 16], F32)
        marg = TM.get([128, 8, 128], F32)
        tmp16 = TM.get([128, 16], F32)
        TM.reset()
        MT = TM.get([128, 2, 4, 128], F32)
        xmTc = [TM.get([128, 8, 128], BF16) for _ in range(3)]
        vtmp = [TM.get([128, 1024], BF16) for _ in range(2)]
        kc = TM.get([128, 8, 64], BF16)
        ks = TM.get([128, 8, 64], BF16)
        qc = TM.get([128, 8, 64], BF16)
        qs = TM.get([128, 8, 64], BF16)
        k_rb = [TM.get([128, 8, 64], BF16) for _ in range(2)]
        kfb = [TM.get([128, 8, 128], BF16) for _ in range(2)]
        Rst = [TM.get([128, 8, 128], F32) for _ in range(2)]
        Fst = [TM.get([128, 8, 128], F32) for _ in range(2)]
        qfb = [TM.get([128, 8, 128], BF16) for _ in range(2)]
        qdTc = [TM.get([128, 8, 128], BF16) for _ in range(2)]
        PT = TM.get([128, 2, 4, 128], BF16)
        retn = TM.get([128, 1024], BF16)
        bst = [TM.get([128, 8, 6], F32) for _ in range(2)]
        bmv = [TM.get([128, 8, 2], F32) for _ in range(2)]
        rstd8 = [TM.get([128, 8], F32) for _ in range(2)]
        nmr8 = [TM.get([128, 8], F32) for _ in range(2)]
        p1_end = TM.off
        TM.reset()
        gxb = TM.get([128, D], F32)
        fnwb = TM.get([128, D], F32)
        h_sbF = TM.get([128, 1024], F32)
        h_sb = [h_sbF[:, 0:512], h_sbF[:, 512:1024]]
        cgh = [TM.get([128, 1026], F32) for _ in range(2)]
        sza = [TM.get([128, 512], F32) for _ in range(2)]
        t1 = [TM.get([128, 1024], F32) for _ in range(2)]
        c0 = TM.get([128, 1024], F32)
        c1 = TM.get([128, 1024], F32)
        sig = [TM.get([128, 512], F32) for _ in range(2)]
        tmpd = [TM.get([128, 512], F32) for _ in range(2)]
        xn = [TM.get([128, D], F32) for _ in range(2)]
        hh = TM.get([128, 4], F32)

        P.dma("sp", lambda e: e.dma_start(out=vrow[0:11, :], in_=vecs_d[:, :]), "vrow", writes=["vrow"])
        P.dma("sp", lambda e: e.dma_start(out=identf, in_=ident_d[:, :]), "identf", writes=["identf"])
        P.dma("sp", lambda e: e.dma_start(out=dpn, in_=dpn_d[:, :]), "dpn", writes=["dpn"])
        P.dma("sp", lambda e: e.dma_start(out=pidx, in_=pidx_d[:, :]), "pidx", writes=["pidx"])
        P.dma("sp", lambda e: e.dma_start(out=dlb, in_=dl_d[0, :].partition_broadcast(128)), "dlb", writes=["dlb"])
        win_v = win_d.rearrange("(k p) c -> p k c", p=128)
        ada_v = ada_d.rearrange("(k p) c -> p k c", p=128)
        wa_v = wa_d.rearrange("(k p) c -> p k c", p=128)
        wb_v = wb_d.rearrange("(k p) c -> p k c", p=128)
        wo_v = wo_d.rearrange("(k p) c -> p k c", p=128)

        def ada_load(i):
            b = adas[i % 3]
            P.dma("pool", lambda e: e.dma_start(out=b, in_=ada_v[:, :, i * 512:(i + 1) * 512]),
                  f"adas{i % 3}", writes=[f"adas{i % 3}"])

        ada_load(0)
        ada_load(1)
        ada_load(2)

        P.op("dve", lambda e: e.tensor_copy(out=identb, in_=identf), reads=["identf"], writes=["identb"])

        def f_vtr(e):
            ins = None
            for k in range(8):
                ins = e.transpose(out=pf(6)[:, k * 16:k * 16 + 11], in_=vrow[0:11, k * 128:(k + 1) * 128],
                                  identity=identf[0:11, 0:11])
            return ins
        P.op("pe", f_vtr, reads=["vrow", "identf"], writes=["P6"])
        P.op("dve", lambda e: e.tensor_copy(out=vT[:, :, 0:11], in_=pf(6)[:, 0:128].rearrange("p (a b) -> p a b", a=8)[:, :, 0:11]),
             reads=["P6"], writes=["vT"])
        P.op("act", lambda e: e.activation(out=scT, in_=vT[:, :, 0:2], func=AF.Silu), reads=["vT"], writes=["scT"])

        P.op("act", lambda e: e.activation(out=tmp16, in_=dlb, func=AF.Exp, scale=-1.0), reads=["dlb"], writes=["tmp16"])
        P.op("act", lambda e: e.activation(out=nlg, in_=tmp16, func=AF.Ln, bias=1.0), reads=["tmp16"], writes=["nlg"])

        def f_decarg(e):
            e.memset(neghalf, -0.5)
            e.memset(onesf, 1.0)
            e.tensor_scalar(out=kdec[:, 0:8], in0=nlg[:, 0:8], scalar1=pidx[:, 1:2], scalar2=None, op0=ALU.mult)
            e.tensor_scalar(out=kdec[:, 8:16], in0=nlg[:, 8:16], scalar1=pidx[:, 0:1], scalar2=None, op0=ALU.mult)
            e.tensor_scalar(out=qdec[:, 0:8], in0=nlg[:, 0:8], scalar1=pidx[:, 2:3], scalar2=None, op0=ALU.mult)
            e.tensor_scalar(out=qdec[:, 8:16], in0=nlg[:, 8:16], scalar1=pidx[:, 3:4], scalar2=None, op0=ALU.mult)
            return e.tensor_scalar(out=dch, in0=nlg, scalar1=128.0, scalar2=None, op0=ALU.mult)
        P.op("dve", f_decarg, reads=["nlg", "pidx"], writes=["decarg", "neghalf", "onesf"])

        def f_decexp(e):
            e.activation(out=kdec, in_=kdec, func=AF.Exp, scale=-1.0)
            e.activation(out=qdec, in_=qdec, func=AF.Exp, scale=-1.0)
            return e.activation(out=dch, in_=dch, func=AF.Exp, scale=-1.0)
        P.op("act", f_decexp, reads=["decarg"], writes=["dec"])

        def f_dchs(e):
            e.memset(dchs0[0:64, :], 0.0)
            return e.tensor_copy(out=dchs0[64:128, :], in_=dch[64:128, 8:16])
        P.op("dve", f_dchs, reads=["dec"], writes=["dchs0"])

        def f_marg0(e):
            ins = None
            for h in range(8):
                ins = e.tensor_scalar(out=marg[:, h, :], in0=dpn[:, 0:128], scalar1=nlg[:, h:h + 1], scalar2=None, op0=ALU.mult)
            return ins
        P.op("dve", f_marg0, reads=["nlg", "dpn"], writes=["marg"])

        def f_marg(e):
            ins = None
            for h in range(8):
                ins = e.scalar_tensor_tensor(out=marg[:, h, :], in0=dpn[:, 128:256], scalar=nlg[:, 8 + h:9 + h],
                                             in1=marg[:, h, :], op0=ALU.mult, op1=ALU.add)
            return ins
        P.op("dve", f_marg, reads=["nlg", "dpn", "marg"], writes=["marg"])
        def f_marg3(e):
            ins = None
            for h in range(8):
                if h % 2 == 0:
                    iv, col = dpn[:, 256:384], nlg[:, h:h + 1]
                else:
                    iv, col = dpn[:, 384:512], nlg[:, 8 + h:9 + h]
                ins = e.scalar_tensor_tensor(out=marg[:, h, :], in0=iv, scalar=col, in1=marg[:, h, :], op0=ALU.mult, op1=ALU.subtract)
            return ins
        P.op("dve", f_marg3, reads=["nlg", "dpn", "marg"], writes=["marg"])
        P.op("act", lambda e: e.activation(out=marg, in_=marg, func=AF.Exp), reads=["marg"], writes=["marg2"])
        P.op("dve", lambda e: e.tensor_scalar(out=diag, in0=identf, scalar1=1.0, scalar2=None, op0=ALU.add), reads=["identf"], writes=["diag"])

        def f_mt(e):
            ins = None
            for h in range(8):
                ins = e.tensor_tensor(out=MT[:, h % 2, h // 2, :], in0=marg[:, h, :], in1=diag, op=ALU.mult)
            return ins
        P.op("dve", f_mt, reads=["marg2", "diag", "vT", "P6"], writes=["MT"])

        gcc = [0]

        def front_a1(rows_ap, np_):
            i = gcc[0] % 2
            gcc[0] += 1
            xcb, xsb = xc[i], xs[i]
            P.dma("sp", lambda e: e.dma_start(out=xcb[0:np_, :], in_=rows_ap), f"xc{i}", writes=[f"xc{i}", f"xc{i}h"])
            P.op("act", lambda e: e.activation(out=xsb[0:np_, :], in_=xcb[0:np_, :], func=AF.Square, accum_out=ssb[i][0:np_, :]),
                 reads=[f"xc{i}"], writes=[f"xs{i}", f"ss{i}"])
            return i

        def front_a2(i, np_, keep=None):
            xcb, xsb = xc[i], xs[i]
            rs_ap, rs_nm = (rsb[i][0:np_, :], f"rs{i}") if keep is None else keep
            P.op("pool", lambda e: e.tensor_scalar(out=msb[i][0:np_, :], in0=ssb[i][0:np_, :], scalar1=1.0 / D, scalar2=EPS, op0=ALU.mult, op1=ALU.add),
                 reads=[f"ss{i}"], writes=[f"ms{i}"])
            P.op("pool", lambda e: e.tensor_tensor(out=rs_ap, in0=msb[i][0:np_, :], in1=neghalf[0:np_, 0:1], op=ALU.pow),
                 reads=[f"ms{i}", "neghalf"], writes=[rs_nm])
            P.op("act", lambda e: e.activation(out=xsb[0:np_, :], in_=xcb[0:np_, :], func=AF.Copy, scale=rs_ap),
                 reads=[f"xc{i}", rs_nm], writes=[f"xs{i}"])

        def front_a(rows_ap, np_, keep=None):
            i = front_a1(rows_ap, np_)
            front_a2(i, np_, keep=keep)
            return i

        def front_a_cached(n):
            i = gcc[0] % 2
            gcc[0] += 1
            xcb, xsb = xc[i], xs[i]
            P.dma("sp", lambda e: e.dma_start(out=xcb, in_=x_d[n * 128:(n + 1) * 128, :]), f"xc{i}", writes=[f"xc{i}", f"xc{i}h"])
            P.op("act", lambda e: e.activation(out=xsb, in_=xcb, func=AF.Copy, scale=rs_own[:, n:n + 1]),
                 reads=[f"xc{i}", f"rso{n}"], writes=[f"xs{i}"])
            return i

        def front_b(i, np_, r, dst, dst_name, ptp=0, alias=()):
            xsb = xs[i]

            def f_tp(e):
                ins = None
                for k in range(8):
                    ins = e.transpose(out=pb(ptp)[:, k * 128:k * 128 + np_], in_=xsb[0:np_, k * 128:(k + 1) * 128],
                                      identity=identb[0:np_, 0:np_])
                return ins
            P.op("pe", f_tp, reads=[f"xs{i}", "identb"], writes=[f"P{ptp}"])

            def f_aff(e):
                ins = None
                for k in range(8):
                    ins = e.tensor_scalar(out=dst[:, k, :], in0=pb(ptp)[:, k * 128:k * 128 + np_],
                                          scalar1=Amod[:, k, r:r + 1], scalar2=modT[:, k, r:r + 1],
                                          op0=ALU.mult, op1=ALU.add)
                return ins
            P.op("dve", f_aff, reads=[f"P{ptp}", "Amod", "modT"], writes=[dst_name] + list(alias))

        def front(rows_ap, np_, r, dst, dst_name, ti=None, ptp=0):
            i = front_a(rows_ap, np_)
            front_b(i, np_, r, dst, dst_name, ptp=ptp)
            return i


        def rope_evac(pbank, pname, tab_lo, tab, tabname, dst_c, dst_s, dst_name):
            def f(e):
                src = pf(pbank).rearrange("p (h t f) -> p h t f", h=8, t=2)
                e.tensor_tensor(out=dst_c, in0=pf(pbank).rearrange("p (h f) -> p h f", h=8),
                                in1=tab[:, tab_lo:tab_lo + 64].unsqueeze(1).broadcast_to([128, 8, 64]), op=ALU.mult)
                e.tensor_tensor(out=dst_s[:, :, 0:32], in0=src[:, :, 1, :],
                                in1=tab[:, tab_lo + 64:tab_lo + 96].unsqueeze(1).broadcast_to([128, 8, 32]), op=ALU.mult)
                return e.tensor_tensor(out=dst_s[:, :, 32:64], in0=src[:, :, 0, :],
                                       in1=tab[:, tab_lo + 96:tab_lo + 128].unsqueeze(1).broadcast_to([128, 8, 32]), op=ALU.mult)
            P.op("dve", f, reads=[pname, tabname], writes=[dst_name])

        seq = [("ctx", ctx_d[128:256, :], 33, None), ("ctx", ctx_d[0:128, :], 32, None)]
        seq += [("other", x_d[n * 128:(n + 1) * 128, :], n, None) for n in range(31, 15, -1)]
        seq += [("own", x_d[n * 128:(n + 1) * 128, :], n, n) for n in range(15, -1, -1)]

        fidx = {}

        def s0a(c):
            kind, rows_ap, ti, n = seq[c]
            j3 = c % 3
            P.dma("sp", lambda e: e.dma_start(out=ropet[j3], in_=rope_d[ti, :, :]), f"ropet{j3}", writes=[f"ropet{j3}"])
            keep = (rs_own[:, n:n + 1], f"rso{n}") if kind == "own" else None
            fidx[c] = front_a(rows_ap, 128, keep=keep)

        def s0b(c):
            kind, rows_ap, ti, n = seq[c]
            r = 1 if kind == "ctx" else 0
            j3 = c % 3
            front_b(fidx[c], 128, r, xmTc[j3], f"xmTc{j3}")

        def s1(c):
            kind, rows_ap, ti, n = seq[c]
            own = kind == "own"
            j3, j = c % 3, c % 2
            xm_ = xmTc[j3]
            xmn = f"xmTc{j3}"
            vdst = vbuf[:, n, :] if own else vtmp[j]
            vname = f"v{n}" if own else f"vtmp{j}"

            def f_kproj(e):
                ins = None
                for k in range(8):
                    ins = e.matmul(pf(1), lhsT=xm_[:, k, :], rhs=wk[:, k, :], start=(k == 0), stop=(k == 7))
                return ins
            P.op("pe", f_kproj, reads=[xmn, "wk"], writes=["P1"])
            for hv in range(2):
                def f_vproj(e, hv=hv):
                    ins = None
                    for k in range(8):
                        ins = e.matmul(pf(3 + hv), lhsT=xm_[:, k, :], rhs=wv[:, k, hv * 512:(hv + 1) * 512],
                                       start=(k == 0), stop=(k == 7))
                    return ins
                P.op("pe", f_vproj, reads=[xmn, f"wv{hv}"], writes=[f"P{3 + hv}"])
            if own:
                def f_qproj(e):
                    ins = None
                    for k in range(8):
                        ins = e.matmul(pf(2), lhsT=xm_[:, k, :], rhs=wq[:, k, :], start=(k == 0), stop=(k == 7))
                    return ins
                P.op("pe", f_qproj, reads=[xmn, "wq"], writes=["P2"])
            rope_evac(1, "P1", 0, ropet[j3], f"ropet{j3}", kc, ks, "kcs")
            for hv in range(2):
                P.op("act", lambda e, hv=hv: e.activation(out=vdst[:, hv * 512:(hv + 1) * 512], in_=pf(3 + hv), func=AF.Copy),
                     reads=[f"P{3 + hv}"], writes=[vname + f"_{hv}"])
            P.op("dve", lambda e: e.tensor_tensor(out=k_rb[j], in0=kc, in1=ks, op=ALU.add), reads=["kcs"], writes=[f"k_rb{j}"])

            def f_kfb(e):
                e.tensor_tensor(out=kfb[j][:, :, 0:64], in0=k_rb[j], in1=kdec[:, 0:8].unsqueeze(2).broadcast_to([128, 8, 64]), op=ALU.mult)
                return e.tensor_tensor(out=kfb[j][:, :, 64:128], in0=k_rb[j], in1=kdec[:, 8:16].unsqueeze(2).broadcast_to([128, 8, 64]), op=ALU.mult)
            P.op("pool", f_kfb, reads=[f"k_rb{j}", "dec"], writes=[f"kfb{j}"])
            if own:
                rope_evac(2, "P2", 128, ropet[j3], f"ropet{j3}", qc, qs, "qcs")

                def f_ktp(e):
                    ins = None
                    kr = k_rb[j].rearrange("p h f -> p (h f)")
                    for hp in range(4):
                        ins = e.transpose(out=pb(7)[:, hp * 128:(hp + 1) * 128], in_=kr[:, hp * 128:(hp + 1) * 128], identity=identb)
                    return ins
                P.op("pe", f_ktp, reads=[f"k_rb{j}", "identb"], writes=["P7"])
                P.op("act", lambda e: e.activation(out=kT[:, :, n * 128:(n + 1) * 128],
                                                   in_=pb(7)[:, 0:512].rearrange("p (a b) -> p a b", a=4), func=AF.Copy),
                     reads=["P7"], writes=[f"kT{n}"])
                P.op("dve", lambda e: e.tensor_tensor(out=q_rb[:, n, :].rearrange("p (h f) -> p h f", h=8), in0=qc, in1=qs, op=ALU.add),
                     reads=["qcs"], writes=[f"q_rb{n}"])

        def s2(c):
            kind, rows_ap, ti, n = seq[c]
            own = kind == "own"
            j = c % 2
            ctx_second = c == 1
            vdst = vbuf[:, n, :] if own else vtmp[j]
            vname = f"v{n}" if own else f"vtmp{j}"

            def f_kv(e):
                ins = None
                for h in range(8):
                    ins = e.matmul(pf(5 + h // 4)[:, (h % 4) * 128:(h % 4 + 1) * 128], lhsT=kfb[j][:, h, :],
                                   rhs=vdst[:, h * 128:(h + 1) * 128], start=True, stop=True)
                return ins
            ro, rn = c % 2, (c + 1) % 2
            Ro, Rn = Rst[ro], Rst[rn]
            if own:
                P.op("act", lambda e: e.activation(out=states[64:128, n], in_=Ro[64:128], func=AF.Copy),
                     reads=[f"R{ro}"], writes=[f"stb{n}"])
            P.op("pe", f_kv, reads=[f"kfb{j}", vname + "_0", vname + "_1"], writes=["P5", "P6"])
            if own:
                def f_stf(e):
                    e.activation(out=states[0:64, n, 0:4, :], in_=pf(5)[0:64, :].rearrange("p (a b) -> p a b", a=4), func=AF.Copy)
                    return e.activation(out=states[0:64, n, 4:8, :], in_=pf(6)[0:64, :].rearrange("p (a b) -> p a b", a=4), func=AF.Copy)
                P.op("act", f_stf, reads=["P5", "P6"], writes=[f"stf{n}"])

            def f_state(e):
                ins = None
                for h in range(8):
                    pk_ = pf(5 + h // 4)[:, (h % 4) * 128:(h % 4 + 1) * 128]
                    if ctx_second:
                        e.scalar_tensor_tensor(out=Rn[0:64, h, :], in0=pk_[0:64, :], scalar=dch[0:64, h:h + 1],
                                               in1=Ro[0:64, h, :], op0=ALU.mult, op1=ALU.add)
                        ins = e.scalar_tensor_tensor(out=Rn[64:128, h, :], in0=Ro[64:128, h, :], scalar=dchs0[64:128, h:h + 1],
                                                     in1=pk_[64:128, :], op0=ALU.mult, op1=ALU.add)
                    else:
                        ins = e.scalar_tensor_tensor(out=Rn[:, h, :], in0=Ro[:, h, :], scalar=dchs0[:, h:h + 1],
                                                     in1=pk_, op0=ALU.mult, op1=ALU.add)
                return ins
            P.op("dve", f_state, reads=["P5", "P6", f"R{ro}", "dchs0", "dec"], writes=[f"R{rn}"])
            if ctx_second:
                P.op("act", lambda e: e.activation(out=Fst[0][0:64], in_=Rn[0:64], func=AF.Copy), reads=[f"R{rn}"], writes=["F0"])

        s0a(0)
        s0a(1)
        def ada_mm(i):
            def f_mod(e):
                ins = None
                b = adas[i % 3]
                for c4 in range(4):
                    ct = i * 4 + c4
                    for k in range(8):
                        ins = e.matmul(pf(7)[:, ct * 2:ct * 2 + 2], lhsT=b[:, k, c4 * 128:(c4 + 1) * 128],
                                       rhs=scT[:, k, :], start=(k == 0), stop=(k == 7))
                return ins
            P.op("pe", f_mod, reads=[f"adas{i % 3}", "scT"], writes=["P7"])

        for i in range(4):
            ada_mm(i)
            if i + 3 < 4:
                ada_load(i + 3)

        P.dma("pool", lambda e: e.dma_start(out=wk, in_=win_v[:, :, O_K:O_K + 512]), "wk", writes=["wk"])
        P.dma("pool", lambda e: e.dma_start(out=wv[:, :, 0:512], in_=win_v[:, :, O_V:O_V + 512]), "wv0", writes=["wv0"])
        P.dma("pool", lambda e: e.dma_start(out=wv[:, :, 512:1024], in_=win_v[:, :, O_V + 512:O_V + 1024]), "wv1", writes=["wv1"])

        def modT_part(ts, name):
            def f_modT(e):
                ins = None
                for t in ts:
                    ins = e.tensor_tensor(out=modT[:, t * 8:(t + 1) * 8, :],
                                          in0=pf(7)[:, t * 16:(t + 1) * 16].rearrange("p (a b) -> p a b", a=8),
                                          in1=vT[:, :, 8 + t:9 + t].broadcast_to([128, 8, 2]), op=ALU.add)
                return ins
            P.op("dve", f_modT, reads=["P7", "vT"], writes=[name])
        modT_part((0, 1), "modT")

        P.op("dve", lambda e: e.tensor_scalar(out=Amod, in0=modT[:, 8:16, :], scalar1=1.0, scalar2=None, op0=ALU.add),
             reads=["modT"], writes=["Amod"])
        P.op("dve", lambda e: e.tensor_tensor(out=Amod, in0=Amod, in1=vT[:, :, 2:3].broadcast_to([128, 8, 2]), op=ALU.mult),
             reads=["Amod", "vT"], writes=["Amod"])

        P.barrier(["vrow", "dpn", "dlb", "marg", "marg2", "tmp16", "decarg", "P6", "P7"])
        P.op("dve", lambda e: e.memset(Rst[0], 0.0), writes=["R0"])
        NS = len(seq)
        for t in range(NS + 3):
            if t == 4:
                P.dma("pool", lambda e: e.dma_start(out=wq, in_=win_v[:, :, O_Q:O_Q + 512]), "wq", writes=["wq"])
            if t == 5:
                ada_load(4)
                ada_load(5)
            if t == 9:
                ada_mm(4)
                ada_mm(5)
                modT_part((2,), "modTg")
            if 2 <= t < NS:
                s0a(t)
            if 0 <= t - 2 < NS:
                s1(t - 2)
            if t < NS:
                s0b(t)
            if 0 <= t - 3 < NS:
                s2(t - 3)

        RETB = [(4, 5), (6, 7)]

        def o_sweep(n):
            Fo, Fn = Fst[n % 2], Fst[(n + 1) % 2]

            def f_sweep(e):
                ins = None
                for h in range(8):
                    ins = e.scalar_tensor_tensor(out=Fn[0:64, h, :], in0=Fo[0:64, h, :], scalar=dch[0:64, h:h + 1],
                                                 in1=states[0:64, n, h, :], op0=ALU.mult, op1=ALU.add)
                return ins
            P.op("dve", f_sweep, reads=[f"F{n % 2}", f"stf{n}", "dec"], writes=[f"F{(n + 1) % 2}"])

        def o_stcopy(n):
            Fo = Fst[n % 2]
            P.op("act", lambda e: e.activation(out=states[0:64, n], in_=Fo[0:64], func=AF.Copy),
                 reads=[f"F{n % 2}"], writes=[f"stf{n}"])

        def o_qfb(n):
            j = n % 2

            def f_qfb(e):
                qv = q_rb[:, n, :].rearrange("p (h f) -> p h f", h=8)
                e.tensor_tensor(out=qfb[j][:, :, 0:64], in0=qv, in1=qdec[:, 0:8].unsqueeze(2).broadcast_to([128, 8, 64]), op=ALU.mult)
                return e.tensor_tensor(out=qfb[j][:, :, 64:128], in0=qv, in1=qdec[:, 8:16].unsqueeze(2).broadcast_to([128, 8, 64]), op=ALU.mult)
            P.op("pool", f_qfb, reads=[f"q_rb{n}", "dec"], writes=[f"qfb{j}"])

        def o_qdtp(n):
            j = n % 2

            def f_qdtp(e):
                ins = None
                for h in range(8):
                    ins = e.transpose(out=pb(1)[:, h * 128:(h + 1) * 128], in_=qfb[j][:, h, :], identity=identb)
                return ins
            P.op("pe", f_qdtp, reads=[f"qfb{j}", "identb"], writes=["P1"])

        def o_qdTc(n):
            j = n % 2
            P.op("act", lambda e: e.activation(out=qdTc[j], in_=pb(1).rearrange("p (a b) -> p a b", a=8), func=AF.Copy),
                 reads=["P1"], writes=[f"qdTc{j}"])

        def o_sc(n):
            j = n % 2

            def f_sc(e):
                ins = None
                for h in range(8):
                    par, hp = h % 2, h // 2
                    b0 = 64 * par
                    ins = e.matmul(pf(2 + par)[:, hp * 128:(hp + 1) * 128], lhsT=kT[b0:b0 + 64, hp, n * 128:(n + 1) * 128],
                                   rhs=qdTc[j][b0:b0 + 64, h, :], start=True, stop=True)
                return ins
            P.op("pe", f_sc, reads=[f"kT{n}", f"qdTc{j}"], writes=["P2", "P3"])

        def o_mask(n):
            def f_mask(e):
                ins = None
                for par in range(2):
                    ins = e.tensor_tensor(out=PT[:, par], in0=pf(2 + par).rearrange("p (a b) -> p a b", a=4), in1=MT[:, par], op=ALU.mult)
                return ins
            P.op("dve", f_mask, reads=["P2", "P3", "MT"], writes=["PT"])

        def o_ret(n):
            j = n % 2
            pA, pB = RETB[n % 2]

            def f_ret(e):
                ins = None
                for h in range(8):
                    par, hp = h % 2, h // 2
                    o = pf((pA, pB)[h // 4])[:, (h % 4) * 128:(h % 4 + 1) * 128]
                    e.matmul(o, lhsT=PT[:, par, hp, :], rhs=vbuf[:, n, h * 128:(h + 1) * 128], start=True, stop=False)
                    ins = e.matmul(o, lhsT=qdTc[j][:, h, :], rhs=states[:, n, h, :], start=False, stop=True)
                return ins
            P.op("pe", f_ret, reads=["PT", f"v{n}_0", f"v{n}_1", f"qdTc{j}", f"stf{n}", f"stb{n}"], writes=[f"P{pA}", f"P{pB}"])

        def o_bn(n):
            pA, pB = RETB[n % 2]
            sj = n % 2

            def f_bn(e):
                ins = None
                for h in range(8):
                    o = pf((pA, pB)[h // 4])[:, (h % 4) * 128:(h % 4 + 1) * 128]
                    ins = e.bn_stats(out=bst[sj][:, h, :], in_=o)
                return ins
            P.op("dve", f_bn, reads=[f"P{pA}", f"P{pB}"], writes=[f"bst{sj}"])

            def f_bna(e):
                ins = None
                for h in range(8):
                    ins = e.bn_aggr(out=bmv[sj][:, h, :], in_=bst[sj][:, h, :])
                return ins
            P.op("dve", f_bna, reads=[f"bst{sj}"], writes=[f"bmv{sj}"])

        def o_stats(n):
            sj = n % 2
            P.op("pool", lambda e: e.tensor_scalar(out=rstd8[sj], in0=bmv[sj][:, :, 1], scalar1=EPS, scalar2=None, op0=ALU.add),
                 reads=[f"bmv{sj}"], writes=[f"rstd8{sj}"])
            P.op("pool", lambda e: e.tensor_tensor(out=rstd8[sj], in0=rstd8[sj], in1=neghalf, op=ALU.pow),
                 reads=[f"rstd8{sj}", "neghalf"], writes=[f"rstd8{sj}"])
            P.op("pool", lambda e: e.tensor_tensor(out=nmr8[sj], in0=bmv[sj][:, :, 0], in1=rstd8[sj], op=ALU.mult),
                 reads=[f"bmv{sj}", f"rstd8{sj}"], writes=[f"nmr8{sj}"])
            P.op("pool", lambda e: e.tensor_scalar(out=nmr8[sj], in0=nmr8[sj], scalar1=-1.0, scalar2=None, op0=ALU.mult),
                 reads=[f"nmr8{sj}"], writes=[f"nmr8{sj}"])

        def o_norm(n):
            pA, pB = RETB[n % 2]
            sj = n % 2

            def f_norm(e):
                ins = None
                for h in range(8):
                    o = pf((pA, pB)[h // 4])[:, (h % 4) * 128:(h % 4 + 1) * 128]
                    ins = e.activation(out=retn[:, h * 128:(h + 1) * 128], in_=o, func=AF.Identity,
                                       scale=rstd8[sj][:, h:h + 1], bias=nmr8[sj][:, h:h + 1])
                return ins
            P.op("act", f_norm, reads=[f"P{pA}", f"P{pB}", f"nmr8{sj}", f"rstd8{sj}"], writes=["retn"])

        def o_rtp(n):
            def f_rtp(e):
                ins = None
                for c in range(8):
                    ins = e.transpose(out=pb(0)[:, c * 128:(c + 1) * 128], in_=retn[:, c * 128:(c + 1) * 128], identity=identb)
                return ins
            P.op("pe", f_rtp, reads=["retn", "identb"], writes=["P0"])

        def o_retnT(n):
            P.op("act", lambda e: e.activation(out=retnT[:, :, n * 128:(n + 1) * 128], in_=pb(0).rearrange("p (a b) -> p a b", a=8), func=AF.Copy),
                 reads=["P0"], writes=["wq", "wk", "wv0", "wv1", f"retnT_{n}"])

        ok = lambda n: 0 <= n < NCH
        o_sweep(0)
        o_stcopy(0)
        o_qfb(0)
        o_qdtp(0)
        o_qdTc(0)
        VB07 = [f"v{n}_{h}" for n in range(8) for h in range(2)]
        s0f = {}
        for t in range(NCH + 2):
            a, b_, c_ = t, t - 1, t - 2
            if ok(a + 1):
                o_qfb(a + 1)
            if ok(b_):
                o_bn(b_)
            if ok(c_):
                o_norm(c_)
            if ok(a + 1):
                o_sweep(a + 1)
                o_qdtp(a + 1)
            if ok(b_):
                o_stats(b_)
            if ok(c_):
                o_rtp(c_)
            if ok(a + 1):
                o_qdTc(a + 1)
                o_stcopy(a + 1)
            if ok(c_):
                o_retnT(c_)
            if ok(a):
                o_sc(a)
                o_mask(a)
                o_ret(a)
            if 9 <= t < 17:
                front_b(s0f[t - 9], 128, 0, xmT[:, :, (t - 9) * 128:(t - 8) * 128], f"xmT{t - 9}", alias=VB07)
            if 8 <= t < 16:
                s0f[t - 8] = front_a_cached(t - 8)

        ALL_RETN = [f"retnT_{n}" for n in range(NCH)]
        PH1_R2 = [f"q_rb{n}" for n in range(NCH)] + [f"kT{n}" for n in range(NCH)] + \
                 [f"v{n}_{h}" for n in range(NCH) for h in range(2)] + [f"stf{n}" for n in range(NCH)] + [f"stb{n}" for n in range(NCH)]
        PH1_TM = ["MT", "PT", "retn", "qfb0", "qfb1", "qdTc0", "qdTc1", "bmv0", "bmv1", "bst0", "bst1", "rstd80", "rstd81", "nmr80", "nmr81", "F0", "F1", "R0", "R1", "kcs", "qcs",
                  "k_rb0", "k_rb1", "kfb0", "kfb1", "xmTc0", "xmTc1", "xmTc2", "vtmp0_0", "vtmp0_1", "vtmp1_0", "vtmp1_1"]

        dump("q_rb", q_rb, [128, NCH, 512], BF16, [f"q_rb{n}" for n in range(NCH)])
        dump("kT", kT, [128, 4, HALF], BF16, [f"kT{n}" for n in range(NCH)])
        dump("vbuf", vbuf, [128, NCH, 1024], BF16, [f"v{n}_{h}" for n in range(NCH) for h in range(2)])
        dump("states", states, [128, NCH, 8, 128], BF16, [f"stf{n}" for n in range(NCH)] + [f"stb{n}" for n in range(NCH)])
        dump("MT", MT, [128, 2, 4, 128], F32, ["MT"])
        dump("F0", Fst[0], [128, 8, 128], F32, ["F0"])
        dump("retn", retn, [128, 1024], BF16, ["retn"])
        dump("PT", PT, [128, 2, 4, 128], BF16, ["PT"])
        dump("qdTc", qdTc[1], [128, 8, 128], BF16, ["qdTc1"])
        P.barrier(PH1_R2 + PH1_TM)
        P.dma("sp", lambda e: e.dma_start(out=fnwb, in_=fnw_d[0, :].partition_broadcast(128)), "fnwb", writes=["fnwb"])
        slot_use = {}

        def slab_load(g, src_v, col0, width=256):
            u = slot_use.get(g, 0)
            slot_use[g] = u + 1
            b = fsl[g][u % 2]
            nm = f"fsl{g}_{u % 2}"
            P.dma("pool", lambda e: e.dma_start(out=b[:, :, 0:width], in_=src_v[:, :, col0:col0 + width]), nm,
                  writes=[nm])
            return b, nm

        def emit_gate_tile():
            dg = c1.rearrange("p (k c) -> p k c", k=8)
            P.op("dve", lambda e: e.tensor_tensor(out=dg, in0=identf.unsqueeze(1).broadcast_to([128, 8, 128]),
                                                  in1=modT[:, 16:24, 0:1].broadcast_to([128, 8, 128]), op=ALU.mult),
                 reads=["identf", "modTg"], writes=["c1"])

            def f_g(e):
                ins = None
                for k in range(8):
                    ins = e.matmul(pf(k // 4)[:, (k % 4) * 128:(k % 4 + 1) * 128], lhsT=onesf, rhs=dg[:, k, :], start=True, stop=True)
                return ins
            P.op("pe", f_g, reads=["c1", "onesf"], writes=["P0", "P1"])
            for hv in range(2):
                P.op("act", lambda e, hv=hv: e.activation(out=gxb[:, hv * 512:(hv + 1) * 512], in_=pf(hv), func=AF.Copy),
                     reads=[f"P{hv}"], writes=[f"gxb{hv}"])

        step_specs = []
        for sb_ in range(2):
            for ctp_ in range(4):
                step_specs.append([(g_, win_v, off_ + ctp_ * 256) for g_, off_ in enumerate((O_H, O_CG, O_BG, O_ZA))])
            for ctp_ in range(4):
                step_specs.append([(0, wa_v, ctp_ * 256), (1, win_v, O_GA + ctp_ * 256)])
            for ctp_ in range(4):
                step_specs.append([(2, win_v, O_ZB + ctp_ * 256)])
            for ctp_ in range(4):
                step_specs.append([(3, wb_v, ctp_ * 256), (0, win_v, O_GB + ctp_ * 256)])
            step_specs.append("wo")
        step_res = {}
        step_ptr = [0]

        def issue_step(si):
            if si >= len(step_specs) or si in step_res:
                return
            spec = step_specs[si]
            if spec == "wo":
                for hv in range(2):
                    P.dma("pool", lambda e, hv=hv: e.dma_start(out=wo[hv], in_=wo_v[:, :, hv * 512:(hv + 1) * 512]), f"wo{hv}",
                          writes=[f"wo{hv}"])
                step_res[si] = None
            else:
                step_res[si] = [slab_load(g_, v_, c_) for (g_, v_, c_) in spec]

        def take_step():
            si = step_ptr[0]
            step_ptr[0] += 1
            issue_step(si)
            issue_step(si + 1)
            if si + 2 < len(step_specs) and step_specs[si + 2] == "wo":
                issue_step(si + 2)
            return step_res[si]

        issue_step(0)
        pair_i = [0]

        def next_pair():
            p = 2 + 2 * (pair_i[0] % 3)
            pair_i[0] += 1
            return p, p + 1

        def mm8(e, pbank, slab, c0_, rhs_fn, n_=512):
            ins = None
            for k in range(8):
                ins = e.matmul(pf(pbank)[:, 0:n_], lhsT=slab[:, k, c0_:c0_ + 128], rhs=rhs_fn(k), start=(k == 0), stop=(k == 7))
            return ins

        def stage0_chunk(sb_, jx):
            T0_ = sb_ * 1024
            front(x_d[T0_ + jx * 128:T0_ + (jx + 1) * 128, :], 128, 0, xmT[:, :, jx * 128:(jx + 1) * 128], f"xmT{jx}")

        def stage0_halo(sb_):
            T0_ = sb_ * 1024
            lrow = max(T0_ - 1, 0)
            i = gcc[0] % 2
            gcc[0] += 1
            P.dma("sp", lambda e: e.dma_start(out=xc[i][0:1, :], in_=x_d[lrow:lrow + 1, :]), "haloL", writes=[f"xc{i}"], n=1)
            P.dma("sp", lambda e: e.dma_start(out=xc[i][1:2, :], in_=x_d[T0_ + 1024:T0_ + 1025, :]), "haloR", writes=[f"xc{i}h"], n=1)
            P.op("act", lambda e: e.activation(out=xs[i][0:2, :], in_=xc[i][0:2, :], func=AF.Square, accum_out=ssb[i][0:2, :]),
                 reads=[f"xc{i}", f"xc{i}h"], writes=[f"xs{i}", f"ss{i}"])
            P.op("pool", lambda e: e.tensor_scalar(out=msb[i][0:2, :], in0=ssb[i][0:2, :], scalar1=1.0 / D, scalar2=EPS, op0=ALU.mult, op1=ALU.add),
                 reads=[f"ss{i}"], writes=[f"ms{i}"])
            P.op("pool", lambda e: e.tensor_tensor(out=rsb[i][0:2, :], in0=msb[i][0:2, :], in1=neghalf[0:2, 0:1], op=ALU.pow),
                 reads=[f"ms{i}", "neghalf"], writes=[f"rs{i}"])
            P.op("act", lambda e: e.activation(out=xs[i][0:2, :], in_=xc[i][0:2, :], func=AF.Copy, scale=rsb[i][0:2, :]),
                 reads=[f"xc{i}", f"xc{i}h", f"rs{i}"], writes=[f"xs{i}"])

            def f_tph(e):
                ins = None
                for k in range(8):
                    ins = e.transpose(out=pb(0)[:, k * 2:k * 2 + 2], in_=xs[i][0:2, k * 128:(k + 1) * 128], identity=identb[0:2, 0:2])
                return ins
            P.op("pe", f_tph, reads=[f"xs{i}", "identb"], writes=["P0"])

            def f_affh(e):
                ins = None
                for k in range(8):
                    ins = e.tensor_scalar(out=xmTh[:, k, :], in0=pb(0)[:, k * 2:k * 2 + 2], scalar1=Amod[:, k, 0:1], scalar2=modT[:, k, 0:1],
                                          op0=ALU.mult, op1=ALU.add)
                return ins
            P.op("dve", f_affh, reads=["P0", "Amod", "modT"], writes=["xmTh"])

        for sb in range(2):
            T0 = sb * 1024
            if sb == 0:
                stage0_halo(0)
            XM = [f"xmT{jx}" for jx in range(8)]

            for ctp in range(4):
                sl = dict(enumerate(take_step()))
                for c2 in range(2):
                    ct = ctp * 2 + c2
                    cb = cgh[ct % 2]
                    cbn = f"cgh{ct % 2}"
                    tb1 = t1[ct % 2]
                    def f_halo(e, c2=c2, sl=sl):
                        ins = None
                        for gi, g in enumerate((0, 1)):
                            for k in range(8):
                                ins = e.matmul(pf(1)[:, gi * 2:gi * 2 + 2], lhsT=sl[g][0][:, k, c2 * 128:(c2 + 1) * 128], rhs=xmTh[:, k, :],
                                               start=(k == 0), stop=(k == 7))
                        return ins
                    P.op("pe", f_halo, reads=[sl[0][1], sl[1][1], "xmTh"], writes=["P1"])
                    P.op("act", lambda e: e.activation(out=hh[:, 0:2], in_=pf(1)[:, 0:2], func=AF.Copy), reads=["P1"], writes=["hh"])

                    def f_halo2(e, cb=cb, sb=sb):
                        if sb == 0:
                            e.memset(cb[:, 0:1], 0.0)
                        else:
                            e.tensor_tensor(out=cb[:, 0:1], in0=pf(1)[:, 2:3], in1=hh[:, 0:1], op=ALU.mult)
                        return e.tensor_tensor(out=cb[:, 1025:1026], in0=pf(1)[:, 3:4], in1=hh[:, 1:2], op=ALU.mult)
                    P.op("dve", f_halo2, reads=["P1", "hh"], writes=[cbn + "h"])
                    for tb in range(2):
                        pa, pb_ = next_pair()
                        hs = h_sb[tb]

                        def f_hcg(e, c2=c2, tb=tb, pa=pa, pb_=pb_, sl=sl):
                            mm8(e, pa, sl[0][0], c2 * 128, lambda k: xmT[:, k, tb * 512:(tb + 1) * 512])
                            return mm8(e, pb_, sl[1][0], c2 * 128, lambda k: xmT[:, k, tb * 512:(tb + 1) * 512])
                        P.op("pe", f_hcg, reads=[sl[0][1], sl[1][1]] + XM[tb * 4:(tb + 1) * 4], writes=[f"P{pa}", f"P{pb_}"])
                        P.op("act", lambda e, pa=pa, hs=hs: e.activation(out=hs, in_=pf(pa), func=AF.Copy), reads=[f"P{pa}"], writes=[f"h_sb{tb}"])
                        P.op("dve", lambda e, pb_=pb_, hs=hs, cb=cb, tb=tb: e.tensor_tensor(out=cb[:, 1 + tb * 512:1 + (tb + 1) * 512], in0=pf(pb_), in1=hs, op=ALU.mult),
                             reads=[f"P{pb_}", f"h_sb{tb}"], writes=[cbn + f"_{tb}"])
                        pa2, pb2 = next_pair()
                        sz = sza[tb]

                        def f_bgza(e, c2=c2, tb=tb, pa2=pa2, pb2=pb2, sl=sl):
                            mm8(e, pa2, sl[2][0], c2 * 128, lambda k: xmT[:, k, tb * 512:(tb + 1) * 512])
                            return mm8(e, pb2, sl[3][0], c2 * 128, lambda k: xmT[:, k, tb * 512:(tb + 1) * 512])
                        P.op("pe", f_bgza, reads=[sl[2][1], sl[3][1]] + XM[tb * 4:(tb + 1) * 4], writes=[f"P{pa2}", f"P{pb2}"])
                        P.op("act", lambda e, pb2=pb2, sz=sz: e.activation(out=sz, in_=pf(pb2), func=AF.Silu), reads=[f"P{pb2}"], writes=[f"sza{tb}"])
                        P.op("dve", lambda e, pa2=pa2, sz=sz, tb1=tb1, tb=tb: e.tensor_tensor(out=tb1[:, tb * 512:(tb + 1) * 512], in0=pf(pa2), in1=sz, op=ALU.mult),
                             reads=[f"P{pa2}", f"sza{tb}"], writes=[f"t1_{ct % 2}_{tb}"])
                    CG = [cbn + "h", cbn + "_0", cbn + "_1"]
                    P.op("act", lambda e, cb=cb, ct=ct: e.activation(out=c0, in_=cb[:, 1:1025], func=AF.Identity, scale=vT[:, ct, 4:5], bias=vT[:, ct, 6:7]),
                         reads=CG + ["vT"], writes=["c0"])
                    P.op("dve", lambda e, cb=cb, ct=ct: e.scalar_tensor_tensor(out=c1, in0=cb[:, 0:1024], scalar=vT[:, ct, 3:4], in1=c0, op0=ALU.mult, op1=ALU.add),
                         reads=CG + ["c0", "vT"], writes=["c1"])
                    P.op("dve", lambda e, cb=cb, ct=ct: e.scalar_tensor_tensor(out=c0, in0=cb[:, 2:1026], scalar=vT[:, ct, 5:6], in1=c1, op0=ALU.mult, op1=ALU.add),
                         reads=CG + ["c1", "vT"], writes=["c0"])
                    P.op("pool", lambda e, ct=ct, tb1=tb1: e.tensor_tensor(out=bufA[:, ct, :], in0=c0, in1=tb1, op=ALU.mult),
                         reads=["c0", f"t1_{ct % 2}_0", f"t1_{ct % 2}_1"], writes=[f"bufA{ct}"])
            BA = [f"bufA{c}" for c in range(8)]

            for ctp in range(4):
                s_wa, s_ga = take_step()
                if sb == 0 and ctp == 1:
                    emit_gate_tile()
                for c2 in range(2):
                    ct = ctp * 2 + c2
                    for tb in range(2):
                        pa, pb_ = next_pair()

                        def f_b(e, c2=c2, tb=tb, pa=pa, pb_=pb_, s_wa=s_wa, s_ga=s_ga):
                            mm8(e, pa, s_wa[0], c2 * 128, lambda k: bufA[:, k, tb * 512:(tb + 1) * 512])
                            return mm8(e, pb_, s_ga[0], c2 * 128, lambda k: xmT[:, k, tb * 512:(tb + 1) * 512])
                        P.op("pe", f_b, reads=[s_wa[1], s_ga[1]] + BA + XM[tb * 4:(tb + 1) * 4], writes=[f"P{pa}", f"P{pb_}"])
                        sg = sig[tb]
                        P.op("act", lambda e, pb_=pb_, sg=sg: e.activation(out=sg, in_=pf(pb_), func=AF.Sigmoid), reads=[f"P{pb_}"], writes=[f"sig{tb}"])
                        P.op("dve", lambda e, pa=pa, sg=sg, ct=ct, tb=tb: e.tensor_tensor(out=mbuf[:, ct, tb * 512:(tb + 1) * 512], in0=pf(pa), in1=sg, op=ALU.mult),
                             reads=[f"P{pa}", f"sig{tb}"], writes=[f"m{ct}_{tb}"])

            for ctp in range(4):
                (s_zb,) = take_step()
                for c2 in range(2):
                    ct = ctp * 2 + c2
                    for tb in range(2):
                        pa, pb_ = next_pair()
                        P.op("pe", lambda e, c2=c2, tb=tb, pa=pa, s_zb=s_zb: mm8(e, pa, s_zb[0], c2 * 128, lambda k: xmT[:, k, tb * 512:(tb + 1) * 512]),
                             reads=[s_zb[1]] + XM[tb * 4:(tb + 1) * 4], writes=[f"P{pa}", f"P{pb_}"])
                        sz = sza[tb]
                        P.op("act", lambda e, pa=pa, sz=sz: e.activation(out=sz, in_=pf(pa), func=AF.Silu), reads=[f"P{pa}"], writes=[f"sza{tb}"])
                        P.op("dve", lambda e, sz=sz, ct=ct, tb=tb, T0=T0: e.scalar_tensor_tensor(
                            out=bufA[:, ct, tb * 512:(tb + 1) * 512], in0=retnT[:, ct, T0 + tb * 512:T0 + (tb + 1) * 512],
                            scalar=vT[:, ct, 7:8], in1=sz, op0=ALU.mult, op1=ALU.mult),
                             reads=[f"sza{tb}", "vT"] + ALL_RETN, writes=[f"bufA{ct}"])

            for ctp in range(4):
                s_wb, s_gb = take_step()
                if ctp == 3:
                    for hv in range(2):
                        P.op("dve", lambda e, hv=hv: e.tensor_tensor(out=wo[hv], in0=wo[hv],
                                                                     in1=gxb[:, hv * 512:(hv + 1) * 512].unsqueeze(1).broadcast_to([128, 8, 512]),
                                                                     op=ALU.mult),
                             reads=[f"wo{hv}", f"gxb{hv}"], writes=[f"wo{hv}"])
                for c2 in range(2):
                    ct = ctp * 2 + c2
                    for tb in range(2):
                        pa, pb_ = next_pair()

                        def f_d(e, c2=c2, tb=tb, pa=pa, pb_=pb_, s_wb=s_wb, s_gb=s_gb):
                            mm8(e, pa, s_wb[0], c2 * 128, lambda k: bufA[:, k, tb * 512:(tb + 1) * 512])
                            return mm8(e, pb_, s_gb[0], c2 * 128, lambda k: xmT[:, k, tb * 512:(tb + 1) * 512])
                        P.op("pe", f_d, reads=[s_wb[1], s_gb[1]] + BA + XM[tb * 4:(tb + 1) * 4], writes=[f"P{pa}", f"P{pb_}"])
                        sg = sig[tb]
                        td = tmpd[tb]
                        P.op("act", lambda e, pb_=pb_, sg=sg: e.activation(out=sg, in_=pf(pb_), func=AF.Sigmoid), reads=[f"P{pb_}"], writes=[f"sig{tb}"])
                        P.op("dve", lambda e, pa=pa, sg=sg, td=td: e.tensor_tensor(out=td, in0=pf(pa), in1=sg, op=ALU.mult),
                             reads=[f"P{pa}", f"sig{tb}"], writes=[f"tmpd{tb}"])
                        P.op("pool", lambda e, td=td, ct=ct, tb=tb: e.tensor_tensor(out=mbuf[:, ct, tb * 512:(tb + 1) * 512],
                                                                                    in0=mbuf[:, ct, tb * 512:(tb + 1) * 512], in1=td, op=ALU.add),
                             reads=[f"tmpd{tb}", f"m{ct}_{tb}"], writes=[f"m{ct}_{tb}"])
            MM = [f"m{c}_{t}" for c in range(8) for t in range(2)]

            take_step()

            ring = [xn[0], xn[1], c1, h_sbF]
            ringn = [["xn0"], ["xn1"], ["c1"], ["h_sb0", "h_sb1"]]

            def e_load(jx):
                r = jx % 4
                r0 = T0 + jx * 128
                P.dma("sp", lambda e: e.dma_start(out=ring[r], in_=x_d[r0:r0 + 128, :]), f"xnr{r}", writes=ringn[r])

            def e_mm(jx):
                r = jx % 4
                xnb = ring[r]
                pa, pb_ = next_pair()

                def f_e(e):
                    ins = None
                    for hv, pbk in enumerate((pa, pb_)):
                        for k in range(8):
                            ins = e.matmul(pf(pbk), lhsT=mbuf[:, k, jx * 128:(jx + 1) * 128], rhs=wo[hv][:, k, :], start=(k == 0), stop=(k == 7))
                    return ins
                P.op("pe", f_e, reads=MM + ["wo0", "wo1"], writes=[f"P{pa}", f"P{pb_}"])
                def f_res(e):
                    e.tensor_tensor(out=xnb[:, 0:512], in0=pf(pa), in1=xnb[:, 0:512], op=ALU.add)
                    return e.tensor_tensor(out=xnb[:, 512:1024], in0=pf(pb_), in1=xnb[:, 512:1024], op=ALU.add)
                P.op("dve", f_res, reads=[f"P{pa}", f"P{pb_}"] + ringn[r], writes=ringn[r])

            def e_stats(jx):
                r = jx % 4
                xnb = ring[r]
                P.op("act", lambda e: e.activation(out=c0, in_=xnb, func=AF.Square, accum_out=sse[r]),
                     reads=ringn[r], writes=["c0", f"sse{r}"])
                P.op("pool", lambda e: e.tensor_scalar(out=mse[r], in0=sse[r], scalar1=1.0 / D, scalar2=EPS, op0=ALU.mult, op1=ALU.add),
                     reads=[f"sse{r}"], writes=[f"mse{r}"])
                P.op("pool", lambda e: e.tensor_tensor(out=rse[r], in0=mse[r], in1=neghalf[:, 0:1], op=ALU.pow),
                     reads=[f"mse{r}", "neghalf"], writes=[f"rse{r}"])

            def e_part2(jx):
                r = jx % 4
                r0 = T0 + jx * 128
                xnb = ring[r]
                P.op("dve", lambda e: e.scalar_tensor_tensor(out=xnb, in0=xnb, scalar=rse[r], in1=fnwb, op0=ALU.mult, op1=ALU.mult),
                     reads=ringn[r] + [f"rse{r}", "fnwb"], writes=ringn[r])
                tok = P.dma("sp", lambda e: e.dma_start(out=out_d[r0:r0 + 128, :], in_=xnb), f"outr{r}", reads=ringn[r])
                P.out_tokens.append(tok)

            fi = {}
            e_load(0)
            if sb == 0:
                fi[0] = front_a_cached(8)
                front_b(fi[0], 128, 0, xmT[:, :, 0:128], "xmT0")
            for jx in range(8):
                if jx + 1 < 8:
                    e_load(jx + 1)
                    if sb == 0:
                        fi[jx + 1] = front_a_cached(8 + jx + 1)
                if jx >= 2:
                    e_part2(jx - 2)
                e_mm(jx)
                if sb == 0 and jx + 1 < 8:
                    front_b(fi[jx + 1], 128, 0, xmT[:, :, (jx + 1) * 128:(jx + 2) * 128], f"xmT{jx + 1}")
                e_stats(jx)
            e_part2(6)
            e_part2(7)
            if sb == 0:
                stage0_halo(1)

        dump("retnT", retnT, [128, 8, HALF], BF16, ALL_RETN)
        dump("xmT", xmT, [128, 8, 1024], BF16, XM)
        dump("bufA", bufA, [128, 8, 1024], BF16, BA)
        dump("mbuf", mbuf, [128, 8, 1024], BF16, MM)
        dump("vT", vT, [128, 8, 16], F32, ["vT"])
        dump("modT", modT, [128, 24, 2], F32, ["modT"])
        dump("Amod", Amod, [128, 8, 2], F32, ["Amod"])
        dump("kdec", kdec, [128, 16], F32, ["dec"])
        dump("qdec", qdec, [128, 16], F32, ["dec"])
        dump("dch", dch, [128, 16], F32, ["dec"])
        dump("gxb", gxb, [128, D], F32, ["gxb0", "gxb1"])
        fin = {}
        for skey, val, _ in P.out_tokens:
            fin[skey] = max(fin.get(skey, 0), val)
        P.streams["sp"].append((list(fin.items()), None, None))

        sems = {}
        keys = set()
        for st in P.streams.values():
            for waits, fn, inc in st:
                for skey, _ in waits:
                    keys.add(skey)
                if inc is not None:
                    keys.add(inc[0])
        for idx, k in enumerate(sorted(keys, key=str)):
            sems[k] = es.enter_context(nc.semaphore(f"sem{idx}"))

        def replay(name, eng):
            for waits, fn, inc in P.streams[name]:
                for skey, val in waits:
                    eng.wait_ge(sems[skey], val)
                if fn is None:
                    continue
                ins = fn(eng)
                ins.then_inc(sems[inc[0]], inc[1])

        block = es.enter_context(nc.Block())

        @block.sync
        def _(e):
            replay("sp", e)

        @block.tensor
        def _(e):
            replay("pe", e)

        @block.scalar
        def _(e):
            replay("act", e)

        @block.vector
        def _(e):
            replay("dve", e)

        @block.gpsimd
        def _(e):
            replay("pool", e)

    return nc


def _host_consts(flip):
    ident = np.eye(128, dtype=np.float32)
    jj = np.arange(128, dtype=np.float32)[:, None]
    ii = np.arange(128, dtype=np.float32)[None, :]
    dmat = ii - jj
    dpn = np.concatenate([np.maximum(dmat, 0), np.maximum(-dmat, 0), np.broadcast_to(ii + 1, (128, 128)),
                          np.broadcast_to(128 - ii, (128, 128))], axis=1).astype(np.float32)
    p = np.arange(128, dtype=np.float32)
    pidx = np.stack([p, 127 - p, p + 1, 128 - p], axis=1).astype(np.float32)
    t = np.arange(SEQ)
    pos = (SEQ - 1 - t) if flip else t
    row = (pos // 64).astype(np.float32)
    col = (pos % 64).astype(np.float32)
    nf = 16
    inv = (10000.0 ** (-np.arange(nf, dtype=np.float32) / nf)).astype(np.float32)
    ang = np.concatenate([row[:, None] * inv, col[:, None] * inv], axis=-1).astype(np.float32)
    cos = np.cos(ang).astype(np.float32)
    sin = np.sin(ang).astype(np.float32)
    qt = np.concatenate([cos, cos, -sin, sin], axis=1)
    kt = (qt * np.float32(0.125)).astype(np.float32)
    rope = np.zeros((34, 128, 256), np.float32)
    rope[:32, :, 0:128] = kt.reshape(32, 128, 128)
    rope[:32, :, 128:256] = qt.reshape(32, 128, 128)
    rope[32:, :, 0:64] = 0.125
    return ident, dpn, pidx, rope


_NC_CACHE = {}


def _in_maps(x, c, ctx, c_ctx, norm_w, ada_w, ada_b, w_in, conv_w, conv_b, decay_logit, gn_w, w_a, w_b, w_out, final_norm_w):
    f = lambda a: np.ascontiguousarray(np.asarray(a, dtype=np.float32))
    x, c, ctx, c_ctx = f(x), f(c), f(ctx), f(c_ctx)
    norm_w, ada_w, ada_b, w_in = f(norm_w)[0], f(ada_w)[0], f(ada_b)[0], f(w_in)[0]
    conv_w, conv_b, decay_logit, gn_w = f(conv_w)[0], f(conv_b)[0], f(decay_logit)[0], f(gn_w)[0]
    w_a, w_b, w_out, fnw = f(w_a)[0], f(w_b)[0], f(w_out)[0], f(final_norm_w)
    consts = {fl: _host_consts(fl) for fl in (False, True)}
    in_maps = []
    for core in range(8):
        b, half = core // 2, core % 2
        flip = half == 1
        ident, dpn, pidx, rope = consts[flip]
        xb = x[b, ::-1] if flip else x[b]
        cb = ctx[b, ::-1] if flip else ctx[b]
        cw = conv_w[::-1] if flip else conv_w
        dl = decay_logit[::-1] if flip else decay_logit
        vecs = np.stack([c[b], c_ctx, norm_w, cw[0], cw[1], cw[2], conv_b, gn_w,
                         ada_b[0:D], ada_b[D:2 * D], ada_b[2 * D:3 * D]], axis=0)
        in_maps.append({
            "x": np.ascontiguousarray(xb), "ctx": np.ascontiguousarray(cb), "vecs": np.ascontiguousarray(vecs),
            "ada_w": ada_w, "w_in": w_in, "w_a": w_a, "w_b": w_b, "w_out": w_out,
            "dl": np.ascontiguousarray(dl.reshape(1, 16)), "fnw": fnw.reshape(1, D),
            "ident": ident, "dpn": dpn, "pidx": pidx, "rope": rope,
        })
    return in_maps


def _gather(results):
    out = np.empty((4, SEQ, D), np.float32)
    for core in range(8):
        b, half = core // 2, core % 2
        o = np.asarray(results[core]["out"], dtype=np.float32)
        if half == 0:
            out[b, 0:HALF] = o
        else:
            out[b, HALF:SEQ] = o[::-1]
    return out


def kernel(x, c, ctx, c_ctx, norm_w, ada_w, ada_b, w_in, conv_w, conv_b, decay_logit, gn_w, w_a, w_b, w_out, final_norm_w):
    in_maps = _in_maps(x, c, ctx, c_ctx, norm_w, ada_w, ada_b, w_in, conv_w, conv_b, decay_logit, gn_w, w_a, w_b, w_out, final_norm_w)
    if "nc" not in _NC_CACHE:
        _NC_CACHE["nc"] = build_program()
    res = run_bass_kernel_spmd(_NC_CACHE["nc"], in_maps, core_ids=list(range(8)))
    return _gather(res.results)
```

```python
import contextlib
import numpy as np
import concourse.bass as bass
import concourse.mybir as mybir
from concourse.bass_utils import run_bass_kernel_spmd

F32 = mybir.dt.float32
BF16 = mybir.dt.bfloat16
AF = mybir.ActivationFunctionType
ALU = mybir.AluOpType

D = 1024
SEQ = 4096
HALF = 2048
NCH = 16
EPS = 1e-6
H = 8
O_H, O_BG, O_CG, O_ZA, O_Q, O_K, O_V, O_ZB, O_GA, O_GB = 0, 1024, 2048, 3072, 4096, 4608, 5120, 6144, 7168, 8192


class Prog:
    def __init__(self):
        self.streams = {e: [] for e in ("pe", "act", "dve", "pool", "sp")}
        self.cnt = {e: 0 for e in self.streams}
        self.lastw = {}
        self.readers = {}
        self.waited = {e: {} for e in self.streams}
        self.dcnt = {}
        self.out_tokens = []

    def _waits(self, eng, reads, writes):
        toks = []
        for r in reads:
            t = self.lastw.get(r)
            if t is not None:
                toks.append(("raw", t))
            if len(r) == 2 and r[0] == "P" and r[1].isdigit():
                for t in self.readers.get(r, ()):
                    toks.append(("rar", t))
        for w in writes:
            t = self.lastw.get(w)
            if t is not None:
                toks.append(("waw", t))
            for t in self.readers.get(w, ()):
                toks.append(("war", t))
        need = {}
        for kind, (skey, val, teng) in toks:
            if teng == eng and (eng == "pe" or kind == "rar"):
                continue
            if val > need.get(skey, 0):
                need[skey] = val
        waits = []
        for skey, val in need.items():
            if self.waited[eng].get(skey, 0) >= val:
                continue
            self.waited[eng][skey] = val
            waits.append((skey, val))
        return waits

    def _record(self, tok, reads, writes):
        for r in reads:
            self.readers.setdefault(r, []).append(tok)
        for w in writes:
            self.lastw[w] = tok
            self.readers[w] = []

    def op(self, eng, fn, reads=(), writes=()):
        waits = self._waits(eng, reads, writes)
        self.cnt[eng] += 1
        tok = (eng, self.cnt[eng], eng)
        self.streams[eng].append((waits, fn, (eng, 1)))
        self._record(tok, reads, writes)
        return tok

    def barrier(self, resources):
        for eng in self.streams:
            waits = self._waits(eng, (), resources)
            if waits:
                self.streams[eng].append((waits, None, None))

    def dma(self, q, fn, slot, reads=(), writes=(), n=1):
        waits = self._waits(q, reads, writes)
        self.dcnt[slot] = self.dcnt.get(slot, 0) + n
        skey = ("dma", slot)
        tok = (skey, 16 * self.dcnt[slot], "dma")
        self.streams[q].append((waits, fn, (skey, 16)))
        self._record(tok, reads, writes)
        return tok


def build_program(debug=False):
    nc = bass.Bass("TRN2", target_bir_lowering=False)

    def din(name, shape):
        return nc.dram_tensor(name, list(shape), F32, kind="ExternalInput").ap()

    x_d = din("x", (SEQ, D))
    ctx_d = din("ctx", (256, D))
    vecs_d = din("vecs", (11, D))
    ada_d = din("ada_w", (D, 3 * D))
    win_d = din("w_in", (D, 9216))
    wa_d = din("w_a", (D, D))
    wb_d = din("w_b", (D, D))
    wo_d = din("w_out", (D, D))
    dl_d = din("dl", (1, 16))
    fnw_d = din("fnw", (1, D))
    ident_d = din("ident", (128, 128))
    dpn_d = din("dpn", (128, 512))
    pidx_d = din("pidx", (128, 4))
    rope_d = din("rope", (34, 128, 256))
    out_d = nc.dram_tensor("out", [HALF, D], F32, kind="ExternalOutput").ap()

    P = Prog()
    es = contextlib.ExitStack()

    def dump(name, ap, shape, dt, reads):
        if not debug:
            return
        t = nc.dram_tensor("dbg_" + name, list(shape), dt, kind="ExternalOutput").ap()
        tok = P.dma("sp", lambda e: e.dma_start(out=t, in_=ap), "dbg_" + name, reads=reads)
        P.out_tokens.append(tok)
    with es:
        def arena(name, nbytes):
            return es.enter_context(nc.sbuf_tensor(name, [128, nbytes // 2], BF16))

        class Bump:
            def __init__(self, name, nbytes):
                self.t = arena(name, nbytes)
                self.n = nbytes
                self.off = 0

            def reset(self, off=0):
                self.off = off

            def get(self, shape, dt):
                esz = 2 if dt == BF16 else 4
                n = int(np.prod(shape[1:]))
                nb = (n * esz + 63) // 64 * 64
                assert self.off + nb <= self.n, (self.off, nb, self.n, shape)
                ap = self.t[:, self.off // 2:(self.off + n * esz) // 2]
                self.off += nb
                if dt != BF16:
                    ap = ap.bitcast(dt)
                if len(shape) == 3:
                    ap = ap.rearrange("p (a b) -> p a b", a=shape[1])
                elif len(shape) == 4:
                    ap = ap.rearrange("p (a b c) -> p a b c", a=shape[1], b=shape[2])
                return ap

        KB = 1024
        CM = Bump("cm", 20 * KB)
        R1 = Bump("r1", 32 * KB)
        R2 = Bump("r2", 96 * KB)
        TM = Bump("tm", 59 * KB)

        PS = [es.enter_context(nc.psum_tensor(f"P{i}", [128, 512], F32)) for i in range(8)]

        def pf(i):
            return PS[i][:, :]

        def pb(i):
            return PS[i][:, :].bitcast(BF16)

        xc = [CM.get([128, D], F32) for _ in range(2)]
        xs = [CM.get([128, D], BF16) for _ in range(2)]
        ropet = [CM.get([128, 256], F32) for _ in range(3)]
        vT = CM.get([128, 8, 16], F32)
        scT = CM.get([128, 8, 2], BF16)
        modT = CM.get([128, 24, 2], F32)
        Amod = CM.get([128, 8, 2], F32)
        identf = CM.get([128, 128], F32)
        identb = CM.get([128, 128], BF16)
        pidx = CM.get([128, 4], F32)
        nlg = CM.get([128, 16], F32)
        kdec = CM.get([128, 16], F32)
        qdec = CM.get([128, 16], F32)
        dch = CM.get([128, 16], F32)
        dchs0 = CM.get([128, 8], F32)
        neghalf = CM.get([128, 8], F32)
        ssb = [CM.get([128, 1], F32) for _ in range(2)]
        msb = [CM.get([128, 1], F32) for _ in range(2)]
        rsb = [CM.get([128, 1], F32) for _ in range(2)]
        xmTh = CM.get([128, 8, 2], BF16)
        rs_own = CM.get([128, 16], F32)
        sse = [CM.get([128, 1], F32) for _ in range(4)]
        mse = [CM.get([128, 1], F32) for _ in range(4)]
        rse = [CM.get([128, 1], F32) for _ in range(4)]
        diag = CM.get([128, 128], F32)
        onesf = CM.get([128, 128], F32)

        wq = R1.get([128, 8, 512], BF16)
        wk = R1.get([128, 8, 512], BF16)
        wv = R1.get([128, 8, 1024], BF16)
        R1.reset()
        retnT = R1.get([128, 8, HALF], BF16)

        q_rb = R2.get([128, NCH, 512], BF16)
        kT = R2.get([128, 4, HALF], BF16)
        vbuf = R2.get([128, NCH, 1024], BF16)
        states = R2.get([128, NCH, 8, 128], BF16)
        R2.reset(64 * KB)
        adas = [R2.get([128, 8, 512], BF16) for _ in range(3)]
        R2.reset()
        bufA = R2.get([128, 8, 1024], BF16)
        mbuf = R2.get([128, 8, 1024], BF16)
        xmT = R2.get([128, 8, 1024], BF16)
        fsl = [[R2.get([128, 8, 256], BF16) for _ in range(2)] for _ in range(4)]
        wo = [R2.get([128, 8, 512], BF16) for _ in range(2)]

        vrow = TM.get([128, D], F32)
        dpn = TM.get([128, 512], F32)
        dlb = TM.get([128, 16], F32)
        marg = TM.get([128, 8, 128], F32)
        tmp16 = TM.get([128, 16], F32)
        TM.reset()
        MT = TM.get([128, 2, 4, 128], F32)
        xmTc = [TM.get([128, 8, 128], BF16) for _ in range(3)]
        vtmp = [TM.get([128, 1024], BF16) for _ in range(2)]
        kc = TM.get([128, 8, 64], BF16)
        ks = TM.get([128, 8, 64], BF16)
        qc = TM.get([128, 8, 64], BF16)
        qs = TM.get([128, 8, 64], BF16)
        k_rb = [TM.get([128, 8, 64], BF16) for _ in range(2)]
        kfb = [TM.get([128, 8, 128], BF16) for _ in range(2)]
        Rst = [TM.get([128, 8, 128], F32) for _ in range(2)]
        Fst = [TM.get([128, 8, 128], F32) for _ in range(2)]
        qfb = [TM.get([128, 8, 128], BF16) for _ in range(2)]
        qdTc = [TM.get([128, 8, 128], BF16) for _ in range(2)]
        PT = TM.get([128, 2, 4, 128], BF16)
        retn = TM.get([128, 1024], BF16)
        bst = [TM.get([128, 8, 6], F32) for _ in range(2)]
        bmv = [TM.get([128, 8, 2], F32) for _ in range(2)]
        rstd8 = [TM.get([128, 8], F32) for _ in range(2)]
        nmr8 = [TM.get([128, 8], F32) for _ in range(2)]
        p1_end = TM.off
        TM.reset()
        gxb = TM.get([128, D], F32)
        fnwb = TM.get([128, D], F32)
        h_sbF = TM.get([128, 1024], F32)
        h_sb = [h_sbF[:, 0:512], h_sbF[:, 512:1024]]
        cgh = [TM.get([128, 1026], F32) for _ in range(2)]
        sza = [TM.get([128, 512], F32) for _ in range(2)]
        t1 = [TM.get([128, 1024], F32) for _ in range(2)]
        c0 = TM.get([128, 1024], F32)
        c1 = TM.get([128, 1024], F32)
        sig = [TM.get([128, 512], F32) for _ in range(2)]
        tmpd = [TM.get([128, 512], F32) for _ in range(2)]
        xn = [TM.get([128, D], F32) for _ in range(2)]
        hh = TM.get([128, 4], F32)

        P.dma("sp", lambda e: e.dma_start(out=vrow[0:11, :], in_=vecs_d[:, :]), "vrow", writes=["vrow"])
        P.dma("sp", lambda e: e.dma_start(out=identf, in_=ident_d[:, :]), "identf", writes=["identf"])
        P.dma("sp", lambda e: e.dma_start(out=dpn, in_=dpn_d[:, :]), "dpn", writes=["dpn"])
        P.dma("sp", lambda e: e.dma_start(out=pidx, in_=pidx_d[:, :]), "pidx", writes=["pidx"])
        P.dma("sp", lambda e: e.dma_start(out=dlb, in_=dl_d[0, :].partition_broadcast(128)), "dlb", writes=["dlb"])
        win_v = win_d.rearrange("(k p) c -> p k c", p=128)
        ada_v = ada_d.rearrange("(k p) c -> p k c", p=128)
        wa_v = wa_d.rearrange("(k p) c -> p k c", p=128)
        wb_v = wb_d.rearrange("(k p) c -> p k c", p=128)
        wo_v = wo_d.rearrange("(k p) c -> p k c", p=128)

        def ada_load(i):
            b = adas[i % 3]
            P.dma("pool", lambda e: e.dma_start(out=b, in_=ada_v[:, :, i * 512:(i + 1) * 512]),
                  f"adas{i % 3}", writes=[f"adas{i % 3}"])

        ada_load(0)
        ada_load(1)
        ada_load(2)

        P.op("dve", lambda e: e.tensor_copy(out=identb, in_=identf), reads=["identf"], writes=["identb"])

        def f_vtr(e):
            ins = None
            for k in range(8):
                ins = e.transpose(out=pf(6)[:, k * 16:k * 16 + 11], in_=vrow[0:11, k * 128:(k + 1) * 128],
                                  identity=identf[0:11, 0:11])
            return ins
        P.op("pe", f_vtr, reads=["vrow", "identf"], writes=["P6"])
        P.op("dve", lambda e: e.tensor_copy(out=vT[:, :, 0:11], in_=pf(6)[:, 0:128].rearrange("p (a b) -> p a b", a=8)[:, :, 0:11]),
             reads=["P6"], writes=["vT"])
        P.op("act", lambda e: e.activation(out=scT, in_=vT[:, :, 0:2], func=AF.Silu), reads=["vT"], writes=["scT"])

        P.op("act", lambda e: e.activation(out=tmp16, in_=dlb, func=AF.Exp, scale=-1.0), reads=["dlb"], writes=["tmp16"])
        P.op("act", lambda e: e.activation(out=nlg, in_=tmp16, func=AF.Ln, bias=1.0), reads=["tmp16"], writes=["nlg"])

        def f_decarg(e):
            e.memset(neghalf, -0.5)
            e.memset(onesf, 1.0)
            e.tensor_scalar(out=kdec[:, 0:8], in0=nlg[:, 0:8], scalar1=pidx[:, 1:2], scalar2=None, op0=ALU.mult)
            e.tensor_scalar(out=kdec[:, 8:16], in0=nlg[:, 8:16], scalar1=pidx[:, 0:1], scalar2=None, op0=ALU.mult)
            e.tensor_scalar(out=qdec[:, 0:8], in0=nlg[:, 0:8], scalar1=pidx[:, 2:3], scalar2=None, op0=ALU.mult)
            e.tensor_scalar(out=qdec[:, 8:16], in0=nlg[:, 8:16], scalar1=pidx[:, 3:4], scalar2=None, op0=ALU.mult)
            return e.tensor_scalar(out=dch, in0=nlg, scalar1=128.0, scalar2=None, op0=ALU.mult)
        P.op("dve", f_decarg, reads=["nlg", "pidx"], writes=["decarg", "neghalf", "onesf"])

        def f_decexp(e):
            e.activation(out=kdec, in_=kdec, func=AF.Exp, scale=-1.0)
            e.activation(out=qdec, in_=qdec, func=AF.Exp, scale=-1.0)
            return e.activation(out=dch, in_=dch, func=AF.Exp, scale=-1.0)
        P.op("act", f_decexp, reads=["decarg"], writes=["dec"])

        def f_dchs(e):
            e.memset(dchs0[0:64, :], 0.0)
            return e.tensor_copy(out=dchs0[64:128, :], in_=dch[64:128, 8:16])
        P.op("dve", f_dchs, reads=["dec"], writes=["dchs0"])

        def f_marg0(e):
            ins = None
            for h in range(8):
                ins = e.tensor_scalar(out=marg[:, h, :], in0=dpn[:, 0:128], scalar1=nlg[:, h:h + 1], scalar2=None, op0=ALU.mult)
            return ins
        P.op("dve", f_marg0, reads=["nlg", "dpn"], writes=["marg"])

        def f_marg(e):
            ins = None
            for h in range(8):
                ins = e.scalar_tensor_tensor(out=marg[:, h, :], in0=dpn[:, 128:256], scalar=nlg[:, 8 + h:9 + h],
                                             in1=marg[:, h, :], op0=ALU.mult, op1=ALU.add)
            return ins
        P.op("dve", f_marg, reads=["nlg", "dpn", "marg"], writes=["marg"])
        def f_marg3(e):
            ins = None
            for h in range(8):
                if h % 2 == 0:
                    iv, col = dpn[:, 256:384], nlg[:, h:h + 1]
                else:
                    iv, col = dpn[:, 384:512], nlg[:, 8 + h:9 + h]
                ins = e.scalar_tensor_tensor(out=marg[:, h, :], in0=iv, scalar=col, in1=marg[:, h, :], op0=ALU.mult, op1=ALU.subtract)
            return ins
        P.op("dve", f_marg3, reads=["nlg", "dpn", "marg"], writes=["marg"])
        P.op("act", lambda e: e.activation(out=marg, in_=marg, func=AF.Exp), reads=["marg"], writes=["marg2"])
        P.op("dve", lambda e: e.tensor_scalar(out=diag, in0=identf, scalar1=1.0, scalar2=None, op0=ALU.add), reads=["identf"], writes=["diag"])

        def f_mt(e):
            ins = None
            for h in range(8):
                ins = e.tensor_tensor(out=MT[:, h % 2, h // 2, :], in0=marg[:, h, :], in1=diag, op=ALU.mult)
            return ins
        P.op("dve", f_mt, reads=["marg2", "diag", "vT", "P6"], writes=["MT"])

        gcc = [0]

        def front_a1(rows_ap, np_):
            i = gcc[0] % 2
            gcc[0] += 1
            xcb, xsb = xc[i], xs[i]
            P.dma("sp", lambda e: e.dma_start(out=xcb[0:np_, :], in_=rows_ap), f"xc{i}", writes=[f"xc{i}", f"xc{i}h"])
            P.op("act", lambda e: e.activation(out=xsb[0:np_, :], in_=xcb[0:np_, :], func=AF.Square, accum_out=ssb[i][0:np_, :]),
                 reads=[f"xc{i}"], writes=[f"xs{i}", f"ss{i}"])
            return i

        def front_a2(i, np_, keep=None):
            xcb, xsb = xc[i], xs[i]
            rs_ap, rs_nm = (rsb[i][0:np_, :], f"rs{i}") if keep is None else keep
            P.op("pool", lambda e: e.tensor_scalar(out=msb[i][0:np_, :], in0=ssb[i][0:np_, :], scalar1=1.0 / D, scalar2=EPS, op0=ALU.mult, op1=ALU.add),
                 reads=[f"ss{i}"], writes=[f"ms{i}"])
            P.op("pool", lambda e: e.tensor_tensor(out=rs_ap, in0=msb[i][0:np_, :], in1=neghalf[0:np_, 0:1], op=ALU.pow),
                 reads=[f"ms{i}", "neghalf"], writes=[rs_nm])
            P.op("act", lambda e: e.activation(out=xsb[0:np_, :], in_=xcb[0:np_, :], func=AF.Copy, scale=rs_ap),
                 reads=[f"xc{i}", rs_nm], writes=[f"xs{i}"])

        def front_a(rows_ap, np_, keep=None):
            i = front_a1(rows_ap, np_)
            front_a2(i, np_, keep=keep)
            return i

        def front_a_cached(n):
            i = gcc[0] % 2
            gcc[0] += 1
            xcb, xsb = xc[i], xs[i]
            P.dma("sp", lambda e: e.dma_start(out=xcb, in_=x_d[n * 128:(n + 1) * 128, :]), f"xc{i}", writes=[f"xc{i}", f"xc{i}h"])
            P.op("act", lambda e: e.activation(out=xsb, in_=xcb, func=AF.Copy, scale=rs_own[:, n:n + 1]),
                 reads=[f"xc{i}", f"rso{n}"], writes=[f"xs{i}"])
            return i

        def front_b(i, np_, r, dst, dst_name, ptp=0, alias=()):
            xsb = xs[i]

            def f_tp(e):
                ins = None
                for k in range(8):
                    ins = e.transpose(out=pb(ptp)[:, k * 128:k * 128 + np_], in_=xsb[0:np_, k * 128:(k + 1) * 128],
                                      identity=identb[0:np_, 0:np_])
                return ins
            P.op("pe", f_tp, reads=[f"xs{i}", "identb"], writes=[f"P{ptp}"])

            def f_aff(e):
                ins = None
                for k in range(8):
                    ins = e.tensor_scalar(out=dst[:, k, :], in0=pb(ptp)[:, k * 128:k * 128 + np_],
                                          scalar1=Amod[:, k, r:r + 1], scalar2=modT[:, k, r:r + 1],
                                          op0=ALU.mult, op1=ALU.add)
                return ins
            P.op("dve", f_aff, reads=[f"P{ptp}", "Amod", "modT"], writes=[dst_name] + list(alias))

        def front(rows_ap, np_, r, dst, dst_name, ti=None, ptp=0):
            i = front_a(rows_ap, np_)
            front_b(i, np_, r, dst, dst_name, ptp=ptp)
            return i


        def rope_evac(pbank, pname, tab_lo, tab, tabname, dst_c, dst_s, dst_name):
            def f(e):
                src = pf(pbank).rearrange("p (h t f) -> p h t f", h=8, t=2)
                e.tensor_tensor(out=dst_c, in0=pf(pbank).rearrange("p (h f) -> p h f", h=8),
                                in1=tab[:, tab_lo:tab_lo + 64].unsqueeze(1).broadcast_to([128, 8, 64]), op=ALU.mult)
                e.tensor_tensor(out=dst_s[:, :, 0:32], in0=src[:, :, 1, :],
                                in1=tab[:, tab_lo + 64:tab_lo + 96].unsqueeze(1).broadcast_to([128, 8, 32]), op=ALU.mult)
                return e.tensor_tensor(out=dst_s[:, :, 32:64], in0=src[:, :, 0, :],
                                       in1=tab[:, tab_lo + 96:tab_lo + 128].unsqueeze(1).broadcast_to([128, 8, 32]), op=ALU.mult)
            P.op("dve", f, reads=[pname, tabname], writes=[dst_name])

        seq = [("ctx", ctx_d[128:256, :], 33, None), ("ctx", ctx_d[0:128, :], 32, None)]
        seq += [("other", x_d[n * 128:(n + 1) * 128, :], n, None) for n in range(31, 15, -1)]
        seq += [("own", x_d[n * 128:(n + 1) * 128, :], n, n) for n in range(15, -1, -1)]

        fidx = {}

        def s0a(c):
            kind, rows_ap, ti, n = seq[c]
            j3 = c % 3
            P.dma("sp", lambda e: e.dma_start(out=ropet[j3], in_=rope_d[ti, :, :]), f"ropet{j3}", writes=[f"ropet{j3}"])
            keep = (rs_own[:, n:n + 1], f"rso{n}") if kind == "own" else None
            fidx[c] = front_a(rows_ap, 128, keep=keep)

        def s0b(c):
            kind, rows_ap, ti, n = seq[c]
            r = 1 if kind == "ctx" else 0
            j3 = c % 3
            front_b(fidx[c], 128, r, xmTc[j3], f"xmTc{j3}")

        def s1(c):
            kind, rows_ap, ti, n = seq[c]
            own = kind == "own"
            j3, j = c % 3, c % 2
            xm_ = xmTc[j3]
            xmn = f"xmTc{j3}"
            vdst = vbuf[:, n, :] if own else vtmp[j]
            vname = f"v{n}" if own else f"vtmp{j}"

            def f_kproj(e):
                ins = None
                for k in range(8):
                    ins = e.matmul(pf(1), lhsT=xm_[:, k, :], rhs=wk[:, k, :], start=(k == 0), stop=(k == 7))
                return ins
            P.op("pe", f_kproj, reads=[xmn, "wk"], writes=["P1"])
            for hv in range(2):
                def f_vproj(e, hv=hv):
                    ins = None
                    for k in range(8):
                        ins = e.matmul(pf(3 + hv), lhsT=xm_[:, k, :], rhs=wv[:, k, hv * 512:(hv + 1) * 512],
                                       start=(k == 0), stop=(k == 7))
                    return ins
                P.op("pe", f_vproj, reads=[xmn, f"wv{hv}"], writes=[f"P{3 + hv}"])
            if own:
                def f_qproj(e):
                    ins = None
                    for k in range(8):
                        ins = e.matmul(pf(2), lhsT=xm_[:, k, :], rhs=wq[:, k, :], start=(k == 0), stop=(k == 7))
                    return ins
                P.op("pe", f_qproj, reads=[xmn, "wq"], writes=["P2"])
            rope_evac(1, "P1", 0, ropet[j3], f"ropet{j3}", kc, ks, "kcs")
            for hv in range(2):
                P.op("act", lambda e, hv=hv: e.activation(out=vdst[:, hv * 512:(hv + 1) * 512], in_=pf(3 + hv), func=AF.Copy),
                     reads=[f"P{3 + hv}"], writes=[vname + f"_{hv}"])
            P.op("dve", lambda e: e.tensor_tensor(out=k_rb[j], in0=kc, in1=ks, op=ALU.add), reads=["kcs"], writes=[f"k_rb{j}"])

            def f_kfb(e):
                e.tensor_tensor(out=kfb[j][:, :, 0:64], in0=k_rb[j], in1=kdec[:, 0:8].unsqueeze(2).broadcast_to([128, 8, 64]), op=ALU.mult)
                return e.tensor_tensor(out=kfb[j][:, :, 64:128], in0=k_rb[j], in1=kdec[:, 8:16].unsqueeze(2).broadcast_to([128, 8, 64]), op=ALU.mult)
            P.op("pool", f_kfb, reads=[f"k_rb{j}", "dec"], writes=[f"kfb{j}"])
            if own:
                rope_evac(2, "P2", 128, ropet[j3], f"ropet{j3}", qc, qs, "qcs")

                def f_ktp(e):
                    ins = None
                    kr = k_rb[j].rearrange("p h f -> p (h f)")
                    for hp in range(4):
                        ins = e.transpose(out=pb(7)[:, hp * 128:(hp + 1) * 128], in_=kr[:, hp * 128:(hp + 1) * 128], identity=identb)
                    return ins
                P.op("pe", f_ktp, reads=[f"k_rb{j}", "identb"], writes=["P7"])
                P.op("act", lambda e: e.activation(out=kT[:, :, n * 128:(n + 1) * 128],
                                                   in_=pb(7)[:, 0:512].rearrange("p (a b) -> p a b", a=4), func=AF.Copy),
                     reads=["P7"], writes=[f"kT{n}"])
                P.op("dve", lambda e: e.tensor_tensor(out=q_rb[:, n, :].rearrange("p (h f) -> p h f", h=8), in0=qc, in1=qs, op=ALU.add),
                     reads=["qcs"], writes=[f"q_rb{n}"])

        def s2(c):
            kind, rows_ap, ti, n = seq[c]
            own = kind == "own"
            j = c % 2
            ctx_second = c == 1
            vdst = vbuf[:, n, :] if own else vtmp[j]
            vname = f"v{n}" if own else f"vtmp{j}"

            def f_kv(e):
                ins = None
                for h in range(8):
                    ins = e.matmul(pf(5 + h // 4)[:, (h % 4) * 128:(h % 4 + 1) * 128], lhsT=kfb[j][:, h, :],
                                   rhs=vdst[:, h * 128:(h + 1) * 128], start=True, stop=True)
                return ins
            ro, rn = c % 2, (c + 1) % 2
            Ro, Rn = Rst[ro], Rst[rn]
            if own:
                P.op("act", lambda e: e.activation(out=states[64:128, n], in_=Ro[64:128], func=AF.Copy),
                     reads=[f"R{ro}"], writes=[f"stb{n}"])
            P.op("pe", f_kv, reads=[f"kfb{j}", vname + "_0", vname + "_1"], writes=["P5", "P6"])
            if own:
                def f_stf(e):
                    e.activation(out=states[0:64, n, 0:4, :], in_=pf(5)[0:64, :].rearrange("p (a b) -> p a b", a=4), func=AF.Copy)
                    return e.activation(out=states[0:64, n, 4:8, :], in_=pf(6)[0:64, :].rearrange("p (a b) -> p a b", a=4), func=AF.Copy)
                P.op("act", f_stf, reads=["P5", "P6"], writes=[f"stf{n}"])

            def f_state(e):
                ins = None
                for h in range(8):
                    pk_ = pf(5 + h // 4)[:, (h % 4) * 128:(h % 4 + 1) * 128]
                    if ctx_second:
                        e.scalar_tensor_tensor(out=Rn[0:64, h, :], in0=pk_[0:64, :], scalar=dch[0:64, h:h + 1],
                                               in1=Ro[0:64, h, :], op0=ALU.mult, op1=ALU.add)
                        ins = e.scalar_tensor_tensor(out=Rn[64:128, h, :], in0=Ro[64:128, h, :], scalar=dchs0[64:128, h:h + 1],
                                                     in1=pk_[64:128, :], op0=ALU.mult, op1=ALU.add)
                    else:
                        ins = e.scalar_tensor_tensor(out=Rn[:, h, :], in0=Ro[:, h, :], scalar=dchs0[:, h:h + 1],
                                                     in1=pk_, op0=ALU.mult, op1=ALU.add)
                return ins
            P.op("dve", f_state, reads=["P5", "P6", f"R{ro}", "dchs0", "dec"], writes=[f"R{rn}"])
            if ctx_second:
                P.op("act", lambda e: e.activation(out=Fst[0][0:64], in_=Rn[0:64], func=AF.Copy), reads=[f"R{rn}"], writes=["F0"])

        s0a(0)
        s0a(1)
        def ada_mm(i):
            def f_mod(e):
                ins = None
                b = adas[i % 3]
                for c4 in range(4):
                    ct = i * 4 + c4
                    for k in range(8):
                        ins = e.matmul(pf(7)[:, ct * 2:ct * 2 + 2], lhsT=b[:, k, c4 * 128:(c4 + 1) * 128],
                                       rhs=scT[:, k, :], start=(k == 0), stop=(k == 7))
                return ins
            P.op("pe", f_mod, reads=[f"adas{i % 3}", "scT"], writes=["P7"])

        for i in range(4):
            ada_mm(i)
            if i + 3 < 4:
                ada_load(i + 3)

        P.dma("pool", lambda e: e.dma_start(out=wk, in_=win_v[:, :, O_K:O_K + 512]), "wk", writes=["wk"])
        P.dma("pool", lambda e: e.dma_start(out=wv[:, :, 0:512], in_=win_v[:, :, O_V:O_V + 512]), "wv0", writes=["wv0"])
        P.dma("pool", lambda e: e.dma_start(out=wv[:, :, 512:1024], in_=win_v[:, :, O_V + 512:O_V + 1024]), "wv1", writes=["wv1"])

        def modT_part(ts, name):
            def f_modT(e):
                ins = None
                for t in ts:
                    ins = e.tensor_tensor(out=modT[:, t * 8:(t + 1) * 8, :],
                                          in0=pf(7)[:, t * 16:(t + 1) * 16].rearrange("p (a b) -> p a b", a=8),
                                          in1=vT[:, :, 8 + t:9 + t].broadcast_to([128, 8, 2]), op=ALU.add)
                return ins
            P.op("dve", f_modT, reads=["P7", "vT"], writes=[name])
        modT_part((0, 1), "modT")

        P.op("dve", lambda e: e.tensor_scalar(out=Amod, in0=modT[:, 8:16, :], scalar1=1.0, scalar2=None, op0=ALU.add),
             reads=["modT"], writes=["Amod"])
        P.op("dve", lambda e: e.tensor_tensor(out=Amod, in0=Amod, in1=vT[:, :, 2:3].broadcast_to([128, 8, 2]), op=ALU.mult),
             reads=["Amod", "vT"], writes=["Amod"])

        P.barrier(["vrow", "dpn", "dlb", "marg", "marg2", "tmp16", "decarg", "P6", "P7"])
        P.op("dve", lambda e: e.memset(Rst[0], 0.0), writes=["R0"])
        NS = len(seq)
        for t in range(NS + 3):
            if t == 4:
                P.dma("pool", lambda e: e.dma_start(out=wq, in_=win_v[:, :, O_Q:O_Q + 512]), "wq", writes=["wq"])
            if t == 5:
                ada_load(4)
                ada_load(5)
            if t == 9:
                ada_mm(4)
                ada_mm(5)
                modT_part((2,), "modTg")
            if 2 <= t < NS:
                s0a(t)
            if 0 <= t - 2 < NS:
                s1(t - 2)
            if t < NS:
                s0b(t)
            if 0 <= t - 3 < NS:
                s2(t - 3)

        RETB = [(4, 5), (6, 7)]

        def o_sweep(n):
            Fo, Fn = Fst[n % 2], Fst[(n + 1) % 2]

            def f_sweep(e):
                ins = None
                for h in range(8):
                    ins = e.scalar_tensor_tensor(out=Fn[0:64, h, :], in0=Fo[0:64, h, :], scalar=dch[0:64, h:h + 1],
                                                 in1=states[0:64, n, h, :], op0=ALU.mult, op1=ALU.add)
                return ins
            P.op("dve", f_sweep, reads=[f"F{n % 2}", f"stf{n}", "dec"], writes=[f"F{(n + 1) % 2}"])

        def o_stcopy(n):
            Fo = Fst[n % 2]
            P.op("act", lambda e: e.activation(out=states[0:64, n], in_=Fo[0:64], func=AF.Copy),
                 reads=[f"F{n % 2}"], writes=[f"stf{n}"])

        def o_qfb(n):
            j = n % 2

            def f_qfb(e):
                qv = q_rb[:, n, :].rearrange("p (h f) -> p h f", h=8)
                e.tensor_tensor(out=qfb[j][:, :, 0:64], in0=qv, in1=qdec[:, 0:8].unsqueeze(2).broadcast_to([128, 8, 64]), op=ALU.mult)
                return e.tensor_tensor(out=qfb[j][:, :, 64:128], in0=qv, in1=qdec[:, 8:16].unsqueeze(2).broadcast_to([128, 8, 64]), op=ALU.mult)
            P.op("pool", f_qfb, reads=[f"q_rb{n}", "dec"], writes=[f"qfb{j}"])

        def o_qdtp(n):
            j = n % 2

            def f_qdtp(e):
                ins = None
                for h in range(8):
                    ins = e.transpose(out=pb(1)[:, h * 128:(h + 1) * 128], in_=qfb[j][:, h, :], identity=identb)
                return ins
            P.op("pe", f_qdtp, reads=[f"qfb{j}", "identb"], writes=["P1"])

        def o_qdTc(n):
            j = n % 2
            P.op("act", lambda e: e.activation(out=qdTc[j], in_=pb(1).rearrange("p (a b) -> p a b", a=8), func=AF.Copy),
                 reads=["P1"], writes=[f"qdTc{j}"])

        def o_sc(n):
            j = n % 2

            def f_sc(e):
                ins = None
                for h in range(8):
                    par, hp = h % 2, h // 2
                    b0 = 64 * par
                    ins = e.matmul(pf(2 + par)[:, hp * 128:(hp + 1) * 128], lhsT=kT[b0:b0 + 64, hp, n * 128:(n + 1) * 128],
                                   rhs=qdTc[j][b0:b0 + 64, h, :], start=True, stop=True)
                return ins
            P.op("pe", f_sc, reads=[f"kT{n}", f"qdTc{j}"], writes=["P2", "P3"])

        def o_mask(n):
            def f_mask(e):
                ins = None
                for par in range(2):
                    ins = e.tensor_tensor(out=PT[:, par], in0=pf(2 + par).rearrange("p (a b) -> p a b", a=4), in1=MT[:, par], op=ALU.mult)
                return ins
            P.op("dve", f_mask, reads=["P2", "P3", "MT"], writes=["PT"])

        def o_ret(n):
            j = n % 2
            pA, pB = RETB[n % 2]

            def f_ret(e):
                ins = None
                for h in range(8):
                    par, hp = h % 2, h // 2
                    o = pf((pA, pB)[h // 4])[:, (h % 4) * 128:(h % 4 + 1) * 128]
                    e.matmul(o, lhsT=PT[:, par, hp, :], rhs=vbuf[:, n, h * 128:(h + 1) * 128], start=True, stop=False)
                    ins = e.matmul(o, lhsT=qdTc[j][:, h, :], rhs=states[:, n, h, :], start=False, stop=True)
                return ins
            P.op("pe", f_ret, reads=["PT", f"v{n}_0", f"v{n}_1", f"qdTc{j}", f"stf{n}", f"stb{n}"], writes=[f"P{pA}", f"P{pB}"])

        def o_bn(n):
            pA, pB = RETB[n % 2]
            sj = n % 2

            def f_bn(e):
                ins = None
                for h in range(8):
                    o = pf((pA, pB)[h // 4])[:, (h % 4) * 128:(h % 4 + 1) * 128]
                    ins = e.bn_stats(out=bst[sj][:, h, :], in_=o)
                return ins
            P.op("dve", f_bn, reads=[f"P{pA}", f"P{pB}"], writes=[f"bst{sj}"])

            def f_bna(e):
                ins = None
                for h in range(8):
                    ins = e.bn_aggr(out=bmv[sj][:, h, :], in_=bst[sj][:, h, :])
                return ins
            P.op("dve", f_bna, reads=[f"bst{sj}"], writes=[f"bmv{sj}"])

        def o_stats(n):
            sj = n % 2
            P.op("pool", lambda e: e.tensor_scalar(out=rstd8[sj], in0=bmv[sj][:, :, 1], scalar1=EPS, scalar2=None, op0=ALU.add),
                 reads=[f"bmv{sj}"], writes=[f"rstd8{sj}"])
            P.op("pool", lambda e: e.tensor_tensor(out=rstd8[sj], in0=rstd8[sj], in1=neghalf, op=ALU.pow),
                 reads=[f"rstd8{sj}", "neghalf"], writes=[f"rstd8{sj}"])
            P.op("pool", lambda e: e.tensor_tensor(out=nmr8[sj], in0=bmv[sj][:, :, 0], in1=rstd8[sj], op=ALU.mult),
                 reads=[f"bmv{sj}", f"rstd8{sj}"], writes=[f"nmr8{sj}"])
            P.op("pool", lambda e: e.tensor_scalar(out=nmr8[sj], in0=nmr8[sj], scalar1=-1.0, scalar2=None, op0=ALU.mult),
                 reads=[f"nmr8{sj}"], writes=[f"nmr8{sj}"])

        def o_norm(n):
            pA, pB = RETB[n % 2]
            sj = n % 2

            def f_norm(e):
                ins = None
                for h in range(8):
                    o = pf((pA, pB)[h // 4])[:, (h % 4) * 128:(h % 4 + 1) * 128]
                    ins = e.activation(out=retn[:, h * 128:(h + 1) * 128], in_=o, func=AF.Identity,
                                       scale=rstd8[sj][:, h:h + 1], bias=nmr8[sj][:, h:h + 1])
                return ins
            P.op("act", f_norm, reads=[f"P{pA}", f"P{pB}", f"nmr8{sj}", f"rstd8{sj}"], writes=["retn"])

        def o_rtp(n):
            def f_rtp(e):
                ins = None
                for c in range(8):
                    ins = e.transpose(out=pb(0)[:, c * 128:(c + 1) * 128], in_=retn[:, c * 128:(c + 1) * 128], identity=identb)
                return ins
            P.op("pe", f_rtp, reads=["retn", "identb"], writes=["P0"])

        def o_retnT(n):
            P.op("act", lambda e: e.activation(out=retnT[:, :, n * 128:(n + 1) * 128], in_=pb(0).rearrange("p (a b) -> p a b", a=8), func=AF.Copy),
                 reads=["P0"], writes=["wq", "wk", "wv0", "wv1", f"retnT_{n}"])

        slot_use = {}

        def slab_load(g, src_v, col0, width=256, alias=()):
            u = slot_use.get(g, 0)
            slot_use[g] = u + 1
            b = fsl[g][u % 2]
            nm = f"fsl{g}_{u % 2}"
            P.dma("pool", lambda e: e.dma_start(out=b[:, :, 0:width], in_=src_v[:, :, col0:col0 + width]), nm,
                  writes=[nm] + list(alias))
            return b, nm

        early0 = []
        ok = lambda n: 0 <= n < NCH
        o_sweep(0)
        o_stcopy(0)
        o_qfb(0)
        o_qdtp(0)
        o_qdTc(0)
        VB07 = [f"v{n}_{h}" for n in range(8) for h in range(2)]
        s0f = {}
        for t in range(NCH + 2):
            a, b_, c_ = t, t - 1, t - 2
            if ok(a + 1):
                o_qfb(a + 1)
            if ok(b_):
                o_bn(b_)
            if ok(c_):
                o_norm(c_)
            if ok(a + 1):
                o_sweep(a + 1)
                o_qdtp(a + 1)
            if ok(b_):
                o_stats(b_)
            if ok(c_):
                o_rtp(c_)
            if ok(a + 1):
                o_qdTc(a + 1)
                o_stcopy(a + 1)
            if ok(c_):
                o_retnT(c_)
            if ok(a):
                o_sc(a)
                o_mask(a)
                o_ret(a)
            if 9 <= t < 17:
                front_b(s0f[t - 9], 128, 0, xmT[:, :, (t - 9) * 128:(t - 8) * 128], f"xmT{t - 9}", alias=VB07)
            if t == 14:
                al = [[f"v{n}_{h}" for n in (8, 9) for h in range(2)], [f"v{n}_{h}" for n in (10, 11) for h in range(2)],
                      ["stf0", "stb0", "stf1", "stb1"], ["stf4", "stb4", "stf5", "stb5"]]
                for g_, off_ in enumerate((O_H, O_CG, O_BG, O_ZA)):
                    early0.append(slab_load(g_, win_v, off_, alias=al[g_]))
            if 8 <= t < 16:
                s0f[t - 8] = front_a_cached(t - 8)

        ALL_RETN = [f"retnT_{n}" for n in range(NCH)]
        PH1_R2 = [f"q_rb{n}" for n in range(NCH)] + [f"kT{n}" for n in range(NCH)] + \
                 [f"v{n}_{h}" for n in range(NCH) for h in range(2)] + [f"stf{n}" for n in range(NCH)] + [f"stb{n}" for n in range(NCH)]
        PH1_TM = ["MT", "PT", "retn", "qfb0", "qfb1", "qdTc0", "qdTc1", "bmv0", "bmv1", "bst0", "bst1", "rstd80", "rstd81", "nmr80", "nmr81", "F0", "F1", "R0", "R1", "kcs", "qcs",
                  "k_rb0", "k_rb1", "kfb0", "kfb1", "xmTc0", "xmTc1", "xmTc2", "vtmp0_0", "vtmp0_1", "vtmp1_0", "vtmp1_1"]

        dump("q_rb", q_rb, [128, NCH, 512], BF16, [f"q_rb{n}" for n in range(NCH)])
        dump("kT", kT, [128, 4, HALF], BF16, [f"kT{n}" for n in range(NCH)])
        dump("vbuf", vbuf, [128, NCH, 1024], BF16, [f"v{n}_{h}" for n in range(NCH) for h in range(2)])
        dump("states", states, [128, NCH, 8, 128], BF16, [f"stf{n}" for n in range(NCH)] + [f"stb{n}" for n in range(NCH)])
        dump("MT", MT, [128, 2, 4, 128], F32, ["MT"])
        dump("F0", Fst[0], [128, 8, 128], F32, ["F0"])
        dump("retn", retn, [128, 1024], BF16, ["retn"])
        dump("PT", PT, [128, 2, 4, 128], BF16, ["PT"])
        dump("qdTc", qdTc[1], [128, 8, 128], BF16, ["qdTc1"])
        P.barrier(PH1_R2 + PH1_TM)
        P.dma("sp", lambda e: e.dma_start(out=fnwb, in_=fnw_d[0, :].partition_broadcast(128)), "fnwb", writes=["fnwb"])

        def emit_gate_tile():
            for k in range(8):
                P.op("dve", lambda e, k=k: e.tensor_scalar(out=diag, in0=identf, scalar1=modT[:, 16 + k, 0:1], scalar2=None, op0=ALU.mult),
                     reads=["identf", "modTg"], writes=["diag"])
                P.op("pe", lambda e, k=k: e.matmul(pf(k // 4)[:, (k % 4) * 128:(k % 4 + 1) * 128], lhsT=onesf, rhs=diag, start=True, stop=True),
                     reads=["diag", "onesf"], writes=[f"P{k // 4}"])
            for hv in range(2):
                P.op("act", lambda e, hv=hv: e.activation(out=gxb[:, hv * 512:(hv + 1) * 512], in_=pf(hv), func=AF.Copy),
                     reads=[f"P{hv}"], writes=[f"gxb{hv}"])

        step_specs = []
        for sb_ in range(2):
            for ctp_ in range(4):
                step_specs.append([(g_, win_v, off_ + ctp_ * 256) for g_, off_ in enumerate((O_H, O_CG, O_BG, O_ZA))])
            for ctp_ in range(4):
                step_specs.append([(0, wa_v, ctp_ * 256), (1, win_v, O_GA + ctp_ * 256)])
            for ctp_ in range(4):
                step_specs.append([(2, win_v, O_ZB + ctp_ * 256)])
            for ctp_ in range(4):
                step_specs.append([(3, wb_v, ctp_ * 256), (0, win_v, O_GB + ctp_ * 256)])
            step_specs.append("wo")
        step_res = {}
        step_ptr = [0]

        def issue_step(si):
            if si >= len(step_specs) or si in step_res:
                return
            spec = step_specs[si]
            if si == 0 and early0:
                step_res[0] = early0
                return
            if spec == "wo":
                for hv in range(2):
                    P.dma("pool", lambda e, hv=hv: e.dma_start(out=wo[hv], in_=wo_v[:, :, hv * 512:(hv + 1) * 512]), f"wo{hv}",
                          writes=[f"wo{hv}"])
                step_res[si] = None
            else:
                step_res[si] = [slab_load(g_, v_, c_) for (g_, v_, c_) in spec]

        def take_step():
            si = step_ptr[0]
            step_ptr[0] += 1
            issue_step(si)
            issue_step(si + 1)
            if si + 2 < len(step_specs) and step_specs[si + 2] == "wo":
                issue_step(si + 2)
            return step_res[si]

        issue_step(0)
        pair_i = [0]

        def next_pair():
            p = 2 + 2 * (pair_i[0] % 3)
            pair_i[0] += 1
            return p, p + 1

        def mm8(e, pbank, slab, c0_, rhs_fn, n_=512):
            ins = None
            for k in range(8):
                ins = e.matmul(pf(pbank)[:, 0:n_], lhsT=slab[:, k, c0_:c0_ + 128], rhs=rhs_fn(k), start=(k == 0), stop=(k == 7))
            return ins

        def stage0_chunk(sb_, jx):
            T0_ = sb_ * 1024
            front(x_d[T0_ + jx * 128:T0_ + (jx + 1) * 128, :], 128, 0, xmT[:, :, jx * 128:(jx + 1) * 128], f"xmT{jx}")

        def stage0_halo(sb_):
            T0_ = sb_ * 1024
            lrow = max(T0_ - 1, 0)
            i = gcc[0] % 2
            gcc[0] += 1
            P.dma("sp", lambda e: e.dma_start(out=xc[i][0:1, :], in_=x_d[lrow:lrow + 1, :]), "haloL", writes=[f"xc{i}"], n=1)
            P.dma("sp", lambda e: e.dma_start(out=xc[i][1:2, :], in_=x_d[T0_ + 1024:T0_ + 1025, :]), "haloR", writes=[f"xc{i}h"], n=1)
            P.op("act", lambda e: e.activation(out=xs[i][0:2, :], in_=xc[i][0:2, :], func=AF.Square, accum_out=ssb[i][0:2, :]),
                 reads=[f"xc{i}", f"xc{i}h"], writes=[f"xs{i}", f"ss{i}"])
            P.op("pool", lambda e: e.tensor_scalar(out=msb[i][0:2, :], in0=ssb[i][0:2, :], scalar1=1.0 / D, scalar2=EPS, op0=ALU.mult, op1=ALU.add),
                 reads=[f"ss{i}"], writes=[f"ms{i}"])
            P.op("pool", lambda e: e.tensor_tensor(out=rsb[i][0:2, :], in0=msb[i][0:2, :], in1=neghalf[0:2, 0:1], op=ALU.pow),
                 reads=[f"ms{i}", "neghalf"], writes=[f"rs{i}"])
            P.op("act", lambda e: e.activation(out=xs[i][0:2, :], in_=xc[i][0:2, :], func=AF.Copy, scale=rsb[i][0:2, :]),
                 reads=[f"xc{i}", f"xc{i}h", f"rs{i}"], writes=[f"xs{i}"])

            def f_tph(e):
                ins = None
                for k in range(8):
                    ins = e.transpose(out=pb(0)[:, k * 2:k * 2 + 2], in_=xs[i][0:2, k * 128:(k + 1) * 128], identity=identb[0:2, 0:2])
                return ins
            P.op("pe", f_tph, reads=[f"xs{i}", "identb"], writes=["P0"])

            def f_affh(e):
                ins = None
                for k in range(8):
                    ins = e.tensor_scalar(out=xmTh[:, k, :], in0=pb(0)[:, k * 2:k * 2 + 2], scalar1=Amod[:, k, 0:1], scalar2=modT[:, k, 0:1],
                                          op0=ALU.mult, op1=ALU.add)
                return ins
            P.op("dve", f_affh, reads=["P0", "Amod", "modT"], writes=["xmTh"])

        for sb in range(2):
            T0 = sb * 1024
            if sb == 0:
                stage0_halo(0)
            XM = [f"xmT{jx}" for jx in range(8)]

            for ctp in range(4):
                sl = dict(enumerate(take_step()))
                for c2 in range(2):
                    ct = ctp * 2 + c2
                    cb = cgh[ct % 2]
                    cbn = f"cgh{ct % 2}"
                    tb1 = t1[ct % 2]
                    def f_halo(e, c2=c2, sl=sl):
                        ins = None
                        for gi, g in enumerate((0, 1)):
                            for k in range(8):
                                ins = e.matmul(pf(1)[:, gi * 2:gi * 2 + 2], lhsT=sl[g][0][:, k, c2 * 128:(c2 + 1) * 128], rhs=xmTh[:, k, :],
                                               start=(k == 0), stop=(k == 7))
                        return ins
                    P.op("pe", f_halo, reads=[sl[0][1], sl[1][1], "xmTh"], writes=["P1"])
                    P.op("act", lambda e: e.activation(out=hh[:, 0:2], in_=pf(1)[:, 0:2], func=AF.Copy), reads=["P1"], writes=["hh"])

                    def f_halo2(e, cb=cb, sb=sb):
                        if sb == 0:
                            e.memset(cb[:, 0:1], 0.0)
                        else:
                            e.tensor_tensor(out=cb[:, 0:1], in0=pf(1)[:, 2:3], in1=hh[:, 0:1], op=ALU.mult)
                        return e.tensor_tensor(out=cb[:, 1025:1026], in0=pf(1)[:, 3:4], in1=hh[:, 1:2], op=ALU.mult)
                    P.op("dve", f_halo2, reads=["P1", "hh"], writes=[cbn + "h"])
                    for tb in range(2):
                        pa, pb_ = next_pair()
                        hs = h_sb[tb]

                        def f_hcg(e, c2=c2, tb=tb, pa=pa, pb_=pb_, sl=sl):
                            mm8(e, pa, sl[0][0], c2 * 128, lambda k: xmT[:, k, tb * 512:(tb + 1) * 512])
                            return mm8(e, pb_, sl[1][0], c2 * 128, lambda k: xmT[:, k, tb * 512:(tb + 1) * 512])
                        P.op("pe", f_hcg, reads=[sl[0][1], sl[1][1]] + XM[tb * 4:(tb + 1) * 4], writes=[f"P{pa}", f"P{pb_}"])
                        P.op("act", lambda e, pa=pa, hs=hs: e.activation(out=hs, in_=pf(pa), func=AF.Copy), reads=[f"P{pa}"], writes=[f"h_sb{tb}"])
                        P.op("dve", lambda e, pb_=pb_, hs=hs, cb=cb, tb=tb: e.tensor_tensor(out=cb[:, 1 + tb * 512:1 + (tb + 1) * 512], in0=pf(pb_), in1=hs, op=ALU.mult),
                             reads=[f"P{pb_}", f"h_sb{tb}"], writes=[cbn + f"_{tb}"])
                        pa2, pb2 = next_pair()
                        sz = sza[tb]

                        def f_bgza(e, c2=c2, tb=tb, pa2=pa2, pb2=pb2, sl=sl):
                            mm8(e, pa2, sl[2][0], c2 * 128, lambda k: xmT[:, k, tb * 512:(tb + 1) * 512])
                            return mm8(e, pb2, sl[3][0], c2 * 128, lambda k: xmT[:, k, tb * 512:(tb + 1) * 512])
                        P.op("pe", f_bgza, reads=[sl[2][1], sl[3][1]] + XM[tb * 4:(tb + 1) * 4], writes=[f"P{pa2}", f"P{pb2}"])
                        P.op("act", lambda e, pb2=pb2, sz=sz: e.activation(out=sz, in_=pf(pb2), func=AF.Silu), reads=[f"P{pb2}"], writes=[f"sza{tb}"])
                        P.op("dve", lambda e, pa2=pa2, sz=sz, tb1=tb1, tb=tb: e.tensor_tensor(out=tb1[:, tb * 512:(tb + 1) * 512], in0=pf(pa2), in1=sz, op=ALU.mult),
                             reads=[f"P{pa2}", f"sza{tb}"], writes=[f"t1_{ct % 2}_{tb}"])
                    CG = [cbn + "h", cbn + "_0", cbn + "_1"]
                    P.op("act", lambda e, cb=cb, ct=ct: e.activation(out=c0, in_=cb[:, 1:1025], func=AF.Identity, scale=vT[:, ct, 4:5], bias=vT[:, ct, 6:7]),
                         reads=CG + ["vT"], writes=["c0"])
                    P.op("dve", lambda e, cb=cb, ct=ct: e.scalar_tensor_tensor(out=c1, in0=cb[:, 0:1024], scalar=vT[:, ct, 3:4], in1=c0, op0=ALU.mult, op1=ALU.add),
                         reads=CG + ["c0", "vT"], writes=["c1"])
                    P.op("dve", lambda e, cb=cb, ct=ct: e.scalar_tensor_tensor(out=c0, in0=cb[:, 2:1026], scalar=vT[:, ct, 5:6], in1=c1, op0=ALU.mult, op1=ALU.add),
                         reads=CG + ["c1", "vT"], writes=["c0"])
                    P.op("pool", lambda e, ct=ct, tb1=tb1: e.tensor_tensor(out=bufA[:, ct, :], in0=c0, in1=tb1, op=ALU.mult),
                         reads=["c0", f"t1_{ct % 2}_0", f"t1_{ct % 2}_1"], writes=[f"bufA{ct}"])
            BA = [f"bufA{c}" for c in range(8)]

            for ctp in range(4):
                s_wa, s_ga = take_step()
                if sb == 0 and ctp == 1:
                    emit_gate_tile()
                for c2 in range(2):
                    ct = ctp * 2 + c2
                    for tb in range(2):
                        pa, pb_ = next_pair()

                        def f_b(e, c2=c2, tb=tb, pa=pa, pb_=pb_, s_wa=s_wa, s_ga=s_ga):
                            mm8(e, pa, s_wa[0], c2 * 128, lambda k: bufA[:, k, tb * 512:(tb + 1) * 512])
                            return mm8(e, pb_, s_ga[0], c2 * 128, lambda k: xmT[:, k, tb * 512:(tb + 1) * 512])
                        P.op("pe", f_b, reads=[s_wa[1], s_ga[1]] + BA + XM[tb * 4:(tb + 1) * 4], writes=[f"P{pa}", f"P{pb_}"])
                        sg = sig[tb]
                        P.op("act", lambda e, pb_=pb_, sg=sg: e.activation(out=sg, in_=pf(pb_), func=AF.Sigmoid), reads=[f"P{pb_}"], writes=[f"sig{tb}"])
                        P.op("dve", lambda e, pa=pa, sg=sg, ct=ct, tb=tb: e.tensor_tensor(out=mbuf[:, ct, tb * 512:(tb + 1) * 512], in0=pf(pa), in1=sg, op=ALU.mult),
                             reads=[f"P{pa}", f"sig{tb}"], writes=[f"m{ct}_{tb}"])

            for ctp in range(4):
                (s_zb,) = take_step()
                for c2 in range(2):
                    ct = ctp * 2 + c2
                    for tb in range(2):
                        pa, pb_ = next_pair()
                        P.op("pe", lambda e, c2=c2, tb=tb, pa=pa, s_zb=s_zb: mm8(e, pa, s_zb[0], c2 * 128, lambda k: xmT[:, k, tb * 512:(tb + 1) * 512]),
                             reads=[s_zb[1]] + XM[tb * 4:(tb + 1) * 4], writes=[f"P{pa}", f"P{pb_}"])
                        sz = sza[tb]
                        P.op("act", lambda e, pa=pa, sz=sz: e.activation(out=sz, in_=pf(pa), func=AF.Silu), reads=[f"P{pa}"], writes=[f"sza{tb}"])
                        P.op("dve", lambda e, sz=sz, ct=ct, tb=tb, T0=T0: e.scalar_tensor_tensor(
                            out=bufA[:, ct, tb * 512:(tb + 1) * 512], in0=retnT[:, ct, T0 + tb * 512:T0 + (tb + 1) * 512],
                            scalar=vT[:, ct, 7:8], in1=sz, op0=ALU.mult, op1=ALU.mult),
                             reads=[f"sza{tb}", "vT"] + ALL_RETN, writes=[f"bufA{ct}"])

            for ctp in range(4):
                s_wb, s_gb = take_step()
                if ctp == 3:
                    for hv in range(2):
                        P.op("dve", lambda e, hv=hv: e.tensor_tensor(out=wo[hv], in0=wo[hv],
                                                                     in1=gxb[:, hv * 512:(hv + 1) * 512].unsqueeze(1).broadcast_to([128, 8, 512]),
                                                                     op=ALU.mult),
                             reads=[f"wo{hv}", f"gxb{hv}"], writes=[f"wo{hv}"])
                for c2 in range(2):
                    ct = ctp * 2 + c2
                    for tb in range(2):
                        pa, pb_ = next_pair()

                        def f_d(e, c2=c2, tb=tb, pa=pa, pb_=pb_, s_wb=s_wb, s_gb=s_gb):
                            mm8(e, pa, s_wb[0], c2 * 128, lambda k: bufA[:, k, tb * 512:(tb + 1) * 512])
                            return mm8(e, pb_, s_gb[0], c2 * 128, lambda k: xmT[:, k, tb * 512:(tb + 1) * 512])
                        P.op("pe", f_d, reads=[s_wb[1], s_gb[1]] + BA + XM[tb * 4:(tb + 1) * 4], writes=[f"P{pa}", f"P{pb_}"])
                        sg = sig[tb]
                        td = tmpd[tb]
                        P.op("act", lambda e, pb_=pb_, sg=sg: e.activation(out=sg, in_=pf(pb_), func=AF.Sigmoid), reads=[f"P{pb_}"], writes=[f"sig{tb}"])
                        P.op("dve", lambda e, pa=pa, sg=sg, td=td: e.tensor_tensor(out=td, in0=pf(pa), in1=sg, op=ALU.mult),
                             reads=[f"P{pa}", f"sig{tb}"], writes=[f"tmpd{tb}"])
                        P.op("pool", lambda e, td=td, ct=ct, tb=tb: e.tensor_tensor(out=mbuf[:, ct, tb * 512:(tb + 1) * 512],
                                                                                    in0=mbuf[:, ct, tb * 512:(tb + 1) * 512], in1=td, op=ALU.add),
                             reads=[f"tmpd{tb}", f"m{ct}_{tb}"], writes=[f"m{ct}_{tb}"])
            MM = [f"m{c}_{t}" for c in range(8) for t in range(2)]

            take_step()

            ring = [xn[0], xn[1], c1, h_sbF]
            ringn = [["xn0"], ["xn1"], ["c1"], ["h_sb0", "h_sb1"]]

            def e_load(jx):
                r = jx % 4
                r0 = T0 + jx * 128
                P.dma("sp", lambda e: e.dma_start(out=ring[r], in_=x_d[r0:r0 + 128, :]), f"xnr{r}", writes=ringn[r])

            def e_mm(jx):
                r = jx % 4
                xnb = ring[r]
                pa, pb_ = next_pair()

                def f_e(e):
                    ins = None
                    for hv, pbk in enumerate((pa, pb_)):
                        for k in range(8):
                            ins = e.matmul(pf(pbk), lhsT=mbuf[:, k, jx * 128:(jx + 1) * 128], rhs=wo[hv][:, k, :], start=(k == 0), stop=(k == 7))
                    return ins
                P.op("pe", f_e, reads=MM + ["wo0", "wo1"], writes=[f"P{pa}", f"P{pb_}"])
                def f_res(e):
                    e.tensor_tensor(out=xnb[:, 0:512], in0=pf(pa), in1=xnb[:, 0:512], op=ALU.add)
                    return e.tensor_tensor(out=xnb[:, 512:1024], in0=pf(pb_), in1=xnb[:, 512:1024], op=ALU.add)
                P.op("dve", f_res, reads=[f"P{pa}", f"P{pb_}"] + ringn[r], writes=ringn[r])

            def e_stats(jx):
                r = jx % 4
                xnb = ring[r]
                P.op("act", lambda e: e.activation(out=c0, in_=xnb, func=AF.Square, accum_out=sse[r]),
                     reads=ringn[r], writes=["c0", f"sse{r}"])
                P.op("pool", lambda e: e.tensor_scalar(out=mse[r], in0=sse[r], scalar1=1.0 / D, scalar2=EPS, op0=ALU.mult, op1=ALU.add),
                     reads=[f"sse{r}"], writes=[f"mse{r}"])
                P.op("pool", lambda e: e.tensor_tensor(out=rse[r], in0=mse[r], in1=neghalf[:, 0:1], op=ALU.pow),
                     reads=[f"mse{r}", "neghalf"], writes=[f"rse{r}"])

            def e_part2(jx):
                r = jx % 4
                r0 = T0 + jx * 128
                xnb = ring[r]
                P.op("dve", lambda e: e.scalar_tensor_tensor(out=xnb, in0=xnb, scalar=rse[r], in1=fnwb, op0=ALU.mult, op1=ALU.mult),
                     reads=ringn[r] + [f"rse{r}", "fnwb"], writes=ringn[r])
                tok = P.dma("sp", lambda e: e.dma_start(out=out_d[r0:r0 + 128, :], in_=xnb), f"outr{r}", reads=ringn[r])
                P.out_tokens.append(tok)

            fi = {}
            e_load(0)
            if sb == 0:
                fi[0] = front_a_cached(8)
                front_b(fi[0], 128, 0, xmT[:, :, 0:128], "xmT0")
            for jx in range(8):
                if jx + 1 < 8:
                    e_load(jx + 1)
                    if sb == 0:
                        fi[jx + 1] = front_a_cached(8 + jx + 1)
                if jx >= 2:
                    e_part2(jx - 2)
                e_mm(jx)
                if sb == 0 and jx + 1 < 8:
                    front_b(fi[jx + 1], 128, 0, xmT[:, :, (jx + 1) * 128:(jx + 2) * 128], f"xmT{jx + 1}")
                e_stats(jx)
            e_part2(6)
            e_part2(7)
            if sb == 0:
                stage0_halo(1)

        dump("retnT", retnT, [128, 8, HALF], BF16, ALL_RETN)
        dump("xmT", xmT, [128, 8, 1024], BF16, XM)
        dump("bufA", bufA, [128, 8, 1024], BF16, BA)
        dump("mbuf", mbuf, [128, 8, 1024], BF16, MM)
        dump("vT", vT, [128, 8, 16], F32, ["vT"])
        dump("modT", modT, [128, 24, 2], F32, ["modT"])
        dump("Amod", Amod, [128, 8, 2], F32, ["Amod"])
        dump("kdec", kdec, [128, 16], F32, ["dec"])
        dump("qdec", qdec, [128, 16], F32, ["dec"])
        dump("dch", dch, [128, 16], F32, ["dec"])
        dump("gxb", gxb, [128, D], F32, ["gxb0", "gxb1"])
        fin = {}
        for skey, val, _ in P.out_tokens:
            fin[skey] = max(fin.get(skey, 0), val)
        P.streams["sp"].append((list(fin.items()), None, None))

        sems = {}
        keys = set()
        for st in P.streams.values():
            for waits, fn, inc in st:
                for skey, _ in waits:
                    keys.add(skey)
                if inc is not None:
                    keys.add(inc[0])
        for idx, k in enumerate(sorted(keys, key=str)):
            sems[k] = es.enter_context(nc.semaphore(f"sem{idx}"))

        def replay(name, eng):
            for waits, fn, inc in P.streams[name]:
                for skey, val in waits:
                    eng.wait_ge(sems[skey], val)
                if fn is None:
                    continue
                ins = fn(eng)
                ins.then_inc(sems[inc[0]], inc[1])

        block = es.enter_context(nc.Block())

        @block.sync
        def _(e):
            replay("sp", e)

        @block.tensor
        def _(e):
            replay("pe", e)

        @block.scalar
        def _(e):
            replay("act", e)

        @block.vector
        def _(e):
            replay("dve", e)

        @block.gpsimd
        def _(e):
            replay("pool", e)

    return nc


def _host_consts(flip):
    ident = np.eye(128, dtype=np.float32)
    jj = np.arange(128, dtype=np.float32)[:, None]
    ii = np.arange(128, dtype=np.float32)[None, :]
    dmat = ii - jj
    dpn = np.concatenate([np.maximum(dmat, 0), np.maximum(-dmat, 0), np.broadcast_to(ii + 1, (128, 128)),
                          np.broadcast_to(128 - ii, (128, 128))], axis=1).astype(np.float32)
    p = np.arange(128, dtype=np.float32)
    pidx = np.stack([p, 127 - p, p + 1, 128 - p], axis=1).astype(np.float32)
    t = np.arange(SEQ)
    pos = (SEQ - 1 - t) if flip else t
    row = (pos // 64).astype(np.float32)
    col = (pos % 64).astype(np.float32)
    nf = 16
    inv = (10000.0 ** (-np.arange(nf, dtype=np.float32) / nf)).astype(np.float32)
    ang = np.concatenate([row[:, None] * inv, col[:, None] * inv], axis=-1).astype(np.float32)
    cos = np.cos(ang).astype(np.float32)
    sin = np.sin(ang).astype(np.float32)
    qt = np.concatenate([cos, cos, -sin, sin], axis=1)
    kt = (qt * np.float32(0.125)).astype(np.float32)
    rope = np.zeros((34, 128, 256), np.float32)
    rope[:32, :, 0:128] = kt.reshape(32, 128, 128)
    rope[:32, :, 128:256] = qt.reshape(32, 128, 128)
    rope[32:, :, 0:64] = 0.125
    return ident, dpn, pidx, rope


_NC_CACHE = {}


def _in_maps(x, c, ctx, c_ctx, norm_w, ada_w, ada_b, w_in, conv_w, conv_b, decay_logit, gn_w, w_a, w_b, w_out, final_norm_w):
    f = lambda a: np.ascontiguousarray(np.asarray(a, dtype=np.float32))
    x, c, ctx, c_ctx = f(x), f(c), f(ctx), f(c_ctx)
    norm_w, ada_w, ada_b, w_in = f(norm_w)[0], f(ada_w)[0], f(ada_b)[0], f(w_in)[0]
    conv_w, conv_b, decay_logit, gn_w = f(conv_w)[0], f(conv_b)[0], f(decay_logit)[0], f(gn_w)[0]
    w_a, w_b, w_out, fnw = f(w_a)[0], f(w_b)[0], f(w_out)[0], f(final_norm_w)
    consts = {fl: _host_consts(fl) for fl in (False, True)}
    in_maps = []
    for core in range(8):
        b, half = core // 2, core % 2
        flip = half == 1
        ident, dpn, pidx, rope = consts[flip]
        xb = x[b, ::-1] if flip else x[b]
        cb = ctx[b, ::-1] if flip else ctx[b]
        cw = conv_w[::-1] if flip else conv_w
        dl = decay_logit[::-1] if flip else decay_logit
        vecs = np.stack([c[b], c_ctx, norm_w, cw[0], cw[1], cw[2], conv_b, gn_w,
                         ada_b[0:D], ada_b[D:2 * D], ada_b[2 * D:3 * D]], axis=0)
        in_maps.append({
            "x": np.ascontiguousarray(xb), "ctx": np.ascontiguousarray(cb), "vecs": np.ascontiguousarray(vecs),
            "ada_w": ada_w, "w_in": w_in, "w_a": w_a, "w_b": w_b, "w_out": w_out,
            "dl": np.ascontiguousarray(dl.reshape(1, 16)), "fnw": fnw.reshape(1, D),
            "ident": ident, "dpn": dpn, "pidx": pidx, "rope": rope,
        })
    return in_maps


def _gather(results):
    out = np.empty((4, SEQ, D), np.float32)
    for core in range(8):
        b, half = core // 2, core % 2
        o = np.asarray(results[core]["out"], dtype=np.float32)
        if half == 0:
            out[b, 0:HALF] = o
        else:
            out[b, HALF:SEQ] = o[::-1]
    return out


def kernel(x, c, ctx, c_ctx, norm_w, ada_w, ada_b, w_in, conv_w, conv_b, decay_logit, gn_w, w_a, w_b, w_out, final_norm_w):
    in_maps = _in_maps(x, c, ctx, c_ctx, norm_w, ada_w, ada_b, w_in, conv_w, conv_b, decay_logit, gn_w, w_a, w_b, w_out, final_norm_w)
    if "nc" not in _NC_CACHE:
        _NC_CACHE["nc"] = build_program()
    res = run_bass_kernel_spmd(_NC_CACHE["nc"], in_maps, core_ids=list(range(8)))
    return _gather(res.results)
```
